# Optimizing a Trainium2 kernel written in Bass

```python
import jax
import jax.numpy as jnp
from jax import lax
import numpy as np

D_MODEL = 2048
BATCH = 4
SEQ = 4096
DEPTH = 4

GRID_W = 64
CTX_LEN = 256
N_MIXERS = 3
N_MOD = 9
D_FF = 5632
EPS = 1e-6
GLA_HEADS = 4
GLA_DK = D_MODEL // (2 * GLA_HEADS)
GLA_DV = D_MODEL // GLA_HEADS
GLA_RANK = 16
GLA_TAU = 16.0
GLA_CHUNK = 64
GLA_HK = GLA_HEADS * GLA_DK
GLA_HV = GLA_HEADS * GLA_DV
GLA_SPLITS = (GLA_HK, 2 * GLA_HK, 2 * GLA_HK + GLA_HV, 2 * GLA_HK + 2 * GLA_HV,
              2 * GLA_HK + 2 * GLA_HV + GLA_RANK)
GLA_IN = 2 * GLA_HK + 2 * GLA_HV + 2 * GLA_RANK
FNET_GROUPS = 8
FNET_GW = D_MODEL // FNET_GROUPS
CONV_W = 31

kernel_name = "hybrid_gla_fnet_conformer_dit"


def rmsnorm(x, g):
    xf = x.astype(jnp.float32)
    y = xf * lax.rsqrt(jnp.mean(xf * xf, axis=-1, keepdims=True) + EPS)
    return (y * g).astype(x.dtype)


def layernorm(x, g, b):
    xf = x.astype(jnp.float32)
    mu = jnp.mean(xf, axis=-1, keepdims=True)
    var = jnp.mean(jnp.square(xf - mu), axis=-1, keepdims=True)
    return ((xf - mu) * lax.rsqrt(var + EPS) * g + b).astype(x.dtype)


def adaln(cond, w_mod, b_mod):
    m = jax.nn.silu(cond) @ w_mod + b_mod
    return m.reshape(m.shape[:-1] + (N_MOD, D_MODEL))


def pre_norm(x, mod, k, g):
    return rmsnorm(x, g) * (1.0 + mod[..., 3 * k + 1, :]) + mod[..., 3 * k, :]


def ffn_half(x, mod, k, g, w_gate, w_up, w_down):
    h = pre_norm(x, mod, k, g)
    y = (jax.nn.silu(h @ w_gate) * (h @ w_up)) @ w_down
    return x + 0.5 * mod[..., 3 * k + 2, :] * y


def _flip(t):
    return jnp.flip(t, axis=1)


def gla_project(h, w_in, w_gate_up, b_gate):
    bq, L, _ = h.shape
    p = (h @ w_in).astype(jnp.float32)
    q, k, v, r, z_f, z_b = jnp.split(p, GLA_SPLITS, axis=-1)
    q = q.reshape(bq, L, GLA_HEADS, GLA_DK) * (GLA_DK ** -0.5)
    k = k.reshape(bq, L, GLA_HEADS, GLA_DK)
    v = v.reshape(bq, L, GLA_HEADS, GLA_DV)
    g_f = (jax.nn.log_sigmoid(z_f @ w_gate_up[0] + b_gate[0]) / GLA_TAU).reshape(bq, L, GLA_HEADS, GLA_DK)
    g_b = (jax.nn.log_sigmoid(z_b @ w_gate_up[1] + b_gate[1]) / GLA_TAU).reshape(bq, L, GLA_HEADS, GLA_DK)
    return q, k, v, r, g_f, g_b


def gla_final_state(k, v, g):
    G = jnp.cumsum(g, axis=1)
    kd = k * jnp.exp(G[:, -1:] - G)
    return jnp.einsum("blhd,blhv->bhdv", kd, v)


def gla_chunked(q, k, v, g, s0):
    bq, L, H, dk = q.shape
    n = L // GLA_CHUNK

    def blocks(t):
        return t.reshape(bq, n, GLA_CHUNK, H, t.shape[-1]).transpose(1, 0, 3, 2, 4)

    qb, kb, vb, gb = blocks(q), blocks(k), blocks(v), blocks(g)
    bcum = jnp.cumsum(gb, axis=3)
    b_last = bcum[:, :, :, -1:, :]
    ref = bcum[:, :, :, GLA_CHUNK // 2 - 1:GLA_CHUNK // 2, :]
    q_intra = qb * jnp.exp(bcum - ref)
    k_intra = kb * jnp.exp(ref - bcum)
    mask = jnp.tril(jnp.ones((GLA_CHUNK, GLA_CHUNK), dtype=bool))
    att = jnp.where(mask, jnp.einsum("nbhtd,nbhsd->nbhts", q_intra, k_intra), 0.0)
    o_intra = jnp.einsum("nbhts,nbhsv->nbhtv", att, vb)
    q_inter = qb * jnp.exp(bcum)
    k_state = kb * jnp.exp(b_last - bcum)
    decay = jnp.exp(b_last[:, :, :, 0, :])

    def step(s, xs):
        q_n, k_n, v_n, d_n = xs
        o = jnp.einsum("bhtd,bhdv->bhtv", q_n, s)
        s = d_n[..., None] * s + jnp.einsum("bhsd,bhsv->bhdv", k_n, v_n)
        return s, o

    _, o_inter = lax.scan(step, s0, (q_inter, k_state, vb, decay))
    o = o_intra + o_inter
    return o.transpose(1, 0, 3, 2, 4).reshape(bq, L, H, v.shape[-1])


def gla_output(o, r, g_head, w_out, dtype):
    bq, L = o.shape[:2]
    o = o * lax.rsqrt(jnp.mean(o * o, axis=-1, keepdims=True) + EPS) * g_head
    o = o * jax.nn.silu(r.reshape(bq, L, GLA_HEADS, GLA_DV))
    return o.reshape(bq, L, GLA_HV).astype(dtype) @ w_out


def gla_mixer(h_l, h_c, w_in, w_gate_up, b_gate, g_head, w_out, emit_ctx):
    ql, kl, vl, rl, gfl, gbl = gla_project(h_l, w_in, w_gate_up, b_gate)
    qc, kc, vc, rc, gfc, gbc = gla_project(h_c, w_in, w_gate_up, b_gate)
    s_f = gla_final_state(kc, vc, gfc)
    s_b = gla_final_state(_flip(kc), _flip(vc), _flip(gbc))
    o_l = gla_chunked(ql, kl, vl, gfl, s_f) + _flip(
        gla_chunked(_flip(ql), _flip(kl), _flip(vl), _flip(gbl), s_b))
    y_l = gla_output(o_l, rl, g_head, w_out, h_l.dtype)
    y_c = None
    if emit_ctx:
        s0 = jnp.zeros_like(s_f)
        o_c = gla_chunked(qc, kc, vc, gfc, s0) + _flip(
            gla_chunked(_flip(qc), _flip(kc), _flip(vc), _flip(gbc), s0))
        y_c = gla_output(o_c, rc, g_head, w_out, h_c.dtype)
    return y_l, y_c


def fourier_mix(h, w_out, b_out):
    bq, L, _ = h.shape
    hg = h.astype(jnp.float32).reshape(bq, L, FNET_GROUPS, FNET_GW)
    f = jnp.fft.fftn(hg, axes=(1, 3), norm="ortho").real
    return f.reshape(bq, L, D_MODEL).astype(h.dtype) @ w_out + b_out


def conv_module(h, w_pw1, b_pw1, w_dw, b_dw, ln_g, ln_b, w_pw2, b_pw2, n_seg):
    u = h @ w_pw1 + b_pw1
    a, gate = jnp.split(u, 2, axis=-1)
    u = a * jax.nn.sigmoid(gate)
    bq, L, dm = u.shape
    seg = u.reshape(bq * n_seg, L // n_seg, dm)
    y = lax.conv_general_dilated(
        seg, w_dw[:, None, :].astype(seg.dtype), window_strides=(1,),
        padding=((CONV_W // 2, CONV_W // 2),),
        dimension_numbers=("NWC", "WIO", "NWC"), feature_group_count=dm)
    y = y.reshape(bq, L, dm) + b_dw
    y = jax.nn.silu(layernorm(y, ln_g, ln_b))
    return y @ w_pw2 + b_pw2


def setup_inputs(seed: int = 0) -> dict:
    key = jax.random.key(seed)
    ks = iter(jax.random.split(key, 32))
    D = D_MODEL
    n_a = len(range(0, DEPTH, N_MIXERS))
    n_b = len(range(1, DEPTH, N_MIXERS))
    n_c = len(range(2, DEPTH, N_MIXERS))

    def nrm(shape, scale):
        return jax.random.normal(next(ks), shape, jnp.float32) * scale

    return {
        "x": nrm((BATCH, SEQ, D), 1.0),
        "c": nrm((BATCH, D), 1.0),
        "ctx": nrm((BATCH, CTX_LEN, D), 1.0),
        "c_ctx": nrm((D,), 1.0),
        "w_mod": nrm((DEPTH, D, N_MOD * D), 0.5 * D ** -0.5),
        "b_mod": nrm((DEPTH, N_MOD * D), 0.02),
        "norm_g": 1.0 + nrm((DEPTH, 3, D), 0.02),
        "ffn_w_gate": nrm((DEPTH, 2, D, D_FF), D ** -0.5),
        "ffn_w_up": nrm((DEPTH, 2, D, D_FF), D ** -0.5),
        "ffn_w_down": nrm((DEPTH, 2, D_FF, D), D_FF ** -0.5),
        "gla_w_in": nrm((n_a, D, GLA_IN), D ** -0.5),
        "gla_w_gate_up": nrm((n_a, 2, GLA_RANK, GLA_HK), GLA_RANK ** -0.5),
        "gla_b_gate": nrm((n_a, 2, GLA_HK), 0.1),
        "gla_g_head": 1.0 + nrm((n_a, GLA_DV), 0.02),
        "gla_w_out": nrm((n_a, GLA_HV, D), GLA_HV ** -0.5),
        "fnet_w_out": nrm((n_b, D, D), D ** -0.5),
        "fnet_b_out": nrm((n_b, D), 0.02),
        "cm_w_pw1": nrm((n_c, D, 2 * D), D ** -0.5),
        "cm_b_pw1": nrm((n_c, 2 * D), 0.02),
        "cm_w_dw": nrm((n_c, CONV_W, D), CONV_W ** -0.5),
        "cm_b_dw": nrm((n_c, D), 0.02),
        "cm_ln_g": 1.0 + nrm((n_c, D), 0.02),
        "cm_ln_b": nrm((n_c, D), 0.02),
        "cm_w_pw2": nrm((n_c, D, D), D ** -0.5),
        "cm_b_pw2": nrm((n_c, D), 0.02),
        "final_g": 1.0 + nrm((D,), 0.02),
    }


def reference(x, c, ctx, c_ctx, w_mod, b_mod, norm_g, ffn_w_gate, ffn_w_up, ffn_w_down,
              gla_w_in, gla_w_gate_up, gla_b_gate, gla_g_head, gla_w_out,
              fnet_w_out, fnet_b_out,
              cm_w_pw1, cm_b_pw1, cm_w_dw, cm_b_dw, cm_ln_g, cm_ln_b, cm_w_pw2, cm_b_pw2,
              final_g):
    rows = x.shape[1] // GRID_W
    xc = ctx
    for i in range(DEPTH):
        kind = i % N_MIXERS
        j = i // N_MIXERS
        last = i == DEPTH - 1
        ctx_in = (not last) or kind == 0
        mod_l = adaln(c, w_mod[i], b_mod[i])[:, None]
        x = ffn_half(x, mod_l, 0, norm_g[i, 0], ffn_w_gate[i, 0], ffn_w_up[i, 0], ffn_w_down[i, 0])
        h_l = pre_norm(x, mod_l, 1, norm_g[i, 1])
        if ctx_in:
            mod_c = adaln(c_ctx, w_mod[i], b_mod[i])[None, None]
            xc = ffn_half(xc, mod_c, 0, norm_g[i, 0], ffn_w_gate[i, 0], ffn_w_up[i, 0], ffn_w_down[i, 0])
            h_c = pre_norm(xc, mod_c, 1, norm_g[i, 1])
        if kind == 0:
            y_l, y_c = gla_mixer(h_l, h_c, gla_w_in[j], gla_w_gate_up[j], gla_b_gate[j],
                                 gla_g_head[j], gla_w_out[j], emit_ctx=not last)
        elif kind == 1:
            y_l = fourier_mix(h_l, fnet_w_out[j], fnet_b_out[j])
            y_c = fourier_mix(h_c, fnet_w_out[j], fnet_b_out[j]) if not last else None
        else:
            cm = (cm_w_pw1[j], cm_b_pw1[j], cm_w_dw[j], cm_b_dw[j], cm_ln_g[j], cm_ln_b[j],
                  cm_w_pw2[j], cm_b_pw2[j])
            y_l = conv_module(h_l, *cm, rows)
            y_c = conv_module(h_c, *cm, 1) if not last else None
        x = x + mod_l[..., 5, :] * y_l
        x = ffn_half(x, mod_l, 2, norm_g[i, 2], ffn_w_gate[i, 1], ffn_w_up[i, 1], ffn_w_down[i, 1])
        if not last:
            xc = xc + mod_c[..., 5, :] * y_c
            xc = ffn_half(xc, mod_c, 2, norm_g[i, 2], ffn_w_gate[i, 1], ffn_w_up[i, 1], ffn_w_down[i, 1])
    return rmsnorm(x, final_g)
```

```python
import numpy as np
from contextlib import ExitStack
import concourse.bass as bass
import concourse.mybir as mybir
from concourse.bass_utils import run_bass_kernel_spmd

F32 = mybir.dt.float32
BF16 = mybir.dt.bfloat16
AF = mybir.ActivationFunctionType
ALU = mybir.AluOpType

D = 2048
KC = D // 128
DFF = 5632
FC = DFF // 128
NMOD = 9
EPS = 1e-6
NCORES = 8


class Buf:
    __slots__ = ("name", "w", "r")

    def __init__(self, name=""):
        self.name = name
        self.w = None
        self.r = {}


class Op:
    __slots__ = ("eng", "fn", "deps", "dma", "ref", "sem", "val", "pos")

    def __init__(self, eng, fn, dma):
        self.eng = eng
        self.fn = fn
        self.dma = dma
        self.deps = []
        self.ref = False
        self.sem = None
        self.val = 0
        self.pos = 0


ENGS = ("pe", "act", "dve", "pool", "sp")
DMA_SLOTS = 8


class Prog:
    def __init__(self, nc, es):
        self.nc = nc
        self.es = es
        self.streams = {e: [] for e in ENGS}
        self.nbuf = 0

    def sb(self, name, shape, dt):
        return self.es.enter_context(self.nc.sbuf_tensor(name, list(shape), dt))

    def ps(self, name, shape, dt=F32):
        return self.es.enter_context(self.nc.psum_tensor(name, list(shape), dt))

    def buf(self, name=""):
        self.nbuf += 1
        return Buf(name or f"b{self.nbuf}")

    def op(self, eng, fn, reads=(), writes=(), dma=False, extra=()):
        o = Op(eng, fn, dma)
        deps = []
        for b in reads:
            if b.w is not None:
                deps.append(b.w)
        for b in writes:
            if b.w is not None:
                deps.append(b.w)
            for k, v in b.r.items():
                if k is None:
                    deps.extend(v)
                else:
                    deps.append(v)
        deps.extend(extra)
        seen = set()
        for d in deps:
            if d is o or id(d) in seen:
                continue
            if d.eng == "pe" and eng == "pe" and not d.dma and not dma:
                continue
            seen.add(id(d))
            o.deps.append(d)
        for b in reads:
            if dma:
                b.r.setdefault(None, []).append(o)
            else:
                b.r[eng] = o
        for b in writes:
            b.w = o
            b.r = {}
        o.pos = len(self.streams[eng])
        self.streams[eng].append(o)
        return o

    def join(self, eng, ops):
        return self.op(eng, None, extra=list(ops))

    def emit(self):
        nc = self.nc
        es = self.es
        for e in ENGS:
            for o in self.streams[e]:
                for d in o.deps:
                    d.ref = True
        csem = {e: es.enter_context(nc.semaphore(f"c_{e}")) for e in ENGS}
        dsem = {e: [es.enter_context(nc.semaphore(f"d_{e}{i}")) for i in range(DMA_SLOTS)]
                for e in ("act", "pool", "sp")}
        for e in ENGS:
            cc = 0
            dcount = 0
            duse = [0] * DMA_SLOTS
            hist = []
            for o in self.streams[e]:
                if o.fn is None:
                    continue
                if o.dma:
                    slot = dcount % DMA_SLOTS
                    duse[slot] += 1
                    o.sem = dsem[e][slot]
                    o.val = 16 * duse[slot]
                    if dcount >= DMA_SLOTS:
                        o.deps.append(hist[dcount - DMA_SLOTS])
                    hist.append(o)
                    dcount += 1
                    o.ref = True
                elif o.ref:
                    cc += 1
                    o.sem = csem[e]
                    o.val = cc
        block = es.enter_context(nc.Block())

        def run(e, eng):
            seen = {}
            for o in self.streams[e]:
                need = {}
                for d in o.deps:
                    k = id(d.sem)
                    if seen.get(k, 0) >= d.val:
                        continue
                    if k not in need or need[k][1] < d.val:
                        need[k] = (d.sem, d.val)
                for k, (s, v) in need.items():
                    eng.wait_ge(s, v)
                    seen[k] = v
                if o.fn is None:
                    continue
                ins = o.fn(eng)
                if o.sem is not None:
                    ins.then_inc(o.sem, 16 if o.dma else 1)

        @block.tensor
        def _(eng):
            run("pe", eng)

        @block.scalar
        def _(eng):
            run("act", eng)

        @block.vector
        def _(eng):
            run("dve", eng)

        @block.gpsimd
        def _(eng):
            run("pool", eng)

        @block.sync
        def _(eng):
            run("sp", eng)


def chunked(ap2d):
    return ap2d.rearrange("(c p) t -> p c t", p=128)


class FFNCtx:
    def __init__(self, P, W):
        self.P = P
        nc = P.nc
        self.W = W
        self.ones = P.sb("ones", [128, 128], F32)
        self.ones_b = P.buf("ones")
        P.op("pool", lambda e: e.memset(self.ones[:], 1.0), writes=[self.ones_b])
        self.banks = [P.ps(f"bank{i}", [128, 512]) for i in range(8)]
        self.bank_b = [P.buf(f"bank{i}") for i in range(8)]
        self.sq = [P.sb(f"sq{i}", [128, W], F32) for i in range(2)]
        self.sq_b = [P.buf(f"sq{i}") for i in range(2)]
        self.rstd = P.sb("rstd", [128, W], F32)
        self.rstd_b = P.buf("rstd")
        self.tmp = [P.sb(f"tmp{i}", [128, W], F32) for i in range(2)]
        self.tmp_b = [P.buf(f"tmp{i}") for i in range(2)]
        self.n_sq = 0
        self.n_tmp = 0


def emit_rstd(C, x, xb, w, bank):
    P = C.P
    for c in range(KC):
        i = C.n_sq % 2
        C.n_sq += 1
        P.op("act", lambda e, c=c, i=i: e.activation(out=C.sq[i][:, :w], in_=x[:, c, :w], func=AF.Square),
             reads=[xb[c]], writes=[C.sq_b[i]])
        P.op("pe", lambda e, c=c, i=i: e.matmul(C.banks[bank][:, :w], C.ones[:], C.sq[i][:, :w],
                                               start=(c == 0), stop=(c == KC - 1)),
             reads=[C.sq_b[i], C.ones_b], writes=[C.bank_b[bank]])
    P.op("act", lambda e: e.activation(out=C.rstd[:, :w], in_=C.banks[bank][:, :w], func=AF.Sqrt,
                                       bias=C.epsb[:, 0:1], scale=1.0 / D),
         reads=[C.bank_b[bank], C.eps_bb], writes=[C.rstd_b])
    P.op("dve", lambda e: e.reciprocal(out=C.rstd[:, :w], in_=C.rstd[:, :w]),
         reads=[C.rstd_b], writes=[C.rstd_b])


def emit_prenorm(C, x, xb, w, gs, shift, msb, out_fn, out_bufs):
    P = C.P
    for c in range(KC):
        i = C.n_tmp % 2
        C.n_tmp += 1
        P.op("dve", lambda e, c=c, i=i: e.scalar_tensor_tensor(
            out=C.tmp[i][:, :w], in0=x[:, c, :w], scalar=gs[:, c:c + 1], in1=C.rstd[:, :w],
            op0=ALU.mult, op1=ALU.mult),
            reads=[xb[c], C.rstd_b, msb], writes=[C.tmp_b[i]])
        P.op("act", lambda e, c=c, i=i: e.activation(out=out_fn(c), in_=C.tmp[i][:, :w], func=AF.Identity,
                                                     bias=shift[:, c:c + 1], scale=1.0),
             reads=[C.tmp_b[i], msb], writes=[out_bufs[c]])


def build_ffn_program(T, tiles, n_sets, n_ffn, proj_in, emit_h, final_norm, gla_in=False):
    W = max(w for _, w, _ in tiles)
    nc = bass.Bass("TRN2", target_bir_lowering=False)
    NV = 4 * n_ffn + (2 if proj_in else 0) + (3 if emit_h else 0) + (1 if final_norm else 0)
    xT = nc.dram_tensor("xT", [D, T], F32, kind="ExternalInput").ap()
    mods = nc.dram_tensor("mods", [128, n_sets * NV * KC], F32, kind="ExternalInput").ap()
    wg = [nc.dram_tensor(f"wg{j}", [D, DFF], F32, kind="ExternalInput").ap() for j in range(n_ffn)]
    wu = [nc.dram_tensor(f"wu{j}", [D, DFF], F32, kind="ExternalInput").ap() for j in range(n_ffn)]
    wd = [nc.dram_tensor(f"wd{j}", [DFF, D], F32, kind="ExternalInput").ap() for j in range(n_ffn)]
    if proj_in:
        if gla_in:
            ofT = nc.dram_tensor("ofT", [D, T], F32, kind="ExternalInput").ap()
            obT = nc.dram_tensor("obT", [D, T], F32, kind="ExternalInput").ap()
            rsT = nc.dram_tensor("rsT", [D, T], F32, kind="ExternalInput").ap()
            gh_d = nc.dram_tensor("gh", [128, 4], F32, kind="ExternalInput").ap()
        else:
            mT = nc.dram_tensor("mT", [D, T], F32, kind="ExternalInput").ap()
        wo = nc.dram_tensor("wo", [D, D], F32, kind="ExternalInput").ap()
    oT = nc.dram_tensor("oT", [D, T], F32, kind="ExternalOutput").ap()
    if emit_h:
        hT = nc.dram_tensor("hT", [D, T], F32, kind="ExternalOutput").ap()

    with ExitStack() as es:
        P = Prog(nc, es)
        C = FFNCtx(P, W)
        C.epsb = P.sb("epsb", [128, 1], F32)
        C.eps_bb = P.buf("eps")
        P.op("pool", lambda e: e.memset(C.epsb[:], EPS), writes=[C.eps_bb])
        mv = P.sb("mv", [128, n_sets * NV * KC], F32)
        mvb = P.buf("mv")
        P.op("sp", lambda e: e.dma_start(out=mv[:], in_=mods), writes=[mvb], dma=True)

        def vec(s, k):
            o = (s * NV + k) * KC
            return mv[:, o:o + KC]

        slot = 0
        ffn_slots = []
        for j in range(n_ffn):
            ffn_slots.append(slot)
            slot += 4
        proj_slot = slot if proj_in else None
        slot += 2 if proj_in else 0
        h_slot = slot if emit_h else None
        slot += 3 if emit_h else 0
        fin_slot = slot if final_norm else None
        for s in range(n_sets):
            norm_slots = list(ffn_slots) + ([h_slot] if emit_h else [])
            for k in norm_slots:
                P.op("dve", lambda e, s=s, k=k: e.scalar_tensor_tensor(
                    out=vec(s, k), in0=vec(s, k + 2), scalar=1.0, in1=vec(s, k),
                    op0=ALU.add, op1=ALU.mult), reads=[mvb], writes=[mvb])
            for k in ffn_slots:
                P.op("dve", lambda e, s=s, k=k: e.tensor_scalar(
                    out=vec(s, k + 3), in0=vec(s, k + 3), scalar1=0.5, scalar2=None, op0=ALU.mult),
                    reads=[mvb], writes=[mvb])

        x = P.sb("x", [128, KC, W], F32)
        xb = [P.buf(f"x{c}") for c in range(KC)]
        h = P.sb("h", [128, KC, W], BF16)
        hb = [P.buf(f"h{c}") for c in range(KC)]
        a = P.sb("a", [128, FC, W], BF16)
        ab = [P.buf(f"a{c}") for c in range(FC)]
        sg = [P.sb(f"sg{i}", [128, W], F32) for i in range(2)]
        sgb = [P.buf(f"sg{i}") for i in range(2)]
        FB = 2
        NWB = 2
        wgt = [P.sb(f"wgt{i}", [128, KC, FB * 128], BF16) for i in range(NWB)]
        wgb = [P.buf(f"wgt{i}") for i in range(NWB)]
        wut = [P.sb(f"wut{i}", [128, KC, FB * 128], BF16) for i in range(NWB)]
        wub = [P.buf(f"wut{i}") for i in range(NWB)]
        DFB = 4
        NDB = 3
        wdt = [P.sb(f"wdt{i}", [128, DFB, 512], BF16) for i in range(NDB)]
        wdb = [P.buf(f"wdt{i}") for i in range(NDB)]
        if emit_h or final_norm:
            ho = [P.sb(f"ho{i}", [128, W], F32) for i in range(2)]
            hob = [P.buf(f"ho{i}") for i in range(2)]
        if proj_in:
            m32 = [P.sb(f"m32_{i}", [128, W], F32) for i in range(2)]
            m32b = [P.buf(f"m32_{i}") for i in range(2)]
            wot = [P.sb(f"wot{i}", [128, KC, 128], BF16) for i in range(2)]
            wotb = [P.buf(f"wot{i}") for i in range(2)]
            if gla_in:
                gh = P.sb("gh_sb", [128, 4], F32)
                ghb = P.buf("gh")
                P.op("sp", lambda e: e.dma_start(out=gh[:], in_=gh_d), writes=[ghb], dma=True)
                osum = P.sb("osum", [128, 4, W], F32)
                osumb = [P.buf(f"osum{i}") for i in range(4)]
                ob32 = [P.sb(f"ob32_{i}", [128, W], F32) for i in range(2)]
                ob32b = [P.buf(f"ob32_{i}") for i in range(2)]
                eps5 = P.sb("eps5", [128, 1], F32)
                eps5b = P.buf("eps5")
                P.op("pool", lambda e: e.memset(eps5[:], EPS), writes=[eps5b])
        cnt = {"w": 0, "d": 0, "sg": 0, "ho": 0, "m": 0, "wo": 0}
        out_dmas = []

        for (t0, w, s) in tiles:
            for half in range(2):
                cs = slice(half * 8, half * 8 + 8)
                P.op("sp", lambda e, cs=cs, t0=t0, w=w: e.dma_start(
                    out=x[:, cs, :w], in_=chunked(xT)[:, cs, t0:t0 + w]),
                    writes=xb[cs], dma=True)
            if proj_in:
                if gla_in:
                    for hd in range(4):
                        for cc in range(4):
                            c = 4 * hd + cc
                            i = cnt["m"] % 2
                            cnt["m"] += 1
                            P.op("sp", lambda e, c=c, cc=cc, t0=t0, w=w: e.dma_start(
                                out=osum[:, cc, :w], in_=ofT[c * 128:(c + 1) * 128, t0:t0 + w]),
                                writes=[osumb[cc]], dma=True)
                            P.op("sp", lambda e, c=c, i=i, t0=t0, w=w: e.dma_start(
                                out=ob32[i][:, :w], in_=obT[c * 128:(c + 1) * 128, t0:t0 + w]),
                                writes=[ob32b[i]], dma=True)
                            P.op("dve", lambda e, cc=cc, i=i, w=w: e.tensor_tensor(
                                out=osum[:, cc, :w], in0=osum[:, cc, :w], in1=ob32[i][:, :w], op=ALU.add),
                                reads=[osumb[cc], ob32b[i]], writes=[osumb[cc]])
                            si = C.n_sq % 2
                            C.n_sq += 1
                            P.op("act", lambda e, cc=cc, si=si, w=w: e.activation(
                                out=C.sq[si][:, :w], in_=osum[:, cc, :w], func=AF.Square),
                                reads=[osumb[cc]], writes=[C.sq_b[si]])
                            P.op("pe", lambda e, cc=cc, si=si, w=w: e.matmul(
                                C.banks[0][:, :w], C.ones[:], C.sq[si][:, :w], start=(cc == 0), stop=(cc == 3)),
                                reads=[C.sq_b[si], C.ones_b], writes=[C.bank_b[0]])
                        P.op("act", lambda e, w=w: e.activation(
                            out=C.rstd[:, :w], in_=C.banks[0][:, :w], func=AF.Sqrt, bias=eps5[:, 0:1],
                            scale=1.0 / 512.0), reads=[C.bank_b[0], eps5b], writes=[C.rstd_b])
                        P.op("dve", lambda e, w=w: e.reciprocal(out=C.rstd[:, :w], in_=C.rstd[:, :w]),
                             reads=[C.rstd_b], writes=[C.rstd_b])
                        for cc in range(4):
                            c = 4 * hd + cc
                            i = cnt["m"] % 2
                            cnt["m"] += 1
                            P.op("sp", lambda e, c=c, i=i, t0=t0, w=w: e.dma_start(
                                out=m32[i][:, :w], in_=rsT[c * 128:(c + 1) * 128, t0:t0 + w]),
                                writes=[m32b[i]], dma=True)
                            P.op("dve", lambda e, cc=cc, w=w: e.scalar_tensor_tensor(
                                out=osum[:, cc, :w], in0=osum[:, cc, :w], scalar=gh[:, cc:cc + 1],
                                in1=C.rstd[:, :w], op0=ALU.mult, op1=ALU.mult),
                                reads=[osumb[cc], ghb, C.rstd_b], writes=[osumb[cc]])
                            P.op("dve", lambda e, c=c, cc=cc, i=i, w=w: e.tensor_tensor(
                                out=h[:, c, :w], in0=osum[:, cc, :w], in1=m32[i][:, :w], op=ALU.mult),
                                reads=[osumb[cc], m32b[i]], writes=[hb[c]])
                else:
                    for c in range(KC):
                        i = cnt["m"] % 2
                        cnt["m"] += 1
                        P.op("sp", lambda e, c=c, i=i, t0=t0, w=w: e.dma_start(
                            out=m32[i][:, :w], in_=mT[c * 128:(c + 1) * 128, t0:t0 + w]),
                            writes=[m32b[i]], dma=True)
                        P.op("act", lambda e, c=c, i=i, w=w: e.activation(out=h[:, c, :w], in_=m32[i][:, :w],
                                                                         func=AF.Identity),
                             reads=[m32b[i]], writes=[hb[c]])
                for dc in range(KC):
                    i = cnt["wo"] % 2
                    cnt["wo"] += 1
                    P.op("pool", lambda e, dc=dc, i=i: e.dma_start(
                        out=wot[i][:], in_=chunked(wo)[:, :, dc * 128:(dc + 1) * 128]),
                        writes=[wotb[i]], dma=True)
                    bank = 4 + (dc % 4)
                    for kc in range(KC):
                        P.op("pe", lambda e, dc=dc, kc=kc, i=i, bank=bank, w=w: e.matmul(
                            C.banks[bank][:, :w], wot[i][:, kc, :], h[:, kc, :w],
                            start=(kc == 0), stop=(kc == KC - 1)),
                            reads=[wotb[i], hb[kc]], writes=[C.bank_b[bank]])
                    ti = C.n_tmp % 2
                    C.n_tmp += 1
                    P.op("act", lambda e, dc=dc, ti=ti, bank=bank, w=w, s=s: e.activation(
                        out=C.tmp[ti][:, :w], in_=C.banks[bank][:, :w], func=AF.Identity,
                        bias=vec(s, proj_slot + 1)[:, dc:dc + 1], scale=1.0),
                        reads=[C.bank_b[bank], mvb], writes=[C.tmp_b[ti]])
                    P.op("dve", lambda e, dc=dc, ti=ti, w=w, s=s: e.scalar_tensor_tensor(
                        out=x[:, dc, :w], in0=C.tmp[ti][:, :w], scalar=vec(s, proj_slot)[:, dc:dc + 1],
                        in1=x[:, dc, :w], op0=ALU.mult, op1=ALU.add),
                        reads=[C.tmp_b[ti], mvb, xb[dc]], writes=[xb[dc]])

            for j in range(n_ffn):
                k0 = ffn_slots[j]
                emit_rstd(C, x, xb, w, 0)
                emit_prenorm(C, x, xb, w, vec(s, k0), vec(s, k0 + 1), mvb,
                             lambda c, w=w: h[:, c, :w], hb)
                for fb in range(FC // FB):
                    i = cnt["w"] % NWB
                    cnt["w"] += 1
                    fsl = slice(fb * FB * 128, (fb + 1) * FB * 128)
                    P.op("pool", lambda e, i=i, fsl=fsl, j=j: e.dma_start(
                        out=wgt[i][:], in_=chunked(wg[j])[:, :, fsl]), writes=[wgb[i]], dma=True)
                    P.op("pool", lambda e, i=i, fsl=fsl, j=j: e.dma_start(
                        out=wut[i][:], in_=chunked(wu[j])[:, :, fsl]), writes=[wub[i]], dma=True)
                    for f in range(FB):
                        fc = fb * FB + f
                        par = fc % 2
                        gb, ub = 2 * par, 2 * par + 1
                        for kc in range(KC):
                            P.op("pe", lambda e, i=i, f=f, kc=kc, gb=gb, w=w: e.matmul(
                                C.banks[gb][:, :w], wgt[i][:, kc, f * 128:(f + 1) * 128], h[:, kc, :w],
                                start=(kc == 0), stop=(kc == KC - 1)),
                                reads=[wgb[i], hb[kc]], writes=[C.bank_b[gb]])
                        for kc in range(KC):
                            P.op("pe", lambda e, i=i, f=f, kc=kc, ub=ub, w=w: e.matmul(
                                C.banks[ub][:, :w], wut[i][:, kc, f * 128:(f + 1) * 128], h[:, kc, :w],
                                start=(kc == 0), stop=(kc == KC - 1)),
                                reads=[wub[i], hb[kc]], writes=[C.bank_b[ub]])
                        si = cnt["sg"] % 2
                        cnt["sg"] += 1
                        P.op("act", lambda e, si=si, gb=gb, w=w: e.activation(
                            out=sg[si][:, :w], in_=C.banks[gb][:, :w], func=AF.Silu),
                            reads=[C.bank_b[gb]], writes=[sgb[si]])
                        P.op("dve", lambda e, si=si, ub=ub, fc=fc, w=w: e.tensor_tensor(
                            out=a[:, fc, :w], in0=sg[si][:, :w], in1=C.banks[ub][:, :w], op=ALU.mult),
                            reads=[sgb[si], C.bank_b[ub]], writes=[ab[fc]])
                for dg in range(4):
                    base = 4 if dg % 2 == 0 else 0
                    for fb in range(FC // DFB):
                        i = cnt["d"] % NDB
                        cnt["d"] += 1
                        P.op("pool", lambda e, i=i, fb=fb, dg=dg, j=j: e.dma_start(
                            out=wdt[i][:],
                            in_=wd[j][fb * DFB * 128:(fb + 1) * DFB * 128, dg * 512:(dg + 1) * 512]
                            .rearrange("(c p) n -> p c n", p=128)), writes=[wdb[i]], dma=True)
                        for f in range(DFB):
                            fc = fb * DFB + f
                            for dc in range(4):
                                P.op("pe", lambda e, i=i, f=f, fc=fc, dc=dc, base=base, w=w: e.matmul(
                                    C.banks[base + dc][:, :w], wdt[i][:, f, dc * 128:(dc + 1) * 128],
                                    a[:, fc, :w], start=(fc == 0), stop=(fc == FC - 1)),
                                    reads=[wdb[i], ab[fc]], writes=[C.bank_b[base + dc]])
                    for dc in range(4):
                        c = dg * 4 + dc
                        P.op("dve", lambda e, c=c, dc=dc, base=base, w=w, s=s, k0=k0: e.scalar_tensor_tensor(
                            out=x[:, c, :w], in0=C.banks[base + dc][:, :w],
                            scalar=vec(s, k0 + 3)[:, c:c + 1], in1=x[:, c, :w],
                            op0=ALU.mult, op1=ALU.add),
                            reads=[C.bank_b[base + dc], mvb, xb[c]], writes=[xb[c]])

            if final_norm:
                emit_rstd(C, x, xb, w, 0)
                for c in range(KC):
                    i = cnt["ho"] % 2
                    cnt["ho"] += 1
                    P.op("dve", lambda e, c=c, i=i, w=w, s=s: e.scalar_tensor_tensor(
                        out=ho[i][:, :w], in0=x[:, c, :w], scalar=vec(s, fin_slot)[:, c:c + 1],
                        in1=C.rstd[:, :w], op0=ALU.mult, op1=ALU.mult),
                        reads=[xb[c], C.rstd_b, mvb], writes=[hob[i]])
                    out_dmas.append(P.op("sp", lambda e, c=c, i=i, t0=t0, w=w: e.dma_start(
                        out=oT[c * 128:(c + 1) * 128, t0:t0 + w], in_=ho[i][:, :w]),
                        reads=[hob[i]], dma=True))
            else:
                for half in range(2):
                    cs = slice(half * 8, half * 8 + 8)
                    out_dmas.append(P.op("sp", lambda e, cs=cs, t0=t0, w=w: e.dma_start(
                        out=chunked(oT)[:, cs, t0:t0 + w], in_=x[:, cs, :w]),
                        reads=xb[cs], dma=True))
            if emit_h:
                emit_rstd(C, x, xb, w, 0)
                for c in range(KC):
                    i = cnt["ho"] % 2
                    cnt["ho"] += 1
                    ti = C.n_tmp % 2
                    C.n_tmp += 1
                    P.op("dve", lambda e, c=c, ti=ti, w=w, s=s: e.scalar_tensor_tensor(
                        out=C.tmp[ti][:, :w], in0=x[:, c, :w], scalar=vec(s, h_slot)[:, c:c + 1],
                        in1=C.rstd[:, :w], op0=ALU.mult, op1=ALU.mult),
                        reads=[xb[c], C.rstd_b, mvb], writes=[C.tmp_b[ti]])
                    P.op("act", lambda e, c=c, i=i, ti=ti, w=w, s=s: e.activation(
                        out=ho[i][:, :w], in_=C.tmp[ti][:, :w], func=AF.Identity,
                        bias=vec(s, h_slot + 1)[:, c:c + 1], scale=1.0),
                        reads=[C.tmp_b[ti], mvb], writes=[hob[i]])
                    out_dmas.append(P.op("sp", lambda e, c=c, i=i, t0=t0, w=w: e.dma_start(
                        out=hT[c * 128:(c + 1) * 128, t0:t0 + w], in_=ho[i][:, :w]),
                        reads=[hob[i]], dma=True))
        P.join("sp", out_dmas)
        P.emit()
    return nc


MODC = NMOD * D // NCORES
NROW = 5


def build_mod_program(depth):
    nc = bass.Bass("TRN2", target_bir_lowering=False)
    scT = nc.dram_tensor("scT", [128, KC * NROW], F32, kind="ExternalInput").ap()
    wm = nc.dram_tensor("wm", [depth, D, MODC], F32, kind="ExternalInput").ap()
    bm = nc.dram_tensor("bm", [depth, MODC], F32, kind="ExternalInput").ap()
    out = nc.dram_tensor("out", [depth, NROW, MODC], F32, kind="ExternalOutput").ap()
    blocks = [(0, 512), (512, 512), (1024, 512), (1536, 512), (2048, 256)]
    with ExitStack() as es:
        P = Prog(nc, es)
        sc = P.sb("sc", [128, KC * NROW], F32)
        scb = P.buf()
        ones = P.sb("ones1", [1, NROW], F32)
        onesb = P.buf()
        P.op("pool", lambda e: e.memset(ones[:], 1.0), writes=[onesb])
        P.op("sp", lambda e: e.dma_start(out=sc[:], in_=scT), writes=[scb], dma=True)
        P.op("act", lambda e: e.activation(out=sc[:], in_=sc[:], func=AF.Silu), reads=[scb], writes=[scb])
        wt = [P.sb(f"wt{i}", [128, KC, 512], F32) for i in range(2)]
        wtb = [[P.buf(), P.buf()] for i in range(2)]
        bt = [P.sb(f"bt{i}", [1, 512], F32) for i in range(2)]
        btb = [P.buf() for i in range(2)]
        ot = [P.sb(f"ot{i}", [NROW, 512], F32) for i in range(2)]
        otb = [P.buf() for i in range(2)]
        banks = [P.ps(f"bk{i}", [128, 512]) for i in range(2)]
        bkb = [P.buf() for i in range(2)]
        n = 0
        outs = []
        for l in range(depth):
            for (c0, cw) in blocks:
                i = n % 2
                n += 1
                for half in range(2):
                    P.op("sp" if half == 0 else "act", lambda e, i=i, l=l, c0=c0, cw=cw, half=half: e.dma_start(
                        out=wt[i][:, half * 8:half * 8 + 8, :cw],
                        in_=wm[l].rearrange("(c p) n -> p c n", p=128)[:, half * 8:half * 8 + 8, c0:c0 + cw]),
                        writes=[wtb[i][half]], dma=True)
                P.op("sp", lambda e, i=i, l=l, c0=c0, cw=cw: e.dma_start(out=bt[i][:, :cw], in_=bm[l:l + 1, c0:c0 + cw]),
                     writes=[btb[i]], dma=True)
                for kc in range(KC):
                    P.op("pe", lambda e, i=i, kc=kc, cw=cw: e.matmul(
                        banks[i][:NROW, :cw], sc[:, kc * NROW:(kc + 1) * NROW], wt[i][:, kc, :cw],
                        start=(kc == 0), stop=False), reads=[wtb[i][kc // 8], scb], writes=[bkb[i]])
                P.op("pe", lambda e, i=i, cw=cw: e.matmul(banks[i][:NROW, :cw], ones[:], bt[i][:, :cw],
                                                         start=False, stop=True),
                     reads=[btb[i], onesb], writes=[bkb[i]])
                P.op("act", lambda e, i=i, cw=cw: e.activation(out=ot[i][:, :cw], in_=banks[i][:NROW, :cw],
                                                              func=AF.Identity),
                     reads=[bkb[i]], writes=[otb[i]])
                outs.append(P.op("sp", lambda e, i=i, l=l, c0=c0, cw=cw: e.dma_start(
                    out=out[l, :, c0:c0 + cw], in_=ot[i][:, :cw]), reads=[otb[i]], dma=True))
        P.join("sp", outs)
        P.emit()
    return nc


CONV_W = 31


def build_conv_program(T, tiles):
    W = max(w for _, w, _ in tiles)
    nc = bass.Bass("TRN2", target_bir_lowering=False)
    hT = nc.dram_tensor("hT", [D, T], F32, kind="ExternalInput").ap()
    w1 = nc.dram_tensor("w1", [D, 2 * D], F32, kind="ExternalInput").ap()
    vecs = nc.dram_tensor("vecs", [128, 5 * KC], F32, kind="ExternalInput").ap()
    wdw = nc.dram_tensor("wdw", [128, KC * CONV_W], F32, kind="ExternalInput").ap()
    mT = nc.dram_tensor("mT", [D, T], F32, kind="ExternalOutput").ap()
    with ExitStack() as es:
        P = Prog(nc, es)
        C = FFNCtx(P, W)
        C.epsb = P.sb("epsb", [128, 1], F32)
        C.eps_bb = P.buf("eps")
        P.op("pool", lambda e: e.memset(C.epsb[:], EPS), writes=[C.eps_bb])
        vv = P.sb("vv", [128, 5 * KC], F32)
        vvb = P.buf()
        P.op("sp", lambda e: e.dma_start(out=vv[:], in_=vecs), writes=[vvb], dma=True)
        wk = P.sb("wk", [128, KC * CONV_W], F32)
        wkb = P.buf()
        P.op("sp", lambda e: e.dma_start(out=wk[:], in_=wdw), writes=[wkb], dma=True)

        def vec(k):
            return vv[:, k * KC:(k + 1) * KC]

        h = P.sb("h", [128, KC, W], BF16)
        hb = [P.buf() for c in range(KC)]
        y = P.sb("y", [128, KC, W], F32)
        yb = [P.buf() for c in range(KC)]
        u = [P.sb(f"u{i}", [128, W], F32) for i in range(2)]
        ub = [P.buf() for i in range(2)]
        sgm = [P.sb(f"sgm{i}", [128, W], F32) for i in range(2)]
        sgmb = [P.buf() for i in range(2)]
        wa = [P.sb(f"wa{i}", [128, KC, 128], BF16) for i in range(2)]
        wab = [P.buf() for i in range(2)]
        wgx = [P.sb(f"wgx{i}", [128, KC, 128], BF16) for i in range(2)]
        wgxb = [P.buf() for i in range(2)]
        mean = P.sb("mean", [128, W], F32)
        meanb = P.buf()
        var = P.sb("var", [128, W], F32)
        varb = P.buf()
        ho = [P.sb(f"ho{i}", [128, W], F32) for i in range(2)]
        hob = [P.buf() for i in range(2)]
        n = {"w": 0, "u": 0, "ho": 0}
        outs = []
        for (t0, w, L) in tiles:
            ns = w // L
            for half in range(2):
                cs = slice(half * 8, half * 8 + 8)
                P.op("pool", lambda e, cs=cs, t0=t0, w=w: e.dma_start(
                    out=h[:, cs, :w], in_=chunked(hT)[:, cs, t0:t0 + w]), writes=hb[cs], dma=True)
            for mc in range(KC):
                i = n["w"] % 2
                n["w"] += 1
                P.op("pool", lambda e, i=i, mc=mc: e.dma_start(
                    out=wa[i][:], in_=chunked(w1)[:, :, mc * 128:(mc + 1) * 128]), writes=[wab[i]], dma=True)
                P.op("pool", lambda e, i=i, mc=mc: e.dma_start(
                    out=wgx[i][:], in_=chunked(w1)[:, :, D + mc * 128:D + (mc + 1) * 128]),
                    writes=[wgxb[i]], dma=True)
                par = mc % 2
                ba, bg = 2 + 2 * par, 3 + 2 * par
                for kc in range(KC):
                    P.op("pe", lambda e, i=i, kc=kc, ba=ba, w=w: e.matmul(
                        C.banks[ba][:, :w], wa[i][:, kc, :], h[:, kc, :w], start=(kc == 0), stop=(kc == KC - 1)),
                        reads=[wab[i], hb[kc]], writes=[C.bank_b[ba]])
                for kc in range(KC):
                    P.op("pe", lambda e, i=i, kc=kc, bg=bg, w=w: e.matmul(
                        C.banks[bg][:, :w], wgx[i][:, kc, :], h[:, kc, :w], start=(kc == 0), stop=(kc == KC - 1)),
                        reads=[wgxb[i], hb[kc]], writes=[C.bank_b[bg]])
                ui = n["u"] % 2
                n["u"] += 1
                P.op("act", lambda e, ui=ui, bg=bg, mc=mc, w=w: e.activation(
                    out=sgm[ui][:, :w], in_=C.banks[bg][:, :w], func=AF.Sigmoid,
                    bias=vec(1)[:, mc:mc + 1], scale=1.0), reads=[C.bank_b[bg], vvb], writes=[sgmb[ui]])
                P.op("dve", lambda e, ui=ui, ba=ba, mc=mc, w=w: e.scalar_tensor_tensor(
                    out=u[ui][:, :w], in0=C.banks[ba][:, :w], scalar=vec(0)[:, mc:mc + 1], in1=sgm[ui][:, :w],
                    op0=ALU.add, op1=ALU.mult), reads=[C.bank_b[ba], sgmb[ui], vvb], writes=[ub[ui]])
                u3 = u[ui][:, :w].rearrange("p (s l) -> p s l", l=L)
                y3 = y[:, mc, :w].rearrange("p (s l) -> p s l", l=L)
                P.op("dve", lambda e, ui=ui, mc=mc, w=w: e.tensor_scalar(
                    out=y[:, mc, :w], in0=u[ui][:, :w], scalar1=wk[:, mc * CONV_W + 15:mc * CONV_W + 16],
                    scalar2=vec(2)[:, mc:mc + 1], op0=ALU.mult, op1=ALU.add),
                    reads=[ub[ui], wkb, vvb], writes=[yb[mc]])
                for k in range(CONV_W):
                    o = k - 15
                    if o == 0 or abs(o) >= L:
                        continue
                    a0, a1 = max(0, -o), min(L, L - o)
                    P.op("dve", lambda e, u3=u3, y3=y3, mc=mc, k=k, a0=a0, a1=a1, o=o: e.scalar_tensor_tensor(
                        out=y3[:, :, a0:a1], in0=u3[:, :, a0 + o:a1 + o],
                        scalar=wk[:, mc * CONV_W + k:mc * CONV_W + k + 1], in1=y3[:, :, a0:a1],
                        op0=ALU.mult, op1=ALU.add), reads=[ub[ui], wkb, yb[mc]], writes=[yb[mc]])
                P.op("pe", lambda e, mc=mc, w=w: e.matmul(C.banks[0][:, :w], C.ones[:], y[:, mc, :w],
                                                         start=(mc == 0), stop=(mc == KC - 1)),
                     reads=[yb[mc], C.ones_b], writes=[C.bank_b[0]])
                si = C.n_sq % 2
                C.n_sq += 1
                P.op("act", lambda e, mc=mc, si=si, w=w: e.activation(out=C.sq[si][:, :w], in_=y[:, mc, :w],
                                                                     func=AF.Square),
                     reads=[yb[mc]], writes=[C.sq_b[si]])
                P.op("pe", lambda e, mc=mc, si=si, w=w: e.matmul(C.banks[1][:, :w], C.ones[:], C.sq[si][:, :w],
                                                                start=(mc == 0), stop=(mc == KC - 1)),
                     reads=[C.sq_b[si], C.ones_b], writes=[C.bank_b[1]])
            P.op("act", lambda e, w=w: e.activation(out=mean[:, :w], in_=C.banks[0][:, :w], func=AF.Identity,
                                                    scale=1.0 / D), reads=[C.bank_b[0]], writes=[meanb])
            P.op("dve", lambda e, w=w: e.tensor_tensor(out=var[:, :w], in0=mean[:, :w], in1=mean[:, :w],
                                                       op=ALU.mult), reads=[meanb], writes=[varb])
            P.op("dve", lambda e, w=w: e.scalar_tensor_tensor(
                out=var[:, :w], in0=C.banks[1][:, :w], scalar=1.0 / D, in1=var[:, :w],
                op0=ALU.mult, op1=ALU.subtract), reads=[C.bank_b[1], varb], writes=[varb])
            P.op("act", lambda e, w=w: e.activation(out=C.rstd[:, :w], in_=var[:, :w], func=AF.Sqrt,
                                                    bias=C.epsb[:, 0:1], scale=1.0),
                 reads=[varb, C.eps_bb], writes=[C.rstd_b])
            P.op("dve", lambda e, w=w: e.reciprocal(out=C.rstd[:, :w], in_=C.rstd[:, :w]),
                 reads=[C.rstd_b], writes=[C.rstd_b])
            for c in range(KC):
                ti = C.n_tmp % 2
                C.n_tmp += 1
                i = n["ho"] % 2
                n["ho"] += 1
                P.op("dve", lambda e, c=c, ti=ti, w=w: e.tensor_tensor(
                    out=C.tmp[ti][:, :w], in0=y[:, c, :w], in1=mean[:, :w], op=ALU.subtract),
                    reads=[yb[c], meanb], writes=[C.tmp_b[ti]])
                P.op("dve", lambda e, c=c, ti=ti, w=w: e.scalar_tensor_tensor(
                    out=C.tmp[ti][:, :w], in0=C.tmp[ti][:, :w], scalar=vec(3)[:, c:c + 1], in1=C.rstd[:, :w],
                    op0=ALU.mult, op1=ALU.mult), reads=[C.tmp_b[ti], C.rstd_b, vvb], writes=[C.tmp_b[ti]])
                P.op("act", lambda e, c=c, ti=ti, i=i, w=w: e.activation(
                    out=ho[i][:, :w], in_=C.tmp[ti][:, :w], func=AF.Silu, bias=vec(4)[:, c:c + 1], scale=1.0),
                    reads=[C.tmp_b[ti], vvb], writes=[hob[i]])
                outs.append(P.op("sp", lambda e, c=c, i=i, t0=t0, w=w: e.dma_start(
                    out=mT[c * 128:(c + 1) * 128, t0:t0 + w], in_=ho[i][:, :w]), reads=[hob[i]], dma=True))
        P.join("sp", outs)
        P.emit()
    return nc


GW = 256
NG = 8


def build_fnet_program(seqs):
    nc = bass.Bass("TRN2", target_bir_lowering=False)
    cw_d = nc.dram_tensor("cw", [128, 2 * 512], BF16, kind="ExternalInput").ap()
    io = {}
    for (name, L, NK) in seqs:
        io[name] = (
            nc.dram_tensor(f"hT_{name}", [D, L], F32, kind="ExternalInput").ap(),
            nc.dram_tensor(f"cl_{name}", [L, NK], BF16, kind="ExternalInput").ap(),
            nc.dram_tensor(f"sl_{name}", [L, NK], BF16, kind="ExternalInput").ap(),
            nc.dram_tensor(f"fT_{name}", [D, NK], F32, kind="ExternalOutput").ap(),
        )
    LMAX = max(L for _, L, _ in seqs)
    with ExitStack() as es:
        P = Prog(nc, es)
        banks = [P.ps(f"bank{i}", [128, 512]) for i in range(8)]
        bkb = [P.buf() for i in range(8)]
        cw = P.sb("cw_sb", [128, 2, 512], BF16)
        cwb = P.buf()
        P.op("sp", lambda e: e.dma_start(out=cw[:].rearrange("p a b -> p (a b)"), in_=cw_d), writes=[cwb], dma=True)
        hg = [P.sb(f"hg{i}", [128, 2, LMAX], BF16) for i in range(2)]
        hgb = [P.buf() for i in range(2)]
        A = P.sb("A", [128, LMAX // 128, 512], BF16)
        Ab = [P.buf() for i in range(LMAX // 128)]
        NB = 4
        clt = [P.sb(f"clt{i}", [128, 8, 512], BF16) for i in range(NB)]
        cltb = [P.buf() for i in range(NB)]
        slt = [P.sb(f"slt{i}", [128, 8, 512], BF16) for i in range(NB)]
        sltb = [P.buf() for i in range(NB)]
        fo = [P.sb(f"fo{i}", [128, 512], F32) for i in range(4)]
        fob = [P.buf() for i in range(4)]
        n = {"hg": 0, "m": 0, "fo": 0, "a": 0}
        outs = []
        for (name, L, NK) in seqs:
            hT, cl, sl, fT = io[name]
            NCH = L // 128
            for g in range(NG):
                gi = n["hg"] % 2
                n["hg"] += 1
                P.op("pool", lambda e, gi=gi, g=g, L=L, hT=hT: e.dma_start(
                    out=hg[gi][:, :, :L], in_=chunked(hT)[:, 2 * g:2 * g + 2, :]), writes=[hgb[gi]], dma=True)
                for nch in range(NCH):
                    bk = n["a"] % 2
                    n["a"] += 1
                    for kc in range(2):
                        P.op("pe", lambda e, gi=gi, nch=nch, kc=kc, bk=bk: e.matmul(
                            banks[bk][:, :], hg[gi][:, kc, nch * 128:(nch + 1) * 128], cw[:, kc, :],
                            start=(kc == 0), stop=(kc == 1)), reads=[hgb[gi], cwb], writes=[bkb[bk]])
                    P.op("act" if nch % 2 == 0 else "dve",
                         (lambda e, nch=nch, bk=bk: e.activation(out=A[:, nch, :], in_=banks[bk][:, :], func=AF.Identity))
                         if nch % 2 == 0 else
                         (lambda e, nch=nch, bk=bk: e.tensor_copy(out=A[:, nch, :], in_=banks[bk][:, :])),
                         reads=[bkb[bk]], writes=[Ab[nch]])
                kblocks = [(k0, min(512, NK - k0)) for k0 in range(0, NK, 512)]
                for (k0, kw) in kblocks:
                    pb = [2 + 2 * (n["fo"] % 2), 3 + 2 * (n["fo"] % 2)]
                    nsub = (NCH + 7) // 8
                    for sb_ in range(nsub):
                        r0 = sb_ * 8
                        rn = min(8, NCH - r0)
                        mi = n["m"] % NB
                        n["m"] += 1
                        P.op("sp", lambda e, mi=mi, r0=r0, rn=rn, k0=k0, kw=kw, cl=cl: e.dma_start(
                            out=clt[mi][:, :rn, :kw],
                            in_=cl[r0 * 128:(r0 + rn) * 128, k0:k0 + kw].rearrange("(c p) n -> p c n", p=128)),
                            writes=[cltb[mi]], dma=True)
                        P.op("act", lambda e, mi=mi, r0=r0, rn=rn, k0=k0, kw=kw, sl=sl: e.dma_start(
                            out=slt[mi][:, :rn, :kw],
                            in_=sl[r0 * 128:(r0 + rn) * 128, k0:k0 + kw].rearrange("(c p) n -> p c n", p=128)),
                            writes=[sltb[mi]], dma=True)
                        for r in range(rn):
                            nch = r0 + r
                            for mcx in range(2):
                                P.op("pe", lambda e, mi=mi, r=r, nch=nch, mcx=mcx, kw=kw, pb=pb: e.matmul(
                                    banks[pb[mcx]][:, :kw], A[:, nch, mcx * 128:(mcx + 1) * 128], clt[mi][:, r, :kw],
                                    start=(nch == 0), stop=False), reads=[Ab[nch], cltb[mi]], writes=[bkb[pb[mcx]]])
                                P.op("pe", lambda e, mi=mi, r=r, nch=nch, mcx=mcx, kw=kw, pb=pb, NCH=NCH: e.matmul(
                                    banks[pb[mcx]][:, :kw], A[:, nch, 256 + mcx * 128:256 + (mcx + 1) * 128],
                                    slt[mi][:, r, :kw], start=False, stop=(nch == NCH - 1)),
                                    reads=[Ab[nch], sltb[mi]], writes=[bkb[pb[mcx]]])
                    for mcx in range(2):
                        fi = n["fo"] % 2
                        P.op("act", lambda e, fi=fi, mcx=mcx, kw=kw, pb=pb: e.activation(
                            out=fo2(fo, fi, mcx)[:, :kw], in_=banks[pb[mcx]][:, :kw],
                            func=AF.Identity), reads=[bkb[pb[mcx]]], writes=[fob2(fob, fi, mcx)])
                        c = 2 * g + mcx
                        outs.append(P.op("sp", lambda e, fi=fi, mcx=mcx, c=c, k0=k0, kw=kw, fT=fT: e.dma_start(
                            out=fT[c * 128:(c + 1) * 128, k0:k0 + kw], in_=fo2(fo, fi, mcx)[:, :kw]),
                            reads=[fob2(fob, fi, mcx)], dma=True))
                    n["fo"] += 1
        P.join("sp", outs)
        P.emit()
    return nc


def fo2(fo, fi, mcx):
    return fo[(2 * fi + mcx) % len(fo)]


def fob2(fob, fi, mcx):
    return fob[(2 * fi + mcx) % len(fob)]


GLA_HK = 1024
GLA_HV = 2048
GLA_IN = 6176
GLA_R = 16


def build_glaproj_program(T, tiles):
    W = max(w for _, w in tiles)
    nc = bass.Bass("TRN2", target_bir_lowering=False)
    hT = nc.dram_tensor("hT", [D, T], F32, kind="ExternalInput").ap()
    win = nc.dram_tensor("win", [D, GLA_IN], F32, kind="ExternalInput").ap()
    wgu = nc.dram_tensor("wgu", [GLA_R, 2 * GLA_HK], F32, kind="ExternalInput").ap()
    bg = nc.dram_tensor("bg", [128, 16], F32, kind="ExternalInput").ap()
    pT = nc.dram_tensor("pT", [6144, T], F32, kind="ExternalOutput").ap()
    gT = nc.dram_tensor("gT", [2 * GLA_HK, T], F32, kind="ExternalOutput").ap()
    with ExitStack() as es:
        P = Prog(nc, es)
        banks = [P.ps(f"bank{i}", [128, 512]) for i in range(8)]
        bkb = [P.buf() for i in range(8)]
        h = P.sb("h", [128, KC, W], BF16)
        hb = [P.buf() for c in range(KC)]
        wt = [P.sb(f"wt{i}", [128, KC, 128], BF16) for i in range(3)]
        wtb = [P.buf() for i in range(3)]
        wz = P.sb("wz", [128, KC, 32], BF16)
        wzb = P.buf()
        P.op("pool", lambda e: e.dma_start(out=wz[:], in_=chunked(win)[:, :, 6144:6176]), writes=[wzb], dma=True)
        wg = P.sb("wg", [GLA_R, 2 * GLA_HK], F32)
        wgb = P.buf()
        P.op("sp", lambda e: e.dma_start(out=wg[:], in_=wgu), writes=[wgb], dma=True)
        bgs = P.sb("bgs", [128, 16], F32)
        bgb = P.buf()
        P.op("sp", lambda e: e.dma_start(out=bgs[:], in_=bg), writes=[bgb], dma=True)
        z = [P.sb(f"z{i}", [GLA_R, W], F32) for i in range(2)]
        zb = [P.buf() for i in range(2)]
        ot = [P.sb(f"ot{i}", [128, W], F32) for i in range(3)]
        otb = [P.buf() for i in range(3)]
        sg = [P.sb(f"sgx{i}", [128, W], F32) for i in range(2)]
        sgb = [P.buf() for i in range(2)]
        n = {"w": 0, "o": 0, "b": 0, "s": 0}
        outs = []
        for (t0, w) in tiles:
            for half in range(2):
                cs = slice(half * 8, half * 8 + 8)
                P.op("pool", lambda e, cs=cs, t0=t0, w=w: e.dma_start(
                    out=h[:, cs, :w], in_=chunked(hT)[:, cs, t0:t0 + w]), writes=hb[cs], dma=True)
            for mc in range(48):
                i = n["w"] % 3
                n["w"] += 1
                P.op("pool", lambda e, i=i, mc=mc: e.dma_start(
                    out=wt[i][:], in_=chunked(win)[:, :, mc * 128:(mc + 1) * 128]), writes=[wtb[i]], dma=True)
                bk = n["b"] % 4
                n["b"] += 1
                for kc in range(KC):
                    P.op("pe", lambda e, i=i, kc=kc, bk=bk, w=w: e.matmul(
                        banks[bk][:, :w], wt[i][:, kc, :], h[:, kc, :w], start=(kc == 0), stop=(kc == KC - 1)),
                        reads=[wtb[i], hb[kc]], writes=[bkb[bk]])
                oi = n["o"] % 3
                n["o"] += 1
                if mc < 8:
                    fn = lambda e, oi=oi, bk=bk, w=w: e.activation(out=ot[oi][:, :w], in_=banks[bk][:, :w],
                                                                   func=AF.Identity, scale=1.0 / 16.0)
                    eng = "act"
                elif mc < 32:
                    fn = lambda e, oi=oi, bk=bk, w=w: e.tensor_copy(out=ot[oi][:, :w], in_=banks[bk][:, :w])
                    eng = "dve"
                else:
                    fn = lambda e, oi=oi, bk=bk, w=w: e.activation(out=ot[oi][:, :w], in_=banks[bk][:, :w],
                                                                   func=AF.Silu)
                    eng = "act"
                P.op(eng, fn, reads=[bkb[bk]], writes=[otb[oi]])
                outs.append(P.op("sp", lambda e, oi=oi, mc=mc, t0=t0, w=w: e.dma_start(
                    out=pT[mc * 128:(mc + 1) * 128, t0:t0 + w], in_=ot[oi][:, :w]), reads=[otb[oi]], dma=True))
            for d in range(2):
                bk = 4 + d
                for kc in range(KC):
                    P.op("pe", lambda e, d=d, kc=kc, bk=bk, w=w: e.matmul(
                        banks[bk][:GLA_R, :w], wz[:, kc, d * 16:(d + 1) * 16], h[:, kc, :w],
                        start=(kc == 0), stop=(kc == KC - 1)), reads=[wzb, hb[kc]], writes=[bkb[bk]])
                P.op("dve", lambda e, d=d, bk=bk, w=w: e.tensor_copy(out=z[d][:, :w], in_=banks[bk][:GLA_R, :w]),
                     reads=[bkb[bk]], writes=[zb[d]])
            for d in range(2):
                for j in range(8):
                    bk = 6 + (j % 2)
                    P.op("pe", lambda e, d=d, j=j, bk=bk, w=w: e.matmul(
                        banks[bk][:, :w], wg[:, d * GLA_HK + j * 128:d * GLA_HK + (j + 1) * 128], z[d][:, :w],
                        start=True, stop=True), reads=[wgb, zb[d]], writes=[bkb[bk]])
                    si = n["s"] % 2
                    n["s"] += 1
                    oi = n["o"] % 3
                    n["o"] += 1
                    P.op("act", lambda e, d=d, j=j, bk=bk, si=si, w=w: e.activation(
                        out=sg[si][:, :w], in_=banks[bk][:, :w], func=AF.Sigmoid,
                        bias=bgs[:, d * 8 + j:d * 8 + j + 1], scale=1.0), reads=[bkb[bk], bgb], writes=[sgb[si]])
                    P.op("act", lambda e, si=si, oi=oi, w=w: e.activation(
                        out=ot[oi][:, :w], in_=sg[si][:, :w], func=AF.Ln), reads=[sgb[si]], writes=[otb[oi]])
                    r = d * 8 + j
                    outs.append(P.op("sp", lambda e, oi=oi, r=r, t0=t0, w=w: e.dma_start(
                        out=gT[r * 128:(r + 1) * 128, t0:t0 + w], in_=ot[oi][:, :w]), reads=[otb[oi]], dma=True))
        P.join("sp", outs)
        P.emit()
    return nc


class Rot:
    def __init__(self, P, name, shape, dt, n):
        self.t = [P.sb(f"{name}{i}", shape, dt) for i in range(n)]
        self.b = [P.buf(f"{name}{i}") for i in range(n)]
        self.n = n
        self.k = 0

    def next(self):
        i = self.k % self.n
        self.k += 1
        return self.t[i], self.b[i]


def build_glascan_program(NU, NCH):
    S_ = NCH * 128
    nc = bass.Bass("TRN2", target_bir_lowering=False)
    qT = nc.dram_tensor("qT", [NU, 256, S_], F32, kind="ExternalInput").ap()
    kT = nc.dram_tensor("kT", [NU, 256, S_], F32, kind="ExternalInput").ap()
    vv = nc.dram_tensor("v", [NU, S_, 512], F32, kind="ExternalInput").ap()
    gg = nc.dram_tensor("g", [NU, S_, 256], F32, kind="ExternalInput").ap()
    tri_d = nc.dram_tensor("tri", [128, 128], F32, kind="ExternalInput").ap()
    mask_d = nc.dram_tensor("mask", [128, 128], F32, kind="ExternalInput").ap()
    ident_d = nc.dram_tensor("ident", [128, 128], BF16, kind="ExternalInput").ap()
    oo = nc.dram_tensor("o", [NU, S_, 512], F32, kind="ExternalOutput").ap()
    with ExitStack() as es:
        P = Prog(nc, es)
        bankB = P.ps("bankB", [128, 2, 128]); bBb = P.buf()
        bankA = P.ps("bankA", [128, 128]); bAb = P.buf()
        bankO = [P.ps(f"bankO{i}", [128, 512]) for i in range(2)]; bOb = [P.buf() for i in range(2)]
        bankT = P.ps("bankT", [128, 2, 128], BF16); bTb = P.buf()
        bankU = [P.ps(f"bankU{j}", [128, 512]) for j in range(2)]; bUb = [P.buf() for j in range(2)]
        tri = P.sb("tri_sb", [128, 128], F32); trib = P.buf()
        mask = P.sb("mask_sb", [128, 128], F32); maskb = P.buf()
        ident = P.sb("ident_sb", [128, 128], BF16); identb = P.buf()
        P.op("sp", lambda e: e.dma_start(out=tri[:], in_=tri_d), writes=[trib], dma=True)
        P.op("sp", lambda e: e.dma_start(out=mask[:], in_=mask_d), writes=[maskb], dma=True)
        P.op("sp", lambda e: e.dma_start(out=ident[:], in_=ident_d), writes=[identb], dma=True)
        Sst = P.sb("Sst", [128, 2, 512], F32); Sb = [P.buf() for j in range(2)]
        Sp = P.sb("Sp", [128, 2, 512], BF16); Spb = [P.buf() for j in range(2)]
        rq = Rot(P, "q", [128, 2, 128], F32, 3)
        rk = Rot(P, "k", [128, 2, 128], F32, 3)
        rv = Rot(P, "v", [128, 512], BF16, 3)
        rg = Rot(P, "g", [128, 256], F32, 3)
        rB = Rot(P, "Bsb", [128, 2, 128], F32, 2)
        rs = Rot(P, "sc", [128, 5, 2], F32, 2)
        rEq = Rot(P, "Eq", [128, 2, 128], F32, 2)
        rEk = Rot(P, "Ek", [128, 2, 128], F32, 2)
        rqi = Rot(P, "qi", [128, 2, 128], BF16, 2)
        rki = Rot(P, "ki", [128, 2, 128], BF16, 2)
        ram = Rot(P, "am", [128, 128], BF16, 2)
        rkit = Rot(P, "kit", [128, 2, 128], BF16, 2)
        ro = Rot(P, "osb", [128, 512], F32, 2)
        rtu = Rot(P, "tu", [128, 512], F32, 2)
        outs = []
        no = 0
        for u in range(NU):
            for j in range(2):
                P.op("dve", lambda e, j=j: e.memset(Sst[:, j, :], 0.0), writes=[Sb[j]])
            for n in range(NCH):
                ts = slice(n * 128, (n + 1) * 128)
                q, qb = rq.next(); k, kb = rk.next(); v, vb = rv.next(); g, gb = rg.next()
                P.op("sp", lambda e, q=q, u=u, ts=ts: e.dma_start(
                    out=q[:], in_=qT[u].rearrange("(j p) t -> p j t", p=128)[:, :, ts]), writes=[qb], dma=True)
                P.op("sp", lambda e, k=k, u=u, ts=ts: e.dma_start(
                    out=k[:], in_=kT[u].rearrange("(j p) t -> p j t", p=128)[:, :, ts]), writes=[kb], dma=True)
                P.op("pool", lambda e, v=v, u=u, ts=ts: e.dma_start(out=v[:], in_=vv[u, ts, :]), writes=[vb], dma=True)
                P.op("sp", lambda e, g=g, u=u, ts=ts: e.dma_start(out=g[:], in_=gg[u, ts, :]), writes=[gb], dma=True)
                for j in range(2):
                    P.op("pe", lambda e, g=g, j=j: e.matmul(bankB[:, j, :], g[:, j * 128:(j + 1) * 128], tri[:],
                                                           start=True, stop=True),
                         reads=[gb, trib], writes=[bBb])
                B, Bb = rB.next()
                P.op("act", lambda e, B=B: e.activation(out=B[:], in_=bankB[:], func=AF.Identity),
                     reads=[bBb], writes=[Bb])
                sc, scb = rs.next()
                P.op("dve", lambda e, B=B, sc=sc: e.tensor_scalar(
                    out=sc[:, 0, :], in0=B[:, :, 63], scalar1=-1.0, scalar2=None, op0=ALU.mult),
                    reads=[Bb], writes=[scb])
                P.op("dve", lambda e, B=B, sc=sc: e.tensor_tensor(
                    out=sc[:, 1, :], in0=B[:, :, 127], in1=sc[:, 0, :], op=ALU.add), reads=[Bb, scb], writes=[scb])
                P.op("act", lambda e, B=B, sc=sc: e.activation(out=sc[:, 2, :], in_=B[:, :, 63], func=AF.Exp),
                     reads=[Bb, scb], writes=[scb])
                P.op("act", lambda e, B=B, sc=sc: e.activation(out=sc[:, 3, :], in_=B[:, :, 127], func=AF.Exp),
                     reads=[Bb, scb], writes=[scb])
                P.op("act", lambda e, sc=sc: e.activation(out=sc[:, 4, :], in_=sc[:, 1, :], func=AF.Exp),
                     reads=[scb], writes=[scb])
                Eq, Eqb = rEq.next(); Ek, Ekb = rEk.next()
                for j in range(2):
                    P.op("act", lambda e, B=B, sc=sc, Eq=Eq, j=j: e.activation(
                        out=Eq[:, j, :], in_=B[:, j, :], func=AF.Exp, bias=sc[:, 0, j:j + 1], scale=1.0),
                        reads=[Bb, scb], writes=[Eqb])
                    P.op("act", lambda e, B=B, Ek=Ek, j=j: e.activation(
                        out=Ek[:, j, :], in_=B[:, j, :], func=AF.Exp, bias=B[:, j, 63:64], scale=-1.0),
                        reads=[Bb], writes=[Ekb])
                qi, qib = rqi.next(); ki, kib = rki.next()
                P.op("dve", lambda e, q=q, Eq=Eq, qi=qi: e.tensor_tensor(out=qi[:], in0=q[:], in1=Eq[:], op=ALU.mult),
                     reads=[qb, Eqb], writes=[qib])
                P.op("dve", lambda e, k=k, Ek=Ek, ki=ki: e.tensor_tensor(out=ki[:], in0=k[:], in1=Ek[:], op=ALU.mult),
                     reads=[kb, Ekb], writes=[kib])
                for j in range(2):
                    P.op("act", lambda e, sc=sc, j=j: e.activation(
                        out=Sp[:, j, :], in_=Sst[:, j, :], func=AF.Identity, scale=sc[:, 2, j:j + 1]),
                        reads=[Sb[j], scb], writes=[Spb[j]])
                for j in range(2):
                    P.op("pe", lambda e, ki=ki, qi=qi, j=j: e.matmul(bankA[:], ki[:, j, :], qi[:, j, :],
                                                                    start=(j == 0), stop=(j == 1)),
                         reads=[kib, qib], writes=[bAb])
                am, amb = ram.next()
                P.op("dve", lambda e, am=am: e.tensor_tensor(out=am[:], in0=bankA[:], in1=mask[:], op=ALU.mult),
                     reads=[bAb, maskb], writes=[amb])
                oi = no % 2
                no += 1
                P.op("pe", lambda e, am=am, v=v, oi=oi: e.matmul(bankO[oi][:], am[:], v[:], start=True, stop=False),
                     reads=[amb, vb], writes=[bOb[oi]])
                for j in range(2):
                    P.op("pe", lambda e, qi=qi, j=j, oi=oi: e.matmul(bankO[oi][:], qi[:, j, :], Sp[:, j, :],
                                                                    start=False, stop=(j == 1)),
                         reads=[qib, Spb[j]], writes=[bOb[oi]])
                osb, osbb = ro.next()
                P.op("act", lambda e, osb=osb, oi=oi: e.activation(out=osb[:], in_=bankO[oi][:], func=AF.Identity),
                     reads=[bOb[oi]], writes=[osbb])
                outs.append(P.op("sp", lambda e, osb=osb, u=u, ts=ts: e.dma_start(out=oo[u, ts, :], in_=osb[:]),
                                 reads=[osbb], dma=True))
                for j in range(2):
                    P.op("pe", lambda e, ki=ki, j=j: e.transpose(out=bankT[:, j, :], in_=ki[:, j, :], identity=ident[:]),
                         reads=[kib, identb], writes=[bTb])
                kit, kitb = rkit.next()
                P.op("dve", lambda e, kit=kit: e.tensor_copy(out=kit[:], in_=bankT[:]), reads=[bTb], writes=[kitb])
                for j in range(2):
                    P.op("pe", lambda e, kit=kit, v=v, j=j: e.matmul(bankU[j][:], kit[:, j, :], v[:],
                                                                    start=True, stop=True),
                         reads=[kitb, vb], writes=[bUb[j]])
                    tu, tub = rtu.next()
                    P.op("act", lambda e, tu=tu, sc=sc, j=j: e.activation(
                        out=tu[:], in_=bankU[j][:], func=AF.Identity, scale=sc[:, 4, j:j + 1]),
                        reads=[bUb[j], scb], writes=[tub])
                    P.op("dve", lambda e, tu=tu, sc=sc, j=j: e.scalar_tensor_tensor(
                        out=Sst[:, j, :], in0=Sst[:, j, :], scalar=sc[:, 3, j:j + 1], in1=tu[:],
                        op0=ALU.mult, op1=ALU.add), reads=[Sb[j], tub, scb], writes=[Sb[j]])
        P.join("sp", outs)
        P.emit()
    return nc


import ml_dtypes

BATCH = 4
SEQ = 4096
CTX = 256
DEPTH = 4
HALF = SEQ // 2
CH = CTX // 2
TL = HALF + CH
CORES = list(range(NCORES))
_PROGS = {}


def _prog(key, builder):
    if key not in _PROGS:
        _PROGS[key] = builder()
    return _PROGS[key]


def _run(nc, in_maps):
    res = run_bass_kernel_spmd(nc, in_maps, core_ids=CORES)
    return res.results


def _fm(v):
    return np.ascontiguousarray(np.asarray(v, np.float32).reshape(-1, 128).T)


def _mods(sets):
    return np.ascontiguousarray(
        np.concatenate([_fm(v) for s in sets for v in s], axis=1))


def _cat(parts):
    return np.ascontiguousarray(np.concatenate(parts, axis=1))


TILES_L = [(0, 512, 0), (512, 512, 0), (1024, 512, 0), (1536, 512, 0)]
TILES_LC = TILES_L + [(2048, 128, 1)]


def kernel(x, c, ctx, c_ctx, w_mod, b_mod, norm_g, ffn_w_gate, ffn_w_up, ffn_w_down,
           gla_w_in, gla_w_gate_up, gla_b_gate, gla_g_head, gla_w_out,
           fnet_w_out, fnet_b_out,
           cm_w_pw1, cm_b_pw1, cm_w_dw, cm_b_dw, cm_ln_g, cm_ln_b, cm_w_pw2, cm_b_pw2,
           final_g):
    f32 = lambda a: np.asarray(a, dtype=np.float32)
    x, c, ctx, c_ctx = f32(x), f32(c), f32(ctx), f32(c_ctx)
    w_mod, b_mod, norm_g = f32(w_mod), f32(b_mod), f32(norm_g)
    ffn_w_gate, ffn_w_up, ffn_w_down = f32(ffn_w_gate), f32(ffn_w_up), f32(ffn_w_down)
    gla_w_in, gla_w_gate_up, gla_b_gate = f32(gla_w_in), f32(gla_w_gate_up), f32(gla_b_gate)
    gla_g_head, gla_w_out = f32(gla_g_head), f32(gla_w_out)
    fnet_w_out, fnet_b_out = f32(fnet_w_out), f32(fnet_b_out)
    cm_w_pw1, cm_b_pw1, cm_w_dw, cm_b_dw = f32(cm_w_pw1), f32(cm_b_pw1), f32(cm_w_dw), f32(cm_b_dw)
    cm_ln_g, cm_ln_b, cm_w_pw2, cm_b_pw2 = f32(cm_ln_g), f32(cm_ln_b), f32(cm_w_pw2), f32(cm_b_pw2)
    final_g = f32(final_g)
    zeros_d = np.zeros(D, np.float32)

    sc = np.concatenate([c, c_ctx[None]], axis=0)
    scT = np.ascontiguousarray(sc.reshape(NROW, KC, 128).transpose(2, 1, 0)).reshape(128, KC * NROW)
    nc_mod = _prog("mod", lambda: build_mod_program(DEPTH))
    ims = [{"scT": scT,
            "wm": np.ascontiguousarray(w_mod[:, :, k * MODC:(k + 1) * MODC]),
            "bm": np.ascontiguousarray(b_mod[:, k * MODC:(k + 1) * MODC])} for k in CORES]
    r = _run(nc_mod, ims)
    mod = np.concatenate([r[k]["out"] for k in CORES], axis=2).reshape(DEPTH, NROW, NMOD, D)

    xT = []
    for k in CORES:
        b, hf = k // 2, k % 2
        xT.append(_cat([x[b, hf * HALF:(hf + 1) * HALF].T, ctx[b, hf * CH:(hf + 1) * CH].T]))

    out = np.zeros((BATCH, SEQ, D), np.float32)
    for i in range(DEPTH):
        kind, j, last = i % 3, i // 3, i == DEPTH - 1
        rows = lambda k: (k // 2, BATCH)
        nc_head = _prog("head", lambda: build_ffn_program(TL, TILES_LC, 2, 1, False, True, False))
        ims = []
        for k in CORES:
            sets = [[norm_g[i, 0], mod[i, rw, 0], mod[i, rw, 1], mod[i, rw, 2],
                     norm_g[i, 1], mod[i, rw, 3], mod[i, rw, 4]] for rw in rows(k)]
            ims.append({"xT": xT[k], "mods": _mods(sets), "wg0": ffn_w_gate[i, 0], "wu0": ffn_w_up[i, 0],
                        "wd0": ffn_w_down[i, 0]})
        r = _run(nc_head, ims)
        xT = [r[k]["oT"] for k in CORES]
        hT = [r[k]["hT"] for k in CORES]
        H_c = [_cat([hT[2 * b][:, HALF:], hT[2 * b + 1][:, HALF:]]) for b in range(BATCH)]

        tail_extra = [dict() for _ in CORES]
        if kind == 0:
            nc_gp = _prog("gproj", lambda: build_glaproj_program(TL, [(t0, w) for t0, w, _ in TILES_LC]))
            wgu = _cat([gla_w_gate_up[j, 0], gla_w_gate_up[j, 1]])
            bg = _fm(gla_b_gate[j].reshape(-1))
            r = _run(nc_gp, [{"hT": hT[k], "win": gla_w_in[j], "wgu": wgu, "bg": bg} for k in CORES])
            pT = [r[k]["pT"] for k in CORES]
            gT = [r[k]["gT"] for k in CORES]
            P_l = [_cat([pT[2 * b][:, :HALF], pT[2 * b + 1][:, :HALF]]) for b in range(BATCH)]
            P_c = [_cat([pT[2 * b][:, HALF:], pT[2 * b + 1][:, HALF:]]) for b in range(BATCH)]
            G_l = [_cat([gT[2 * b][:, :HALF], gT[2 * b + 1][:, :HALF]]) for b in range(BATCH)]
            G_c = [_cat([gT[2 * b][:, HALF:], gT[2 * b + 1][:, HALF:]]) for b in range(BATCH)]
            NCH = (CTX + SEQ) // 128
            nc_sc = _prog("gscan", lambda: build_glascan_program(4, NCH))
            s_, t_ = np.arange(128)[:, None], np.arange(128)[None, :]
            maskc = (s_ <= t_).astype(np.float32)
            tric = (maskc / 16.0).astype(np.float32)
            identc = np.eye(128).astype(ml_dtypes.bfloat16)

            def seq(ac, al, d):
                if d == 0:
                    return np.concatenate([ac, al], axis=1)
                return np.concatenate([ac[:, ::-1], al[:, ::-1]], axis=1)

            ims = []
            for k in CORES:
                b, hf = k // 2, k % 2
                qs, ks, vs, gs = [], [], [], []
                for hl in range(2):
                    h = 2 * hf + hl
                    for d in range(2):
                        qs.append(seq(P_c[b][h * 256:(h + 1) * 256], P_l[b][h * 256:(h + 1) * 256], d))
                        ks.append(seq(P_c[b][1024 + h * 256:1024 + (h + 1) * 256],
                                      P_l[b][1024 + h * 256:1024 + (h + 1) * 256], d))
                        vs.append(seq(P_c[b][2048 + h * 512:2048 + (h + 1) * 512],
                                      P_l[b][2048 + h * 512:2048 + (h + 1) * 512], d).T)
                        gs.append(seq(G_c[b][d * 1024 + h * 256:d * 1024 + (h + 1) * 256],
                                      G_l[b][d * 1024 + h * 256:d * 1024 + (h + 1) * 256], d).T)
                ims.append({"qT": np.ascontiguousarray(np.stack(qs)), "kT": np.ascontiguousarray(np.stack(ks)),
                            "v": np.ascontiguousarray(np.stack(vs)), "g": np.ascontiguousarray(np.stack(gs)),
                            "tri": tric, "mask": maskc, "ident": identc})
            r = _run(nc_sc, ims)
            O_l = [[np.zeros((D, SEQ), np.float32) for _ in range(2)] for _ in range(BATCH)]
            O_c = [[np.zeros((D, CTX), np.float32) for _ in range(2)] for _ in range(BATCH)]
            for k in CORES:
                b, hf = k // 2, k % 2
                o = r[k]["o"]
                for hl in range(2):
                    h = 2 * hf + hl
                    for d in range(2):
                        ou = o[hl * 2 + d]
                        oc, ol = ou[:CTX], ou[CTX:]
                        if d == 1:
                            oc, ol = oc[::-1], ol[::-1]
                        O_l[b][d][h * 512:(h + 1) * 512] = ol.T
                        O_c[b][d][h * 512:(h + 1) * 512] = oc.T
            for k in CORES:
                b, hf = k // 2, k % 2
                def tok(al, ac):
                    if last:
                        return np.ascontiguousarray(al[:, hf * HALF:(hf + 1) * HALF])
                    return _cat([al[:, hf * HALF:(hf + 1) * HALF], ac[:, hf * CH:(hf + 1) * CH]])
                rs = pT[k][4096:6144]
                tail_extra[k] = {"ofT": tok(O_l[b][0], O_c[b][0]), "obT": tok(O_l[b][1], O_c[b][1]),
                                 "rsT": np.ascontiguousarray(rs[:, :HALF] if last else rs),
                                 "gh": _fm(gla_g_head[j]), "wo": gla_w_out[j]}
            bias_vec = zeros_d
        elif kind == 1:
            seqs = [("l", SEQ, HALF), ("c", CTX, CTX)]
            nc_fn = _prog("fnet", lambda: build_fnet_program(seqs))
            m_ = np.arange(GW)
            ang = 2 * np.pi * np.outer(m_, m_) / GW
            cwf = np.concatenate([np.cos(ang), np.sin(ang)], axis=1)
            cwc = np.ascontiguousarray(cwf.reshape(2, 128, 512).transpose(1, 0, 2)).reshape(128, 1024) \
                .astype(ml_dtypes.bfloat16)

            def dft(L, k0, nk):
                n_ = np.arange(L, dtype=np.int64)[:, None]
                k_ = (np.arange(nk, dtype=np.int64) + k0)[None, :]
                a = 2 * np.pi * ((n_ * k_) % L).astype(np.float64) / L
                nrm = 1.0 / np.sqrt(L * GW)
                return (np.cos(a) * nrm).astype(ml_dtypes.bfloat16), (-np.sin(a) * nrm).astype(ml_dtypes.bfloat16)

            dl = [dft(SEQ, hf * HALF, HALF) for hf in range(2)]
            dc = dft(CTX, 0, CTX)
            H_l = [_cat([hT[2 * b][:, :HALF], hT[2 * b + 1][:, :HALF]]) for b in range(BATCH)]
            ims = []
            for k in CORES:
                b, hf = k // 2, k % 2
                ims.append({"cw": cwc, "hT_l": H_l[b], "cl_l": dl[hf][0], "sl_l": dl[hf][1],
                            "hT_c": H_c[b], "cl_c": dc[0], "sl_c": dc[1]})
            r = _run(nc_fn, ims)
            for k in CORES:
                hf = k % 2
                tail_extra[k] = {"mT": _cat([r[k]["fT_l"], r[k]["fT_c"][:, hf * CH:(hf + 1) * CH]]),
                                 "wo": fnet_w_out[j]}
            bias_vec = fnet_b_out[j]
        else:
            tiles_cv = [(t0, w, 64) for t0, w, _ in TILES_L] + [(HALF, CTX, CTX)]
            nc_cv = _prog("conv", lambda: build_conv_program(HALF + CTX, tiles_cv))
            vecs = np.ascontiguousarray(np.concatenate(
                [_fm(cm_b_pw1[j][:D]), _fm(cm_b_pw1[j][D:]), _fm(cm_b_dw[j]), _fm(cm_ln_g[j]), _fm(cm_ln_b[j])], axis=1))
            wk = np.ascontiguousarray(cm_w_dw[j].T.reshape(KC, 128, CONV_W).transpose(1, 0, 2)).reshape(128, KC * CONV_W)
            ims = []
            for k in CORES:
                b = k // 2
                ims.append({"hT": _cat([hT[k][:, :HALF], H_c[b]]), "w1": cm_w_pw1[j], "vecs": vecs, "wdw": wk})
            r = _run(nc_cv, ims)
            for k in CORES:
                hf = k % 2
                m = r[k]["mT"]
                tail_extra[k] = {"mT": _cat([m[:, :HALF], m[:, HALF + hf * CH:HALF + (hf + 1) * CH]]),
                                 "wo": cm_w_pw2[j]}
            bias_vec = cm_b_pw2[j]

        gla = kind == 0
        if last:
            nc_tail = _prog(("tail_last", gla), lambda: build_ffn_program(HALF, TILES_L, 1, 1, True, False, True, gla_in=gla))
        else:
            nc_tail = _prog(("tail", gla), lambda: build_ffn_program(TL, TILES_LC, 2, 1, True, False, False, gla_in=gla))
        ims = []
        for k in CORES:
            rws = rows(k)[:1] if last else rows(k)
            sets = []
            for rw in rws:
                s = [norm_g[i, 2], mod[i, rw, 6], mod[i, rw, 7], mod[i, rw, 8], mod[i, rw, 5], bias_vec]
                if last:
                    s.append(final_g)
                sets.append(s)
            im = {"xT": np.ascontiguousarray(xT[k][:, :HALF]) if last else xT[k], "mods": _mods(sets),
                  "wg0": ffn_w_gate[i, 1], "wu0": ffn_w_up[i, 1], "wd0": ffn_w_down[i, 1]}
            im.update(tail_extra[k])
            ims.append(im)
        r = _run(nc_tail, ims)
        xT = [r[k]["oT"] for k in CORES]

    for k in CORES:
        b, hf = k // 2, k % 2
        out[b, hf * HALF:(hf + 1) * HALF] = xT[k].T
    return out
```

```python
import numpy as np
from contextlib import ExitStack
import concourse.bass as bass
import concourse.mybir as mybir
from concourse.bass_utils import run_bass_kernel_spmd

F32 = mybir.dt.float32
BF16 = mybir.dt.bfloat16
AF = mybir.ActivationFunctionType
ALU = mybir.AluOpType

D = 2048
KC = D // 128
DFF = 5632
FC = DFF // 128
NMOD = 9
EPS = 1e-6
NCORES = 8


class Buf:
    __slots__ = ("name", "w", "r")

    def __init__(self, name=""):
        self.name = name
        self.w = None
        self.r = {}


class Op:
    __slots__ = ("eng", "fn", "deps", "dma", "ref", "sem", "val", "pos", "cc")

    def __init__(self, eng, fn, dma):
        self.eng = eng
        self.fn = fn
        self.dma = dma
        self.cc = False
        self.deps = []
        self.ref = False
        self.sem = None
        self.val = 0
        self.pos = 0


ENGS = ("pe", "act", "dve", "pool", "sp")
DMA_SLOTS = 8


class Prog:
    def __init__(self, nc, es):
        self.nc = nc
        self.es = es
        self.streams = {e: [] for e in ENGS}
        self.nbuf = 0
        self.arena = None

    def sb(self, name, shape, dt):
        if self.arena is not None:
            return self.arena.alloc(shape, dt)
        return self.es.enter_context(self.nc.sbuf_tensor(name, list(shape), dt))

    def barrier(self):
        deps = []
        for e in ENGS:
            st = self.streams[e]
            last_c = None
            dmas = []
            for o in reversed(st):
                if o.fn is None:
                    continue
                if o.dma or o.cc:
                    if len(dmas) < DMA_SLOTS + 2:
                        dmas.append(o)
                elif last_c is None:
                    last_c = o
                if last_c is not None and len(dmas) >= DMA_SLOTS + 2:
                    break
            if last_c is not None:
                deps.append(last_c)
            deps.extend(dmas)
        for e in ENGS:
            self.op(e, None, extra=list(deps))
        if self.arena is not None:
            self.arena.reset()

    def cc_op(self, fn, extra=()):
        o = self.op("pool", fn, extra=extra)
        o.cc = True
        return o

    def ps(self, name, shape, dt=F32):
        return self.es.enter_context(self.nc.psum_tensor(name, list(shape), dt))

    def buf(self, name=""):
        self.nbuf += 1
        return Buf(name or f"b{self.nbuf}")

    def op(self, eng, fn, reads=(), writes=(), dma=False, extra=()):
        o = Op(eng, fn, dma)
        deps = []
        for b in reads:
            if b.w is not None:
                deps.append(b.w)
        for b in writes:
            if b.w is not None:
                deps.append(b.w)
            for k, v in b.r.items():
                if k is None:
                    deps.extend(v)
                else:
                    deps.append(v)
        deps.extend(extra)
        seen = set()
        for d in deps:
            if d is o or id(d) in seen:
                continue
            if d.eng == "pe" and eng == "pe" and not d.dma and not dma:
                continue
            seen.add(id(d))
            o.deps.append(d)
        for b in reads:
            if dma:
                b.r.setdefault(None, []).append(o)
            else:
                b.r[eng] = o
        for b in writes:
            b.w = o
            b.r = {}
        o.pos = len(self.streams[eng])
        self.streams[eng].append(o)
        return o

    def join(self, eng, ops):
        return self.op(eng, None, extra=list(ops))

    def emit(self):
        nc = self.nc
        es = self.es
        for e in ENGS:
            for o in self.streams[e]:
                for d in o.deps:
                    d.ref = True
        csem = {e: es.enter_context(nc.semaphore(f"c_{e}")) for e in ENGS}
        dsem = {e: [es.enter_context(nc.semaphore(f"d_{e}{i}")) for i in range(DMA_SLOTS)]
                for e in ("act", "pool", "sp")}
        for e in ENGS:
            cc = 0
            dcount = 0
            duse = [0] * DMA_SLOTS
            hist = []
            for o in self.streams[e]:
                if o.fn is None:
                    continue
                if o.cc:
                    o.sem = es.enter_context(nc.semaphore(f"cc_{e}{o.pos}"))
                    o.val = 1
                    o.ref = True
                elif o.dma:
                    slot = dcount % DMA_SLOTS
                    duse[slot] += 1
                    o.sem = dsem[e][slot]
                    o.val = 16 * duse[slot]
                    if dcount >= DMA_SLOTS:
                        o.deps.append(hist[dcount - DMA_SLOTS])
                    hist.append(o)
                    dcount += 1
                    o.ref = True
                elif o.ref:
                    cc += 1
                    o.sem = csem[e]
                    o.val = cc
        block = es.enter_context(nc.Block())

        def run(e, eng):
            seen = {}
            for o in self.streams[e]:
                need = {}
                for d in o.deps:
                    k = id(d.sem)
                    if seen.get(k, 0) >= d.val:
                        continue
                    if k not in need or need[k][1] < d.val:
                        need[k] = (d.sem, d.val)
                for k, (s, v) in need.items():
                    eng.wait_ge(s, v)
                    seen[k] = v
                if o.fn is None:
                    continue
                ins = o.fn(eng)
                if o.sem is not None:
                    ins.then_inc(o.sem, 16 if o.dma else 1)

        @block.tensor
        def _(eng):
            run("pe", eng)

        @block.scalar
        def _(eng):
            run("act", eng)

        @block.vector
        def _(eng):
            run("dve", eng)

        @block.gpsimd
        def _(eng):
            run("pool", eng)

        @block.sync
        def _(eng):
            run("sp", eng)


class Arena:
    def __init__(self, P, nbytes):
        self.t = P.es.enter_context(P.nc.sbuf_tensor("arena", [128, nbytes // 4], F32))
        self.n = nbytes // 4
        self.off = 0

    def reset(self):
        self.off = 0

    def alloc(self, shape, dt):
        shape = list(shape)
        p = shape[0]
        nel = 1
        for d in shape[1:]:
            nel *= d
        esz = 4 if dt == F32 else 2
        words = (nel * esz + 3) // 4
        words = (words + 7) // 8 * 8
        assert self.off + words <= self.n, f"arena overflow {self.off}+{words}>{self.n}"
        ap = self.t[0:p, self.off:self.off + words]
        self.off += words
        if dt != F32:
            ap = ap.bitcast(dt)
        ap = ap[:, 0:nel]
        if len(shape) == 3:
            ap = ap.rearrange("p (a b) -> p a b", b=shape[2])
        elif len(shape) == 4:
            ap = ap.rearrange("p (a b c) -> p a b c", b=shape[2], c=shape[3])
        return ap


def chunked(ap2d):
    return ap2d.rearrange("(c p) t -> p c t", p=128)


class FFNCtx:
    def __init__(self, P, W):
        self.P = P
        nc = P.nc
        self.W = W
        self.ones = P.sb("ones", [128, 128], F32)
        self.ones_b = P.buf("ones")
        P.op("pool", lambda e: e.memset(self.ones[:], 1.0), writes=[self.ones_b])
        self.banks = [P.ps(f"bank{i}", [128, 512]) for i in range(8)]
        self.bank_b = [P.buf(f"bank{i}") for i in range(8)]
        self.sq = [P.sb(f"sq{i}", [128, W], F32) for i in range(2)]
        self.sq_b = [P.buf(f"sq{i}") for i in range(2)]
        self.rstd = P.sb("rstd", [128, W], F32)
        self.rstd_b = P.buf("rstd")
        self.tmp = [P.sb(f"tmp{i}", [128, W], F32) for i in range(2)]
        self.tmp_b = [P.buf(f"tmp{i}") for i in range(2)]
        self.n_sq = 0
        self.n_tmp = 0


def emit_rstd(C, x, xb, w, bank):
    P = C.P
    for c in range(KC):
        i = C.n_sq % 2
        C.n_sq += 1
        P.op("act", lambda e, c=c, i=i: e.activation(out=C.sq[i][:, :w], in_=x[:, c, :w], func=AF.Square),
             reads=[xb[c]], writes=[C.sq_b[i]])
        P.op("pe", lambda e, c=c, i=i: e.matmul(C.banks[bank][:, :w], C.ones[:], C.sq[i][:, :w],
                                               start=(c == 0), stop=(c == KC - 1)),
             reads=[C.sq_b[i], C.ones_b], writes=[C.bank_b[bank]])
    P.op("act", lambda e: e.activation(out=C.rstd[:, :w], in_=C.banks[bank][:, :w], func=AF.Sqrt,
                                       bias=C.epsb[:, 0:1], scale=1.0 / D),
         reads=[C.bank_b[bank], C.eps_bb], writes=[C.rstd_b])
    P.op("dve", lambda e: e.reciprocal(out=C.rstd[:, :w], in_=C.rstd[:, :w]),
         reads=[C.rstd_b], writes=[C.rstd_b])


def emit_prenorm(C, x, xb, w, gs, shift, msb, out_fn, out_bufs):
    P = C.P
    for c in range(KC):
        i = C.n_tmp % 2
        C.n_tmp += 1
        P.op("dve", lambda e, c=c, i=i: e.scalar_tensor_tensor(
            out=C.tmp[i][:, :w], in0=x[:, c, :w], scalar=gs[:, c:c + 1], in1=C.rstd[:, :w],
            op0=ALU.mult, op1=ALU.mult),
            reads=[xb[c], C.rstd_b, msb], writes=[C.tmp_b[i]])
        P.op("act", lambda e, c=c, i=i: e.activation(out=out_fn(c), in_=C.tmp[i][:, :w], func=AF.Identity,
                                                     bias=shift[:, c:c + 1], scale=1.0),
             reads=[C.tmp_b[i], msb], writes=[out_bufs[c]])


def build_ffn_program(T, tiles, n_sets, n_ffn, proj_in, emit_h, final_norm, gla_in=False):
    W = max(w for _, w, _ in tiles)
    nc = bass.Bass("TRN2", target_bir_lowering=False)
    NV = 4 * n_ffn + (2 if proj_in else 0) + (3 if emit_h else 0) + (1 if final_norm else 0)
    xT = nc.dram_tensor("xT", [D, T], F32, kind="ExternalInput").ap()
    mods = nc.dram_tensor("mods", [128, n_sets * NV * KC], F32, kind="ExternalInput").ap()
    wg = [nc.dram_tensor(f"wg{j}", [D, DFF], F32, kind="ExternalInput").ap() for j in range(n_ffn)]
    wu = [nc.dram_tensor(f"wu{j}", [D, DFF], F32, kind="ExternalInput").ap() for j in range(n_ffn)]
    wd = [nc.dram_tensor(f"wd{j}", [DFF, D], F32, kind="ExternalInput").ap() for j in range(n_ffn)]
    if proj_in:
        if gla_in:
            ofT = nc.dram_tensor("ofT", [D, T], F32, kind="ExternalInput").ap()
            obT = nc.dram_tensor("obT", [D, T], F32, kind="ExternalInput").ap()
            rsT = nc.dram_tensor("rsT", [D, T], F32, kind="ExternalInput").ap()
            gh_d = nc.dram_tensor("gh", [128, 4], F32, kind="ExternalInput").ap()
        else:
            mT = nc.dram_tensor("mT", [D, T], F32, kind="ExternalInput").ap()
        wo = nc.dram_tensor("wo", [D, D], F32, kind="ExternalInput").ap()
    oT = nc.dram_tensor("oT", [D, T], F32, kind="ExternalOutput").ap()
    if emit_h:
        hT = nc.dram_tensor("hT", [D, T], F32, kind="ExternalOutput").ap()

    with ExitStack() as es:
        P = Prog(nc, es)
        C = FFNCtx(P, W)
        C.epsb = P.sb("epsb", [128, 1], F32)
        C.eps_bb = P.buf("eps")
        P.op("pool", lambda e: e.memset(C.epsb[:], EPS), writes=[C.eps_bb])
        mv = P.sb("mv", [128, n_sets * NV * KC], F32)
        mvb = P.buf("mv")
        P.op("sp", lambda e: e.dma_start(out=mv[:], in_=mods), writes=[mvb], dma=True)

        def vec(s, k):
            o = (s * NV + k) * KC
            return mv[:, o:o + KC]

        slot = 0
        ffn_slots = []
        for j in range(n_ffn):
            ffn_slots.append(slot)
            slot += 4
        proj_slot = slot if proj_in else None
        slot += 2 if proj_in else 0
        h_slot = slot if emit_h else None
        slot += 3 if emit_h else 0
        fin_slot = slot if final_norm else None
        for s in range(n_sets):
            norm_slots = list(ffn_slots) + ([h_slot] if emit_h else [])
            for k in norm_slots:
                P.op("dve", lambda e, s=s, k=k: e.scalar_tensor_tensor(
                    out=vec(s, k), in0=vec(s, k + 2), scalar=1.0, in1=vec(s, k),
                    op0=ALU.add, op1=ALU.mult), reads=[mvb], writes=[mvb])
            for k in ffn_slots:
                P.op("dve", lambda e, s=s, k=k: e.tensor_scalar(
                    out=vec(s, k + 3), in0=vec(s, k + 3), scalar1=0.5, scalar2=None, op0=ALU.mult),
                    reads=[mvb], writes=[mvb])

        x = P.sb("x", [128, KC, W], F32)
        xb = [P.buf(f"x{c}") for c in range(KC)]
        h = P.sb("h", [128, KC, W], BF16)
        hb = [P.buf(f"h{c}") for c in range(KC)]
        a = P.sb("a", [128, FC, W], BF16)
        ab = [P.buf(f"a{c}") for c in range(FC)]
        sg = [P.sb(f"sg{i}", [128, W], F32) for i in range(2)]
        sgb = [P.buf(f"sg{i}") for i in range(2)]
        FB = 2
        NWB = 2
        wgt = [P.sb(f"wgt{i}", [128, KC, FB * 128], BF16) for i in range(NWB)]
        wgb = [P.buf(f"wgt{i}") for i in range(NWB)]
        wut = [P.sb(f"wut{i}", [128, KC, FB * 128], BF16) for i in range(NWB)]
        wub = [P.buf(f"wut{i}") for i in range(NWB)]
        DFB = 4
        NDB = 3
        wdt = [P.sb(f"wdt{i}", [128, DFB, 512], BF16) for i in range(NDB)]
        wdb = [P.buf(f"wdt{i}") for i in range(NDB)]
        if emit_h or final_norm:
            ho = [P.sb(f"ho{i}", [128, W], F32) for i in range(2)]
            hob = [P.buf(f"ho{i}") for i in range(2)]
        if proj_in:
            m32 = [P.sb(f"m32_{i}", [128, W], F32) for i in range(2)]
            m32b = [P.buf(f"m32_{i}") for i in range(2)]
            wot = [P.sb(f"wot{i}", [128, KC, 128], BF16) for i in range(2)]
            wotb = [P.buf(f"wot{i}") for i in range(2)]
            if gla_in:
                gh = P.sb("gh_sb", [128, 4], F32)
                ghb = P.buf("gh")
                P.op("sp", lambda e: e.dma_start(out=gh[:], in_=gh_d), writes=[ghb], dma=True)
                osum = P.sb("osum", [128, 4, W], F32)
                osumb = [P.buf(f"osum{i}") for i in range(4)]
                ob32 = [P.sb(f"ob32_{i}", [128, W], F32) for i in range(2)]
                ob32b = [P.buf(f"ob32_{i}") for i in range(2)]
                eps5 = P.sb("eps5", [128, 1], F32)
                eps5b = P.buf("eps5")
                P.op("pool", lambda e: e.memset(eps5[:], EPS), writes=[eps5b])
        cnt = {"w": 0, "d": 0, "sg": 0, "ho": 0, "m": 0, "wo": 0}
        out_dmas = []

        for (t0, w, s) in tiles:
            for half in range(2):
                cs = slice(half * 8, half * 8 + 8)
                P.op("sp", lambda e, cs=cs, t0=t0, w=w: e.dma_start(
                    out=x[:, cs, :w], in_=chunked(xT)[:, cs, t0:t0 + w]),
                    writes=xb[cs], dma=True)
            if proj_in:
                if gla_in:
                    for hd in range(4):
                        for cc in range(4):
                            c = 4 * hd + cc
                            i = cnt["m"] % 2
                            cnt["m"] += 1
                            P.op("sp", lambda e, c=c, cc=cc, t0=t0, w=w: e.dma_start(
                                out=osum[:, cc, :w], in_=ofT[c * 128:(c + 1) * 128, t0:t0 + w]),
                                writes=[osumb[cc]], dma=True)
                            P.op("sp", lambda e, c=c, i=i, t0=t0, w=w: e.dma_start(
                                out=ob32[i][:, :w], in_=obT[c * 128:(c + 1) * 128, t0:t0 + w]),
                                writes=[ob32b[i]], dma=True)
                            P.op("dve", lambda e, cc=cc, i=i, w=w: e.tensor_tensor(
                                out=osum[:, cc, :w], in0=osum[:, cc, :w], in1=ob32[i][:, :w], op=ALU.add),
                                reads=[osumb[cc], ob32b[i]], writes=[osumb[cc]])
                            si = C.n_sq % 2
                            C.n_sq += 1
                            P.op("act", lambda e, cc=cc, si=si, w=w: e.activation(
                                out=C.sq[si][:, :w], in_=osum[:, cc, :w], func=AF.Square),
                                reads=[osumb[cc]], writes=[C.sq_b[si]])
                            P.op("pe", lambda e, cc=cc, si=si, w=w: e.matmul(
                                C.banks[0][:, :w], C.ones[:], C.sq[si][:, :w], start=(cc == 0), stop=(cc == 3)),
                                reads=[C.sq_b[si], C.ones_b], writes=[C.bank_b[0]])
                        P.op("act", lambda e, w=w: e.activation(
                            out=C.rstd[:, :w], in_=C.banks[0][:, :w], func=AF.Sqrt, bias=eps5[:, 0:1],
                            scale=1.0 / 512.0), reads=[C.bank_b[0], eps5b], writes=[C.rstd_b])
                        P.op("dve", lambda e, w=w: e.reciprocal(out=C.rstd[:, :w], in_=C.rstd[:, :w]),
                             reads=[C.rstd_b], writes=[C.rstd_b])
                        for cc in range(4):
                            c = 4 * hd + cc
                            i = cnt["m"] % 2
                            cnt["m"] += 1
                            P.op("sp", lambda e, c=c, i=i, t0=t0, w=w: e.dma_start(
                                out=m32[i][:, :w], in_=rsT[c * 128:(c + 1) * 128, t0:t0 + w]),
                                writes=[m32b[i]], dma=True)
                            P.op("dve", lambda e, cc=cc, w=w: e.scalar_tensor_tensor(
                                out=osum[:, cc, :w], in0=osum[:, cc, :w], scalar=gh[:, cc:cc + 1],
                                in1=C.rstd[:, :w], op0=ALU.mult, op1=ALU.mult),
                                reads=[osumb[cc], ghb, C.rstd_b], writes=[osumb[cc]])
                            P.op("dve", lambda e, c=c, cc=cc, i=i, w=w: e.tensor_tensor(
                                out=h[:, c, :w], in0=osum[:, cc, :w], in1=m32[i][:, :w], op=ALU.mult),
                                reads=[osumb[cc], m32b[i]], writes=[hb[c]])
                else:
                    for c in range(KC):
                        i = cnt["m"] % 2
                        cnt["m"] += 1
                        P.op("sp", lambda e, c=c, i=i, t0=t0, w=w: e.dma_start(
                            out=m32[i][:, :w], in_=mT[c * 128:(c + 1) * 128, t0:t0 + w]),
                            writes=[m32b[i]], dma=True)
                        P.op("act", lambda e, c=c, i=i, w=w: e.activation(out=h[:, c, :w], in_=m32[i][:, :w],
                                                                         func=AF.Identity),
                             reads=[m32b[i]], writes=[hb[c]])
                for dc in range(KC):
                    i = cnt["wo"] % 2
                    cnt["wo"] += 1
                    P.op("pool", lambda e, dc=dc, i=i: e.dma_start(
                        out=wot[i][:], in_=chunked(wo)[:, :, dc * 128:(dc + 1) * 128]),
                        writes=[wotb[i]], dma=True)
                    bank = 4 + (dc % 4)
                    for kc in range(KC):
                        P.op("pe", lambda e, dc=dc, kc=kc, i=i, bank=bank, w=w: e.matmul(
                            C.banks[bank][:, :w], wot[i][:, kc, :], h[:, kc, :w],
                            start=(kc == 0), stop=(kc == KC - 1)),
                            reads=[wotb[i], hb[kc]], writes=[C.bank_b[bank]])
                    ti = C.n_tmp % 2
                    C.n_tmp += 1
                    P.op("act", lambda e, dc=dc, ti=ti, bank=bank, w=w, s=s: e.activation(
                        out=C.tmp[ti][:, :w], in_=C.banks[bank][:, :w], func=AF.Identity,
                        bias=vec(s, proj_slot + 1)[:, dc:dc + 1], scale=1.0),
                        reads=[C.bank_b[bank], mvb], writes=[C.tmp_b[ti]])
                    P.op("dve", lambda e, dc=dc, ti=ti, w=w, s=s: e.scalar_tensor_tensor(
                        out=x[:, dc, :w], in0=C.tmp[ti][:, :w], scalar=vec(s, proj_slot)[:, dc:dc + 1],
                        in1=x[:, dc, :w], op0=ALU.mult, op1=ALU.add),
                        reads=[C.tmp_b[ti], mvb, xb[dc]], writes=[xb[dc]])

            for j in range(n_ffn):
                k0 = ffn_slots[j]
                emit_rstd(C, x, xb, w, 0)
                emit_prenorm(C, x, xb, w, vec(s, k0), vec(s, k0 + 1), mvb,
                             lambda c, w=w: h[:, c, :w], hb)
                for fb in range(FC // FB):
                    i = cnt["w"] % NWB
                    cnt["w"] += 1
                    fsl = slice(fb * FB * 128, (fb + 1) * FB * 128)
                    P.op("pool", lambda e, i=i, fsl=fsl, j=j: e.dma_start(
                        out=wgt[i][:], in_=chunked(wg[j])[:, :, fsl]), writes=[wgb[i]], dma=True)
                    P.op("pool", lambda e, i=i, fsl=fsl, j=j: e.dma_start(
                        out=wut[i][:], in_=chunked(wu[j])[:, :, fsl]), writes=[wub[i]], dma=True)
                    for f in range(FB):
                        fc = fb * FB + f
                        par = fc % 2
                        gb, ub = 2 * par, 2 * par + 1
                        for kc in range(KC):
                            P.op("pe", lambda e, i=i, f=f, kc=kc, gb=gb, w=w: e.matmul(
                                C.banks[gb][:, :w], wgt[i][:, kc, f * 128:(f + 1) * 128], h[:, kc, :w],
                                start=(kc == 0), stop=(kc == KC - 1)),
                                reads=[wgb[i], hb[kc]], writes=[C.bank_b[gb]])
                        for kc in range(KC):
                            P.op("pe", lambda e, i=i, f=f, kc=kc, ub=ub, w=w: e.matmul(
                                C.banks[ub][:, :w], wut[i][:, kc, f * 128:(f + 1) * 128], h[:, kc, :w],
                                start=(kc == 0), stop=(kc == KC - 1)),
                                reads=[wub[i], hb[kc]], writes=[C.bank_b[ub]])
                        si = cnt["sg"] % 2
                        cnt["sg"] += 1
                        P.op("act", lambda e, si=si, gb=gb, w=w: e.activation(
                            out=sg[si][:, :w], in_=C.banks[gb][:, :w], func=AF.Silu),
                            reads=[C.bank_b[gb]], writes=[sgb[si]])
                        P.op("dve", lambda e, si=si, ub=ub, fc=fc, w=w: e.tensor_tensor(
                            out=a[:, fc, :w], in0=sg[si][:, :w], in1=C.banks[ub][:, :w], op=ALU.mult),
                            reads=[sgb[si], C.bank_b[ub]], writes=[ab[fc]])
                for dg in range(4):
                    base = 4 if dg % 2 == 0 else 0
                    for fb in range(FC // DFB):
                        i = cnt["d"] % NDB
                        cnt["d"] += 1
                        P.op("pool", lambda e, i=i, fb=fb, dg=dg, j=j: e.dma_start(
                            out=wdt[i][:],
                            in_=wd[j][fb * DFB * 128:(fb + 1) * DFB * 128, dg * 512:(dg + 1) * 512]
                            .rearrange("(c p) n -> p c n", p=128)), writes=[wdb[i]], dma=True)
                        for f in range(DFB):
                            fc = fb * DFB + f
                            for dc in range(4):
                                P.op("pe", lambda e, i=i, f=f, fc=fc, dc=dc, base=base, w=w: e.matmul(
                                    C.banks[base + dc][:, :w], wdt[i][:, f, dc * 128:(dc + 1) * 128],
                                    a[:, fc, :w], start=(fc == 0), stop=(fc == FC - 1)),
                                    reads=[wdb[i], ab[fc]], writes=[C.bank_b[base + dc]])
                    for dc in range(4):
                        c = dg * 4 + dc
                        P.op("dve", lambda e, c=c, dc=dc, base=base, w=w, s=s, k0=k0: e.scalar_tensor_tensor(
                            out=x[:, c, :w], in0=C.banks[base + dc][:, :w],
                            scalar=vec(s, k0 + 3)[:, c:c + 1], in1=x[:, c, :w],
                            op0=ALU.mult, op1=ALU.add),
                            reads=[C.bank_b[base + dc], mvb, xb[c]], writes=[xb[c]])

            if final_norm:
                emit_rstd(C, x, xb, w, 0)
                for c in range(KC):
                    i = cnt["ho"] % 2
                    cnt["ho"] += 1
                    P.op("dve", lambda e, c=c, i=i, w=w, s=s: e.scalar_tensor_tensor(
                        out=ho[i][:, :w], in0=x[:, c, :w], scalar=vec(s, fin_slot)[:, c:c + 1],
                        in1=C.rstd[:, :w], op0=ALU.mult, op1=ALU.mult),
                        reads=[xb[c], C.rstd_b, mvb], writes=[hob[i]])
                    out_dmas.append(P.op("sp", lambda e, c=c, i=i, t0=t0, w=w: e.dma_start(
                        out=oT[c * 128:(c + 1) * 128, t0:t0 + w], in_=ho[i][:, :w]),
                        reads=[hob[i]], dma=True))
            else:
                for half in range(2):
                    cs = slice(half * 8, half * 8 + 8)
                    out_dmas.append(P.op("sp", lambda e, cs=cs, t0=t0, w=w: e.dma_start(
                        out=chunked(oT)[:, cs, t0:t0 + w], in_=x[:, cs, :w]),
                        reads=xb[cs], dma=True))
            if emit_h:
                emit_rstd(C, x, xb, w, 0)
                for c in range(KC):
                    i = cnt["ho"] % 2
                    cnt["ho"] += 1
                    ti = C.n_tmp % 2
                    C.n_tmp += 1
                    P.op("dve", lambda e, c=c, ti=ti, w=w, s=s: e.scalar_tensor_tensor(
                        out=C.tmp[ti][:, :w], in0=x[:, c, :w], scalar=vec(s, h_slot)[:, c:c + 1],
                        in1=C.rstd[:, :w], op0=ALU.mult, op1=ALU.mult),
                        reads=[xb[c], C.rstd_b, mvb], writes=[C.tmp_b[ti]])
                    P.op("act", lambda e, c=c, i=i, ti=ti, w=w, s=s: e.activation(
                        out=ho[i][:, :w], in_=C.tmp[ti][:, :w], func=AF.Identity,
                        bias=vec(s, h_slot + 1)[:, c:c + 1], scale=1.0),
                        reads=[C.tmp_b[ti], mvb], writes=[hob[i]])
                    out_dmas.append(P.op("sp", lambda e, c=c, i=i, t0=t0, w=w: e.dma_start(
                        out=hT[c * 128:(c + 1) * 128, t0:t0 + w], in_=ho[i][:, :w]),
                        reads=[hob[i]], dma=True))
        P.join("sp", out_dmas)
        P.emit()
    return nc


MODC = NMOD * D // NCORES
NROW = 5


def build_mod_program(depth):
    nc = bass.Bass("TRN2", target_bir_lowering=False)
    scT = nc.dram_tensor("scT", [128, KC * NROW], F32, kind="ExternalInput").ap()
    wm = nc.dram_tensor("wm", [depth, D, MODC], F32, kind="ExternalInput").ap()
    bm = nc.dram_tensor("bm", [depth, MODC], F32, kind="ExternalInput").ap()
    out = nc.dram_tensor("out", [depth, NROW, MODC], F32, kind="ExternalOutput").ap()
    blocks = [(0, 512), (512, 512), (1024, 512), (1536, 512), (2048, 256)]
    with ExitStack() as es:
        P = Prog(nc, es)
        sc = P.sb("sc", [128, KC * NROW], F32)
        scb = P.buf()
        ones = P.sb("ones1", [1, NROW], F32)
        onesb = P.buf()
        P.op("pool", lambda e: e.memset(ones[:], 1.0), writes=[onesb])
        P.op("sp", lambda e: e.dma_start(out=sc[:], in_=scT), writes=[scb], dma=True)
        P.op("act", lambda e: e.activation(out=sc[:], in_=sc[:], func=AF.Silu), reads=[scb], writes=[scb])
        wt = [P.sb(f"wt{i}", [128, KC, 512], F32) for i in range(2)]
        wtb = [[P.buf(), P.buf()] for i in range(2)]
        bt = [P.sb(f"bt{i}", [1, 512], F32) for i in range(2)]
        btb = [P.buf() for i in range(2)]
        ot = [P.sb(f"ot{i}", [NROW, 512], F32) for i in range(2)]
        otb = [P.buf() for i in range(2)]
        banks = [P.ps(f"bk{i}", [128, 512]) for i in range(2)]
        bkb = [P.buf() for i in range(2)]
        n = 0
        outs = []
        for l in range(depth):
            for (c0, cw) in blocks:
                i = n % 2
                n += 1
                for half in range(2):
                    P.op("sp" if half == 0 else "act", lambda e, i=i, l=l, c0=c0, cw=cw, half=half: e.dma_start(
                        out=wt[i][:, half * 8:half * 8 + 8, :cw],
                        in_=wm[l].rearrange("(c p) n -> p c n", p=128)[:, half * 8:half * 8 + 8, c0:c0 + cw]),
                        writes=[wtb[i][half]], dma=True)
                P.op("sp", lambda e, i=i, l=l, c0=c0, cw=cw: e.dma_start(out=bt[i][:, :cw], in_=bm[l:l + 1, c0:c0 + cw]),
                     writes=[btb[i]], dma=True)
                for kc in range(KC):
                    P.op("pe", lambda e, i=i, kc=kc, cw=cw: e.matmul(
                        banks[i][:NROW, :cw], sc[:, kc * NROW:(kc + 1) * NROW], wt[i][:, kc, :cw],
                        start=(kc == 0), stop=False), reads=[wtb[i][kc // 8], scb], writes=[bkb[i]])
                P.op("pe", lambda e, i=i, cw=cw: e.matmul(banks[i][:NROW, :cw], ones[:], bt[i][:, :cw],
                                                         start=False, stop=True),
                     reads=[btb[i], onesb], writes=[bkb[i]])
                P.op("act", lambda e, i=i, cw=cw: e.activation(out=ot[i][:, :cw], in_=banks[i][:NROW, :cw],
                                                              func=AF.Identity),
                     reads=[bkb[i]], writes=[otb[i]])
                outs.append(P.op("sp", lambda e, i=i, l=l, c0=c0, cw=cw: e.dma_start(
                    out=out[l, :, c0:c0 + cw], in_=ot[i][:, :cw]), reads=[otb[i]], dma=True))
        P.join("sp", outs)
        P.emit()
    return nc


CONV_W = 31


def build_conv_program(T, tiles):
    W = max(w for _, w, _ in tiles)
    nc = bass.Bass("TRN2", target_bir_lowering=False)
    hT = nc.dram_tensor("hT", [D, T], F32, kind="ExternalInput").ap()
    w1 = nc.dram_tensor("w1", [D, 2 * D], F32, kind="ExternalInput").ap()
    vecs = nc.dram_tensor("vecs", [128, 5 * KC], F32, kind="ExternalInput").ap()
    wdw = nc.dram_tensor("wdw", [128, KC * CONV_W], F32, kind="ExternalInput").ap()
    mT = nc.dram_tensor("mT", [D, T], F32, kind="ExternalOutput").ap()
    with ExitStack() as es:
        P = Prog(nc, es)
        C = FFNCtx(P, W)
        C.epsb = P.sb("epsb", [128, 1], F32)
        C.eps_bb = P.buf("eps")
        P.op("pool", lambda e: e.memset(C.epsb[:], EPS), writes=[C.eps_bb])
        vv = P.sb("vv", [128, 5 * KC], F32)
        vvb = P.buf()
        P.op("sp", lambda e: e.dma_start(out=vv[:], in_=vecs), writes=[vvb], dma=True)
        wk = P.sb("wk", [128, KC * CONV_W], F32)
        wkb = P.buf()
        P.op("sp", lambda e: e.dma_start(out=wk[:], in_=wdw), writes=[wkb], dma=True)

        def vec(k):
            return vv[:, k * KC:(k + 1) * KC]

        h = P.sb("h", [128, KC, W], BF16)
        hb = [P.buf() for c in range(KC)]
        y = P.sb("y", [128, KC, W], F32)
        yb = [P.buf() for c in range(KC)]
        u = [P.sb(f"u{i}", [128, W], F32) for i in range(2)]
        ub = [P.buf() for i in range(2)]
        sgm = [P.sb(f"sgm{i}", [128, W], F32) for i in range(2)]
        sgmb = [P.buf() for i in range(2)]
        wa = [P.sb(f"wa{i}", [128, KC, 128], BF16) for i in range(2)]
        wab = [P.buf() for i in range(2)]
        wgx = [P.sb(f"wgx{i}", [128, KC, 128], BF16) for i in range(2)]
        wgxb = [P.buf() for i in range(2)]
        mean = P.sb("mean", [128, W], F32)
        meanb = P.buf()
        var = P.sb("var", [128, W], F32)
        varb = P.buf()
        ho = [P.sb(f"ho{i}", [128, W], F32) for i in range(2)]
        hob = [P.buf() for i in range(2)]
        n = {"w": 0, "u": 0, "ho": 0}
        outs = []
        for (t0, w, L) in tiles:
            ns = w // L
            for half in range(2):
                cs = slice(half * 8, half * 8 + 8)
                P.op("pool", lambda e, cs=cs, t0=t0, w=w: e.dma_start(
                    out=h[:, cs, :w], in_=chunked(hT)[:, cs, t0:t0 + w]), writes=hb[cs], dma=True)
            for mc in range(KC):
                i = n["w"] % 2
                n["w"] += 1
                P.op("pool", lambda e, i=i, mc=mc: e.dma_start(
                    out=wa[i][:], in_=chunked(w1)[:, :, mc * 128:(mc + 1) * 128]), writes=[wab[i]], dma=True)
                P.op("pool", lambda e, i=i, mc=mc: e.dma_start(
                    out=wgx[i][:], in_=chunked(w1)[:, :, D + mc * 128:D + (mc + 1) * 128]),
                    writes=[wgxb[i]], dma=True)
                par = mc % 2
                ba, bg = 2 + 2 * par, 3 + 2 * par
                for kc in range(KC):
                    P.op("pe", lambda e, i=i, kc=kc, ba=ba, w=w: e.matmul(
                        C.banks[ba][:, :w], wa[i][:, kc, :], h[:, kc, :w], start=(kc == 0), stop=(kc == KC - 1)),
                        reads=[wab[i], hb[kc]], writes=[C.bank_b[ba]])
                for kc in range(KC):
                    P.op("pe", lambda e, i=i, kc=kc, bg=bg, w=w: e.matmul(
                        C.banks[bg][:, :w], wgx[i][:, kc, :], h[:, kc, :w], start=(kc == 0), stop=(kc == KC - 1)),
                        reads=[wgxb[i], hb[kc]], writes=[C.bank_b[bg]])
                ui = n["u"] % 2
                n["u"] += 1
                P.op("act", lambda e, ui=ui, bg=bg, mc=mc, w=w: e.activation(
                    out=sgm[ui][:, :w], in_=C.banks[bg][:, :w], func=AF.Sigmoid,
                    bias=vec(1)[:, mc:mc + 1], scale=1.0), reads=[C.bank_b[bg], vvb], writes=[sgmb[ui]])
                P.op("dve", lambda e, ui=ui, ba=ba, mc=mc, w=w: e.scalar_tensor_tensor(
                    out=u[ui][:, :w], in0=C.banks[ba][:, :w], scalar=vec(0)[:, mc:mc + 1], in1=sgm[ui][:, :w],
                    op0=ALU.add, op1=ALU.mult), reads=[C.bank_b[ba], sgmb[ui], vvb], writes=[ub[ui]])
                u3 = u[ui][:, :w].rearrange("p (s l) -> p s l", l=L)
                y3 = y[:, mc, :w].rearrange("p (s l) -> p s l", l=L)
                P.op("dve", lambda e, ui=ui, mc=mc, w=w: e.tensor_scalar(
                    out=y[:, mc, :w], in0=u[ui][:, :w], scalar1=wk[:, mc * CONV_W + 15:mc * CONV_W + 16],
                    scalar2=vec(2)[:, mc:mc + 1], op0=ALU.mult, op1=ALU.add),
                    reads=[ub[ui], wkb, vvb], writes=[yb[mc]])
                for k in range(CONV_W):
                    o = k - 15
                    if o == 0 or abs(o) >= L:
                        continue
                    a0, a1 = max(0, -o), min(L, L - o)
                    P.op("dve", lambda e, u3=u3, y3=y3, mc=mc, k=k, a0=a0, a1=a1, o=o: e.scalar_tensor_tensor(
                        out=y3[:, :, a0:a1], in0=u3[:, :, a0 + o:a1 + o],
                        scalar=wk[:, mc * CONV_W + k:mc * CONV_W + k + 1], in1=y3[:, :, a0:a1],
                        op0=ALU.mult, op1=ALU.add), reads=[ub[ui], wkb, yb[mc]], writes=[yb[mc]])
                P.op("pe", lambda e, mc=mc, w=w: e.matmul(C.banks[0][:, :w], C.ones[:], y[:, mc, :w],
                                                         start=(mc == 0), stop=(mc == KC - 1)),
                     reads=[yb[mc], C.ones_b], writes=[C.bank_b[0]])
                si = C.n_sq % 2
                C.n_sq += 1
                P.op("act", lambda e, mc=mc, si=si, w=w: e.activation(out=C.sq[si][:, :w], in_=y[:, mc, :w],
                                                                     func=AF.Square),
                     reads=[yb[mc]], writes=[C.sq_b[si]])
                P.op("pe", lambda e, mc=mc, si=si, w=w: e.matmul(C.banks[1][:, :w], C.ones[:], C.sq[si][:, :w],
                                                                start=(mc == 0), stop=(mc == KC - 1)),
                     reads=[C.sq_b[si], C.ones_b], writes=[C.bank_b[1]])
            P.op("act", lambda e, w=w: e.activation(out=mean[:, :w], in_=C.banks[0][:, :w], func=AF.Identity,
                                                    scale=1.0 / D), reads=[C.bank_b[0]], writes=[meanb])
            P.op("dve", lambda e, w=w: e.tensor_tensor(out=var[:, :w], in0=mean[:, :w], in1=mean[:, :w],
                                                       op=ALU.mult), reads=[meanb], writes=[varb])
            P.op("dve", lambda e, w=w: e.scalar_tensor_tensor(
                out=var[:, :w], in0=C.banks[1][:, :w], scalar=1.0 / D, in1=var[:, :w],
                op0=ALU.mult, op1=ALU.subtract), reads=[C.bank_b[1], varb], writes=[varb])
            P.op("act", lambda e, w=w: e.activation(out=C.rstd[:, :w], in_=var[:, :w], func=AF.Sqrt,
                                                    bias=C.epsb[:, 0:1], scale=1.0),
                 reads=[varb, C.eps_bb], writes=[C.rstd_b])
            P.op("dve", lambda e, w=w: e.reciprocal(out=C.rstd[:, :w], in_=C.rstd[:, :w]),
                 reads=[C.rstd_b], writes=[C.rstd_b])
            for c in range(KC):
                ti = C.n_tmp % 2
                C.n_tmp += 1
                i = n["ho"] % 2
                n["ho"] += 1
                P.op("dve", lambda e, c=c, ti=ti, w=w: e.tensor_tensor(
                    out=C.tmp[ti][:, :w], in0=y[:, c, :w], in1=mean[:, :w], op=ALU.subtract),
                    reads=[yb[c], meanb], writes=[C.tmp_b[ti]])
                P.op("dve", lambda e, c=c, ti=ti, w=w: e.scalar_tensor_tensor(
                    out=C.tmp[ti][:, :w], in0=C.tmp[ti][:, :w], scalar=vec(3)[:, c:c + 1], in1=C.rstd[:, :w],
                    op0=ALU.mult, op1=ALU.mult), reads=[C.tmp_b[ti], C.rstd_b, vvb], writes=[C.tmp_b[ti]])
                P.op("act", lambda e, c=c, ti=ti, i=i, w=w: e.activation(
                    out=ho[i][:, :w], in_=C.tmp[ti][:, :w], func=AF.Silu, bias=vec(4)[:, c:c + 1], scale=1.0),
                    reads=[C.tmp_b[ti], vvb], writes=[hob[i]])
                outs.append(P.op("sp", lambda e, c=c, i=i, t0=t0, w=w: e.dma_start(
                    out=mT[c * 128:(c + 1) * 128, t0:t0 + w], in_=ho[i][:, :w]), reads=[hob[i]], dma=True))
        P.join("sp", outs)
        P.emit()
    return nc


GW = 256
NG = 8


def build_fnet_program(seqs):
    nc = bass.Bass("TRN2", target_bir_lowering=False)
    cw_d = nc.dram_tensor("cw", [128, 2 * 512], BF16, kind="ExternalInput").ap()
    io = {}
    for (name, L, NK) in seqs:
        io[name] = (
            nc.dram_tensor(f"hT_{name}", [D, L], F32, kind="ExternalInput").ap(),
            nc.dram_tensor(f"cl_{name}", [L, NK], BF16, kind="ExternalInput").ap(),
            nc.dram_tensor(f"sl_{name}", [L, NK], BF16, kind="ExternalInput").ap(),
            nc.dram_tensor(f"fT_{name}", [D, NK], F32, kind="ExternalOutput").ap(),
        )
    LMAX = max(L for _, L, _ in seqs)
    with ExitStack() as es:
        P = Prog(nc, es)
        banks = [P.ps(f"bank{i}", [128, 512]) for i in range(8)]
        bkb = [P.buf() for i in range(8)]
        cw = P.sb("cw_sb", [128, 2, 512], BF16)
        cwb = P.buf()
        P.op("sp", lambda e: e.dma_start(out=cw[:].rearrange("p a b -> p (a b)"), in_=cw_d), writes=[cwb], dma=True)
        hg = [P.sb(f"hg{i}", [128, 2, LMAX], BF16) for i in range(2)]
        hgb = [P.buf() for i in range(2)]
        A = P.sb("A", [128, LMAX // 128, 512], BF16)
        Ab = [P.buf() for i in range(LMAX // 128)]
        NB = 4
        clt = [P.sb(f"clt{i}", [128, 8, 512], BF16) for i in range(NB)]
        cltb = [P.buf() for i in range(NB)]
        slt = [P.sb(f"slt{i}", [128, 8, 512], BF16) for i in range(NB)]
        sltb = [P.buf() for i in range(NB)]
        fo = [P.sb(f"fo{i}", [128, 512], F32) for i in range(4)]
        fob = [P.buf() for i in range(4)]
        n = {"hg": 0, "m": 0, "fo": 0, "a": 0}
        outs = []
        for (name, L, NK) in seqs:
            hT, cl, sl, fT = io[name]
            NCH = L // 128
            for g in range(NG):
                gi = n["hg"] % 2
                n["hg"] += 1
                P.op("pool", lambda e, gi=gi, g=g, L=L, hT=hT: e.dma_start(
                    out=hg[gi][:, :, :L], in_=chunked(hT)[:, 2 * g:2 * g + 2, :]), writes=[hgb[gi]], dma=True)
                for nch in range(NCH):
                    bk = n["a"] % 2
                    n["a"] += 1
                    for kc in range(2):
                        P.op("pe", lambda e, gi=gi, nch=nch, kc=kc, bk=bk: e.matmul(
                            banks[bk][:, :], hg[gi][:, kc, nch * 128:(nch + 1) * 128], cw[:, kc, :],
                            start=(kc == 0), stop=(kc == 1)), reads=[hgb[gi], cwb], writes=[bkb[bk]])
                    P.op("act" if nch % 2 == 0 else "dve",
                         (lambda e, nch=nch, bk=bk: e.activation(out=A[:, nch, :], in_=banks[bk][:, :], func=AF.Identity))
                         if nch % 2 == 0 else
                         (lambda e, nch=nch, bk=bk: e.tensor_copy(out=A[:, nch, :], in_=banks[bk][:, :])),
                         reads=[bkb[bk]], writes=[Ab[nch]])
                kblocks = [(k0, min(512, NK - k0)) for k0 in range(0, NK, 512)]
                for (k0, kw) in kblocks:
                    pb = [2 + 2 * (n["fo"] % 2), 3 + 2 * (n["fo"] % 2)]
                    nsub = (NCH + 7) // 8
                    for sb_ in range(nsub):
                        r0 = sb_ * 8
                        rn = min(8, NCH - r0)
                        mi = n["m"] % NB
                        n["m"] += 1
                        P.op("sp", lambda e, mi=mi, r0=r0, rn=rn, k0=k0, kw=kw, cl=cl: e.dma_start(
                            out=clt[mi][:, :rn, :kw],
                            in_=cl[r0 * 128:(r0 + rn) * 128, k0:k0 + kw].rearrange("(c p) n -> p c n", p=128)),
                            writes=[cltb[mi]], dma=True)
                        P.op("act", lambda e, mi=mi, r0=r0, rn=rn, k0=k0, kw=kw, sl=sl: e.dma_start(
                            out=slt[mi][:, :rn, :kw],
                            in_=sl[r0 * 128:(r0 + rn) * 128, k0:k0 + kw].rearrange("(c p) n -> p c n", p=128)),
                            writes=[sltb[mi]], dma=True)
                        for r in range(rn):
                            nch = r0 + r
                            for mcx in range(2):
                                P.op("pe", lambda e, mi=mi, r=r, nch=nch, mcx=mcx, kw=kw, pb=pb: e.matmul(
                                    banks[pb[mcx]][:, :kw], A[:, nch, mcx * 128:(mcx + 1) * 128], clt[mi][:, r, :kw],
                                    start=(nch == 0), stop=False), reads=[Ab[nch], cltb[mi]], writes=[bkb[pb[mcx]]])
                                P.op("pe", lambda e, mi=mi, r=r, nch=nch, mcx=mcx, kw=kw, pb=pb, NCH=NCH: e.matmul(
                                    banks[pb[mcx]][:, :kw], A[:, nch, 256 + mcx * 128:256 + (mcx + 1) * 128],
                                    slt[mi][:, r, :kw], start=False, stop=(nch == NCH - 1)),
                                    reads=[Ab[nch], sltb[mi]], writes=[bkb[pb[mcx]]])
                    for mcx in range(2):
                        fi = n["fo"] % 2
                        P.op("act", lambda e, fi=fi, mcx=mcx, kw=kw, pb=pb: e.activation(
                            out=fo2(fo, fi, mcx)[:, :kw], in_=banks[pb[mcx]][:, :kw],
                            func=AF.Identity), reads=[bkb[pb[mcx]]], writes=[fob2(fob, fi, mcx)])
                        c = 2 * g + mcx
                        outs.append(P.op("sp", lambda e, fi=fi, mcx=mcx, c=c, k0=k0, kw=kw, fT=fT: e.dma_start(
                            out=fT[c * 128:(c + 1) * 128, k0:k0 + kw], in_=fo2(fo, fi, mcx)[:, :kw]),
                            reads=[fob2(fob, fi, mcx)], dma=True))
                    n["fo"] += 1
        P.join("sp", outs)
        P.emit()
    return nc


def fo2(fo, fi, mcx):
    return fo[(2 * fi + mcx) % len(fo)]


def fob2(fob, fi, mcx):
    return fob[(2 * fi + mcx) % len(fob)]


GLA_HK = 1024
GLA_HV = 2048
GLA_IN = 6176
GLA_R = 16


def build_glaproj_program(T, tiles):
    W = max(w for _, w in tiles)
    nc = bass.Bass("TRN2", target_bir_lowering=False)
    hT = nc.dram_tensor("hT", [D, T], F32, kind="ExternalInput").ap()
    win = nc.dram_tensor("win", [D, GLA_IN], F32, kind="ExternalInput").ap()
    wgu = nc.dram_tensor("wgu", [GLA_R, 2 * GLA_HK], F32, kind="ExternalInput").ap()
    bg = nc.dram_tensor("bg", [128, 16], F32, kind="ExternalInput").ap()
    pT = nc.dram_tensor("pT", [6144, T], F32, kind="ExternalOutput").ap()
    gT = nc.dram_tensor("gT", [2 * GLA_HK, T], F32, kind="ExternalOutput").ap()
    with ExitStack() as es:
        P = Prog(nc, es)
        banks = [P.ps(f"bank{i}", [128, 512]) for i in range(8)]
        bkb = [P.buf() for i in range(8)]
        h = P.sb("h", [128, KC, W], BF16)
        hb = [P.buf() for c in range(KC)]
        wt = [P.sb(f"wt{i}", [128, KC, 128], BF16) for i in range(3)]
        wtb = [P.buf() for i in range(3)]
        wz = P.sb("wz", [128, KC, 32], BF16)
        wzb = P.buf()
        P.op("pool", lambda e: e.dma_start(out=wz[:], in_=chunked(win)[:, :, 6144:6176]), writes=[wzb], dma=True)
        wg = P.sb("wg", [GLA_R, 2 * GLA_HK], F32)
        wgb = P.buf()
        P.op("sp", lambda e: e.dma_start(out=wg[:], in_=wgu), writes=[wgb], dma=True)
        bgs = P.sb("bgs", [128, 16], F32)
        bgb = P.buf()
        P.op("sp", lambda e: e.dma_start(out=bgs[:], in_=bg), writes=[bgb], dma=True)
        z = [P.sb(f"z{i}", [GLA_R, W], F32) for i in range(2)]
        zb = [P.buf() for i in range(2)]
        ot = [P.sb(f"ot{i}", [128, W], F32) for i in range(3)]
        otb = [P.buf() for i in range(3)]
        sg = [P.sb(f"sgx{i}", [128, W], F32) for i in range(2)]
        sgb = [P.buf() for i in range(2)]
        n = {"w": 0, "o": 0, "b": 0, "s": 0}
        outs = []
        for (t0, w) in tiles:
            for half in range(2):
                cs = slice(half * 8, half * 8 + 8)
                P.op("pool", lambda e, cs=cs, t0=t0, w=w: e.dma_start(
                    out=h[:, cs, :w], in_=chunked(hT)[:, cs, t0:t0 + w]), writes=hb[cs], dma=True)
            for mc in range(48):
                i = n["w"] % 3
                n["w"] += 1
                P.op("pool", lambda e, i=i, mc=mc: e.dma_start(
                    out=wt[i][:], in_=chunked(win)[:, :, mc * 128:(mc + 1) * 128]), writes=[wtb[i]], dma=True)
                bk = n["b"] % 4
                n["b"] += 1
                for kc in range(KC):
                    P.op("pe", lambda e, i=i, kc=kc, bk=bk, w=w: e.matmul(
                        banks[bk][:, :w], wt[i][:, kc, :], h[:, kc, :w], start=(kc == 0), stop=(kc == KC - 1)),
                        reads=[wtb[i], hb[kc]], writes=[bkb[bk]])
                oi = n["o"] % 3
                n["o"] += 1
                if mc < 8:
                    fn = lambda e, oi=oi, bk=bk, w=w: e.activation(out=ot[oi][:, :w], in_=banks[bk][:, :w],
                                                                   func=AF.Identity, scale=1.0 / 16.0)
                    eng = "act"
                elif mc < 32:
                    fn = lambda e, oi=oi, bk=bk, w=w: e.tensor_copy(out=ot[oi][:, :w], in_=banks[bk][:, :w])
                    eng = "dve"
                else:
                    fn = lambda e, oi=oi, bk=bk, w=w: e.activation(out=ot[oi][:, :w], in_=banks[bk][:, :w],
                                                                   func=AF.Silu)
                    eng = "act"
                P.op(eng, fn, reads=[bkb[bk]], writes=[otb[oi]])
                outs.append(P.op("sp", lambda e, oi=oi, mc=mc, t0=t0, w=w: e.dma_start(
                    out=pT[mc * 128:(mc + 1) * 128, t0:t0 + w], in_=ot[oi][:, :w]), reads=[otb[oi]], dma=True))
            for d in range(2):
                bk = 4 + d
                for kc in range(KC):
                    P.op("pe", lambda e, d=d, kc=kc, bk=bk, w=w: e.matmul(
                        banks[bk][:GLA_R, :w], wz[:, kc, d * 16:(d + 1) * 16], h[:, kc, :w],
                        start=(kc == 0), stop=(kc == KC - 1)), reads=[wzb, hb[kc]], writes=[bkb[bk]])
                P.op("dve", lambda e, d=d, bk=bk, w=w: e.tensor_copy(out=z[d][:, :w], in_=banks[bk][:GLA_R, :w]),
                     reads=[bkb[bk]], writes=[zb[d]])
            for d in range(2):
                for j in range(8):
                    bk = 6 + (j % 2)
                    P.op("pe", lambda e, d=d, j=j, bk=bk, w=w: e.matmul(
                        banks[bk][:, :w], wg[:, d * GLA_HK + j * 128:d * GLA_HK + (j + 1) * 128], z[d][:, :w],
                        start=True, stop=True), reads=[wgb, zb[d]], writes=[bkb[bk]])
                    si = n["s"] % 2
                    n["s"] += 1
                    oi = n["o"] % 3
                    n["o"] += 1
                    P.op("act", lambda e, d=d, j=j, bk=bk, si=si, w=w: e.activation(
                        out=sg[si][:, :w], in_=banks[bk][:, :w], func=AF.Sigmoid,
                        bias=bgs[:, d * 8 + j:d * 8 + j + 1], scale=1.0), reads=[bkb[bk], bgb], writes=[sgb[si]])
                    P.op("act", lambda e, si=si, oi=oi, w=w: e.activation(
                        out=ot[oi][:, :w], in_=sg[si][:, :w], func=AF.Ln), reads=[sgb[si]], writes=[otb[oi]])
                    r = d * 8 + j
                    outs.append(P.op("sp", lambda e, oi=oi, r=r, t0=t0, w=w: e.dma_start(
                        out=gT[r * 128:(r + 1) * 128, t0:t0 + w], in_=ot[oi][:, :w]), reads=[otb[oi]], dma=True))
        P.join("sp", outs)
        P.emit()
    return nc


class Rot:
    def __init__(self, P, name, shape, dt, n):
        self.t = [P.sb(f"{name}{i}", shape, dt) for i in range(n)]
        self.b = [P.buf(f"{name}{i}") for i in range(n)]
        self.n = n
        self.k = 0

    def next(self):
        i = self.k % self.n
        self.k += 1
        return self.t[i], self.b[i]


def build_glascan_program(NU, NCH):
    S_ = NCH * 128
    nc = bass.Bass("TRN2", target_bir_lowering=False)
    qT = nc.dram_tensor("qT", [NU, 256, S_], F32, kind="ExternalInput").ap()
    kT = nc.dram_tensor("kT", [NU, 256, S_], F32, kind="ExternalInput").ap()
    vv = nc.dram_tensor("v", [NU, S_, 512], F32, kind="ExternalInput").ap()
    gg = nc.dram_tensor("g", [NU, S_, 256], F32, kind="ExternalInput").ap()
    tri_d = nc.dram_tensor("tri", [128, 128], F32, kind="ExternalInput").ap()
    mask_d = nc.dram_tensor("mask", [128, 128], F32, kind="ExternalInput").ap()
    ident_d = nc.dram_tensor("ident", [128, 128], BF16, kind="ExternalInput").ap()
    oo = nc.dram_tensor("o", [NU, S_, 512], F32, kind="ExternalOutput").ap()
    with ExitStack() as es:
        P = Prog(nc, es)
        bankB = P.ps("bankB", [128, 2, 128]); bBb = P.buf()
        bankA = P.ps("bankA", [128, 128]); bAb = P.buf()
        bankO = [P.ps(f"bankO{i}", [128, 512]) for i in range(2)]; bOb = [P.buf() for i in range(2)]
        bankT = P.ps("bankT", [128, 2, 128], BF16); bTb = P.buf()
        bankU = [P.ps(f"bankU{j}", [128, 512]) for j in range(2)]; bUb = [P.buf() for j in range(2)]
        tri = P.sb("tri_sb", [128, 128], F32); trib = P.buf()
        mask = P.sb("mask_sb", [128, 128], F32); maskb = P.buf()
        ident = P.sb("ident_sb", [128, 128], BF16); identb = P.buf()
        P.op("sp", lambda e: e.dma_start(out=tri[:], in_=tri_d), writes=[trib], dma=True)
        P.op("sp", lambda e: e.dma_start(out=mask[:], in_=mask_d), writes=[maskb], dma=True)
        P.op("sp", lambda e: e.dma_start(out=ident[:], in_=ident_d), writes=[identb], dma=True)
        Sst = P.sb("Sst", [128, 2, 512], F32); Sb = [P.buf() for j in range(2)]
        Sp = P.sb("Sp", [128, 2, 512], BF16); Spb = [P.buf() for j in range(2)]
        rq = Rot(P, "q", [128, 2, 128], F32, 3)
        rk = Rot(P, "k", [128, 2, 128], F32, 3)
        rv = Rot(P, "v", [128, 512], BF16, 3)
        rg = Rot(P, "g", [128, 256], F32, 3)
        rB = Rot(P, "Bsb", [128, 2, 128], F32, 2)
        rs = Rot(P, "sc", [128, 5, 2], F32, 2)
        rEq = Rot(P, "Eq", [128, 2, 128], F32, 2)
        rEk = Rot(P, "Ek", [128, 2, 128], F32, 2)
        rqi = Rot(P, "qi", [128, 2, 128], BF16, 2)
        rki = Rot(P, "ki", [128, 2, 128], BF16, 2)
        ram = Rot(P, "am", [128, 128], BF16, 2)
        rkit = Rot(P, "kit", [128, 2, 128], BF16, 2)
        ro = Rot(P, "osb", [128, 512], F32, 2)
        rtu = Rot(P, "tu", [128, 512], F32, 2)
        outs = []
        no = 0
        for u in range(NU):
            for j in range(2):
                P.op("dve", lambda e, j=j: e.memset(Sst[:, j, :], 0.0), writes=[Sb[j]])
            for n in range(NCH):
                ts = slice(n * 128, (n + 1) * 128)
                q, qb = rq.next(); k, kb = rk.next(); v, vb = rv.next(); g, gb = rg.next()
                P.op("sp", lambda e, q=q, u=u, ts=ts: e.dma_start(
                    out=q[:], in_=qT[u].rearrange("(j p) t -> p j t", p=128)[:, :, ts]), writes=[qb], dma=True)
                P.op("sp", lambda e, k=k, u=u, ts=ts: e.dma_start(
                    out=k[:], in_=kT[u].rearrange("(j p) t -> p j t", p=128)[:, :, ts]), writes=[kb], dma=True)
                P.op("pool", lambda e, v=v, u=u, ts=ts: e.dma_start(out=v[:], in_=vv[u, ts, :]), writes=[vb], dma=True)
                P.op("sp", lambda e, g=g, u=u, ts=ts: e.dma_start(out=g[:], in_=gg[u, ts, :]), writes=[gb], dma=True)
                for j in range(2):
                    P.op("pe", lambda e, g=g, j=j: e.matmul(bankB[:, j, :], g[:, j * 128:(j + 1) * 128], tri[:],
                                                           start=True, stop=True),
                         reads=[gb, trib], writes=[bBb])
                B, Bb = rB.next()
                P.op("act", lambda e, B=B: e.activation(out=B[:], in_=bankB[:], func=AF.Identity),
                     reads=[bBb], writes=[Bb])
                sc, scb = rs.next()
                P.op("dve", lambda e, B=B, sc=sc: e.tensor_scalar(
                    out=sc[:, 0, :], in0=B[:, :, 63], scalar1=-1.0, scalar2=None, op0=ALU.mult),
                    reads=[Bb], writes=[scb])
                P.op("dve", lambda e, B=B, sc=sc: e.tensor_tensor(
                    out=sc[:, 1, :], in0=B[:, :, 127], in1=sc[:, 0, :], op=ALU.add), reads=[Bb, scb], writes=[scb])
                P.op("act", lambda e, B=B, sc=sc: e.activation(out=sc[:, 2, :], in_=B[:, :, 63], func=AF.Exp),
                     reads=[Bb, scb], writes=[scb])
                P.op("act", lambda e, B=B, sc=sc: e.activation(out=sc[:, 3, :], in_=B[:, :, 127], func=AF.Exp),
                     reads=[Bb, scb], writes=[scb])
                P.op("act", lambda e, sc=sc: e.activation(out=sc[:, 4, :], in_=sc[:, 1, :], func=AF.Exp),
                     reads=[scb], writes=[scb])
                Eq, Eqb = rEq.next(); Ek, Ekb = rEk.next()
                for j in range(2):
                    P.op("act", lambda e, B=B, sc=sc, Eq=Eq, j=j: e.activation(
                        out=Eq[:, j, :], in_=B[:, j, :], func=AF.Exp, bias=sc[:, 0, j:j + 1], scale=1.0),
                        reads=[Bb, scb], writes=[Eqb])
                    P.op("act", lambda e, B=B, Ek=Ek, j=j: e.activation(
                        out=Ek[:, j, :], in_=B[:, j, :], func=AF.Exp, bias=B[:, j, 63:64], scale=-1.0),
                        reads=[Bb], writes=[Ekb])
                qi, qib = rqi.next(); ki, kib = rki.next()
                P.op("dve", lambda e, q=q, Eq=Eq, qi=qi: e.tensor_tensor(out=qi[:], in0=q[:], in1=Eq[:], op=ALU.mult),
                     reads=[qb, Eqb], writes=[qib])
                P.op("dve", lambda e, k=k, Ek=Ek, ki=ki: e.tensor_tensor(out=ki[:], in0=k[:], in1=Ek[:], op=ALU.mult),
                     reads=[kb, Ekb], writes=[kib])
                for j in range(2):
                    P.op("act", lambda e, sc=sc, j=j: e.activation(
                        out=Sp[:, j, :], in_=Sst[:, j, :], func=AF.Identity, scale=sc[:, 2, j:j + 1]),
                        reads=[Sb[j], scb], writes=[Spb[j]])
                for j in range(2):
                    P.op("pe", lambda e, ki=ki, qi=qi, j=j: e.matmul(bankA[:], ki[:, j, :], qi[:, j, :],
                                                                    start=(j == 0), stop=(j == 1)),
                         reads=[kib, qib], writes=[bAb])
                am, amb = ram.next()
                P.op("dve", lambda e, am=am: e.tensor_tensor(out=am[:], in0=bankA[:], in1=mask[:], op=ALU.mult),
                     reads=[bAb, maskb], writes=[amb])
                oi = no % 2
                no += 1
                P.op("pe", lambda e, am=am, v=v, oi=oi: e.matmul(bankO[oi][:], am[:], v[:], start=True, stop=False),
                     reads=[amb, vb], writes=[bOb[oi]])
                for j in range(2):
                    P.op("pe", lambda e, qi=qi, j=j, oi=oi: e.matmul(bankO[oi][:], qi[:, j, :], Sp[:, j, :],
                                                                    start=False, stop=(j == 1)),
                         reads=[qib, Spb[j]], writes=[bOb[oi]])
                osb, osbb = ro.next()
                P.op("act", lambda e, osb=osb, oi=oi: e.activation(out=osb[:], in_=bankO[oi][:], func=AF.Identity),
                     reads=[bOb[oi]], writes=[osbb])
                outs.append(P.op("sp", lambda e, osb=osb, u=u, ts=ts: e.dma_start(out=oo[u, ts, :], in_=osb[:]),
                                 reads=[osbb], dma=True))
                for j in range(2):
                    P.op("pe", lambda e, ki=ki, j=j: e.transpose(out=bankT[:, j, :], in_=ki[:, j, :], identity=ident[:]),
                         reads=[kib, identb], writes=[bTb])
                kit, kitb = rkit.next()
                P.op("dve", lambda e, kit=kit: e.tensor_copy(out=kit[:], in_=bankT[:]), reads=[bTb], writes=[kitb])
                for j in range(2):
                    P.op("pe", lambda e, kit=kit, v=v, j=j: e.matmul(bankU[j][:], kit[:, j, :], v[:],
                                                                    start=True, stop=True),
                         reads=[kitb, vb], writes=[bUb[j]])
                    tu, tub = rtu.next()
                    P.op("act", lambda e, tu=tu, sc=sc, j=j: e.activation(
                        out=tu[:], in_=bankU[j][:], func=AF.Identity, scale=sc[:, 4, j:j + 1]),
                        reads=[bUb[j], scb], writes=[tub])
                    P.op("dve", lambda e, tu=tu, sc=sc, j=j: e.scalar_tensor_tensor(
                        out=Sst[:, j, :], in0=Sst[:, j, :], scalar=sc[:, 3, j:j + 1], in1=tu[:],
                        op0=ALU.mult, op1=ALU.add), reads=[Sb[j], tub, scb], writes=[Sb[j]])
        P.join("sp", outs)
        P.emit()
    return nc


import ml_dtypes

BATCH = 4
SEQ = 4096
CTX = 256
DEPTH = 4
HALF = SEQ // 2
CH = CTX // 2
TL = HALF + CH
CORES = list(range(NCORES))
_PROGS = {}
_DUMP = None


def _prog(key, builder):
    if key not in _PROGS:
        _PROGS[key] = builder()
    return _PROGS[key]


def _run(nc, in_maps):
    res = run_bass_kernel_spmd(nc, in_maps, core_ids=CORES)
    return res.results


def _fm(v):
    return np.ascontiguousarray(np.asarray(v, np.float32).reshape(-1, 128).T)


def _mods(sets):
    return np.ascontiguousarray(
        np.concatenate([_fm(v) for s in sets for v in s], axis=1))


def _cat(parts):
    return np.ascontiguousarray(np.concatenate(parts, axis=1))


TILES_L = [(0, 512, 0), (512, 512, 0), (1024, 512, 0), (1536, 512, 0)]
TILES_LC = TILES_L + [(2048, 128, 1)]


def kernel_unfused(x, c, ctx, c_ctx, w_mod, b_mod, norm_g, ffn_w_gate, ffn_w_up, ffn_w_down,
           gla_w_in, gla_w_gate_up, gla_b_gate, gla_g_head, gla_w_out,
           fnet_w_out, fnet_b_out,
           cm_w_pw1, cm_b_pw1, cm_w_dw, cm_b_dw, cm_ln_g, cm_ln_b, cm_w_pw2, cm_b_pw2,
           final_g):
    f32 = lambda a: np.asarray(a, dtype=np.float32)
    x, c, ctx, c_ctx = f32(x), f32(c), f32(ctx), f32(c_ctx)
    w_mod, b_mod, norm_g = f32(w_mod), f32(b_mod), f32(norm_g)
    ffn_w_gate, ffn_w_up, ffn_w_down = f32(ffn_w_gate), f32(ffn_w_up), f32(ffn_w_down)
    gla_w_in, gla_w_gate_up, gla_b_gate = f32(gla_w_in), f32(gla_w_gate_up), f32(gla_b_gate)
    gla_g_head, gla_w_out = f32(gla_g_head), f32(gla_w_out)
    fnet_w_out, fnet_b_out = f32(fnet_w_out), f32(fnet_b_out)
    cm_w_pw1, cm_b_pw1, cm_w_dw, cm_b_dw = f32(cm_w_pw1), f32(cm_b_pw1), f32(cm_w_dw), f32(cm_b_dw)
    cm_ln_g, cm_ln_b, cm_w_pw2, cm_b_pw2 = f32(cm_ln_g), f32(cm_ln_b), f32(cm_w_pw2), f32(cm_b_pw2)
    final_g = f32(final_g)
    zeros_d = np.zeros(D, np.float32)

    sc = np.concatenate([c, c_ctx[None]], axis=0)
    scT = np.ascontiguousarray(sc.reshape(NROW, KC, 128).transpose(2, 1, 0)).reshape(128, KC * NROW)
    nc_mod = _prog("mod", lambda: build_mod_program(DEPTH))
    ims = [{"scT": scT,
            "wm": np.ascontiguousarray(w_mod[:, :, k * MODC:(k + 1) * MODC]),
            "bm": np.ascontiguousarray(b_mod[:, k * MODC:(k + 1) * MODC])} for k in CORES]
    r = _run(nc_mod, ims)
    mod = np.concatenate([r[k]["out"] for k in CORES], axis=2).reshape(DEPTH, NROW, NMOD, D)

    xT = []
    for k in CORES:
        b, hf = k // 2, k % 2
        xT.append(_cat([x[b, hf * HALF:(hf + 1) * HALF].T, ctx[b, hf * CH:(hf + 1) * CH].T]))

    out = np.zeros((BATCH, SEQ, D), np.float32)
    for i in range(DEPTH):
        kind, j, last = i % 3, i // 3, i == DEPTH - 1
        rows = lambda k: (k // 2, BATCH)
        nc_head = _prog("head", lambda: build_ffn_program(TL, TILES_LC, 2, 1, False, True, False))
        ims = []
        for k in CORES:
            sets = [[norm_g[i, 0], mod[i, rw, 0], mod[i, rw, 1], mod[i, rw, 2],
                     norm_g[i, 1], mod[i, rw, 3], mod[i, rw, 4]] for rw in rows(k)]
            ims.append({"xT": xT[k], "mods": _mods(sets), "wg0": ffn_w_gate[i, 0], "wu0": ffn_w_up[i, 0],
                        "wd0": ffn_w_down[i, 0]})
        r = _run(nc_head, ims)
        xT = [r[k]["oT"] for k in CORES]
        hT = [r[k]["hT"] for k in CORES]
        H_c = [_cat([hT[2 * b][:, HALF:], hT[2 * b + 1][:, HALF:]]) for b in range(BATCH)]

        tail_extra = [dict() for _ in CORES]
        if kind == 0:
            nc_gp = _prog("gproj", lambda: build_glaproj_program(TL, [(t0, w) for t0, w, _ in TILES_LC]))
            wgu = _cat([gla_w_gate_up[j, 0], gla_w_gate_up[j, 1]])
            bg = _fm(gla_b_gate[j].reshape(-1))
            r = _run(nc_gp, [{"hT": hT[k], "win": gla_w_in[j], "wgu": wgu, "bg": bg} for k in CORES])
            pT = [r[k]["pT"] for k in CORES]
            gT = [r[k]["gT"] for k in CORES]
            P_l = [_cat([pT[2 * b][:, :HALF], pT[2 * b + 1][:, :HALF]]) for b in range(BATCH)]
            P_c = [_cat([pT[2 * b][:, HALF:], pT[2 * b + 1][:, HALF:]]) for b in range(BATCH)]
            G_l = [_cat([gT[2 * b][:, :HALF], gT[2 * b + 1][:, :HALF]]) for b in range(BATCH)]
            G_c = [_cat([gT[2 * b][:, HALF:], gT[2 * b + 1][:, HALF:]]) for b in range(BATCH)]
            NCH = (CTX + SEQ) // 128
            nc_sc = _prog("gscan", lambda: build_glascan_program(4, NCH))
            s_, t_ = np.arange(128)[:, None], np.arange(128)[None, :]
            maskc = (s_ <= t_).astype(np.float32)
            tric = (maskc / 16.0).astype(np.float32)
            identc = np.eye(128).astype(ml_dtypes.bfloat16)

            def seq(ac, al, d):
                if d == 0:
                    return np.concatenate([ac, al], axis=1)
                return np.concatenate([ac[:, ::-1], al[:, ::-1]], axis=1)

            ims = []
            for k in CORES:
                b, hf = k // 2, k % 2
                qs, ks, vs, gs = [], [], [], []
                for hl in range(2):
                    h = 2 * hf + hl
                    for d in range(2):
                        qs.append(seq(P_c[b][h * 256:(h + 1) * 256], P_l[b][h * 256:(h + 1) * 256], d))
                        ks.append(seq(P_c[b][1024 + h * 256:1024 + (h + 1) * 256],
                                      P_l[b][1024 + h * 256:1024 + (h + 1) * 256], d))
                        vs.append(seq(P_c[b][2048 + h * 512:2048 + (h + 1) * 512],
                                      P_l[b][2048 + h * 512:2048 + (h + 1) * 512], d).T)
                        gs.append(seq(G_c[b][d * 1024 + h * 256:d * 1024 + (h + 1) * 256],
                                      G_l[b][d * 1024 + h * 256:d * 1024 + (h + 1) * 256], d).T)
                ims.append({"qT": np.ascontiguousarray(np.stack(qs)), "kT": np.ascontiguousarray(np.stack(ks)),
                            "v": np.ascontiguousarray(np.stack(vs)), "g": np.ascontiguousarray(np.stack(gs)),
                            "tri": tric, "mask": maskc, "ident": identc})
            r = _run(nc_sc, ims)
            O_l = [[np.zeros((D, SEQ), np.float32) for _ in range(2)] for _ in range(BATCH)]
            O_c = [[np.zeros((D, CTX), np.float32) for _ in range(2)] for _ in range(BATCH)]
            for k in CORES:
                b, hf = k // 2, k % 2
                o = r[k]["o"]
                for hl in range(2):
                    h = 2 * hf + hl
                    for d in range(2):
                        ou = o[hl * 2 + d]
                        oc, ol = ou[:CTX], ou[CTX:]
                        if d == 1:
                            oc, ol = oc[::-1], ol[::-1]
                        O_l[b][d][h * 512:(h + 1) * 512] = ol.T
                        O_c[b][d][h * 512:(h + 1) * 512] = oc.T
            for k in CORES:
                b, hf = k // 2, k % 2
                def tok(al, ac):
                    if last:
                        return np.ascontiguousarray(al[:, hf * HALF:(hf + 1) * HALF])
                    return _cat([al[:, hf * HALF:(hf + 1) * HALF], ac[:, hf * CH:(hf + 1) * CH]])
                rs = pT[k][4096:6144]
                tail_extra[k] = {"ofT": tok(O_l[b][0], O_c[b][0]), "obT": tok(O_l[b][1], O_c[b][1]),
                                 "rsT": np.ascontiguousarray(rs[:, :HALF] if last else rs),
                                 "gh": _fm(gla_g_head[j]), "wo": gla_w_out[j]}
            bias_vec = zeros_d
        elif kind == 1:
            seqs = [("l", SEQ, HALF), ("c", CTX, CTX)]
            nc_fn = _prog("fnet", lambda: build_fnet_program(seqs))
            m_ = np.arange(GW)
            ang = 2 * np.pi * np.outer(m_, m_) / GW
            cwf = np.concatenate([np.cos(ang), np.sin(ang)], axis=1)
            cwc = np.ascontiguousarray(cwf.reshape(2, 128, 512).transpose(1, 0, 2)).reshape(128, 1024) \
                .astype(ml_dtypes.bfloat16)

            def dft(L, k0, nk):
                n_ = np.arange(L, dtype=np.int64)[:, None]
                k_ = (np.arange(nk, dtype=np.int64) + k0)[None, :]
                a = 2 * np.pi * ((n_ * k_) % L).astype(np.float64) / L
                nrm = 1.0 / np.sqrt(L * GW)
                return (np.cos(a) * nrm).astype(ml_dtypes.bfloat16), (-np.sin(a) * nrm).astype(ml_dtypes.bfloat16)

            dl = [dft(SEQ, hf * HALF, HALF) for hf in range(2)]
            dc = dft(CTX, 0, CTX)
            H_l = [_cat([hT[2 * b][:, :HALF], hT[2 * b + 1][:, :HALF]]) for b in range(BATCH)]
            ims = []
            for k in CORES:
                b, hf = k // 2, k % 2
                ims.append({"cw": cwc, "hT_l": H_l[b], "cl_l": dl[hf][0], "sl_l": dl[hf][1],
                            "hT_c": H_c[b], "cl_c": dc[0], "sl_c": dc[1]})
            r = _run(nc_fn, ims)
            for k in CORES:
                hf = k % 2
                tail_extra[k] = {"mT": _cat([r[k]["fT_l"], r[k]["fT_c"][:, hf * CH:(hf + 1) * CH]]),
                                 "wo": fnet_w_out[j]}
            bias_vec = fnet_b_out[j]
        else:
            tiles_cv = [(t0, w, 64) for t0, w, _ in TILES_L] + [(HALF, CTX, CTX)]
            nc_cv = _prog("conv", lambda: build_conv_program(HALF + CTX, tiles_cv))
            vecs = np.ascontiguousarray(np.concatenate(
                [_fm(cm_b_pw1[j][:D]), _fm(cm_b_pw1[j][D:]), _fm(cm_b_dw[j]), _fm(cm_ln_g[j]), _fm(cm_ln_b[j])], axis=1))
            wk = np.ascontiguousarray(cm_w_dw[j].T.reshape(KC, 128, CONV_W).transpose(1, 0, 2)).reshape(128, KC * CONV_W)
            ims = []
            for k in CORES:
                b = k // 2
                ims.append({"hT": _cat([hT[k][:, :HALF], H_c[b]]), "w1": cm_w_pw1[j], "vecs": vecs, "wdw": wk})
            r = _run(nc_cv, ims)
            for k in CORES:
                hf = k % 2
                m = r[k]["mT"]
                tail_extra[k] = {"mT": _cat([m[:, :HALF], m[:, HALF + hf * CH:HALF + (hf + 1) * CH]]),
                                 "wo": cm_w_pw2[j]}
            bias_vec = cm_b_pw2[j]

        gla = kind == 0
        if last:
            nc_tail = _prog(("tail_last", gla), lambda: build_ffn_program(HALF, TILES_L, 1, 1, True, False, True, gla_in=gla))
        else:
            nc_tail = _prog(("tail", gla), lambda: build_ffn_program(TL, TILES_LC, 2, 1, True, False, False, gla_in=gla))
        ims = []
        for k in CORES:
            rws = rows(k)[:1] if last else rows(k)
            sets = []
            for rw in rws:
                s = [norm_g[i, 2], mod[i, rw, 6], mod[i, rw, 7], mod[i, rw, 8], mod[i, rw, 5], bias_vec]
                if last:
                    s.append(final_g)
                sets.append(s)
            im = {"xT": np.ascontiguousarray(xT[k][:, :HALF]) if last else xT[k], "mods": _mods(sets),
                  "wg0": ffn_w_gate[i, 1], "wu0": ffn_w_up[i, 1], "wd0": ffn_w_down[i, 1]}
            im.update(tail_extra[k])
            ims.append(im)
        r = _run(nc_tail, ims)
        xT = [r[k]["oT"] for k in CORES]
        if _DUMP is not None:
            _DUMP(i, xT)

    for k in CORES:
        b, hf = k // 2, k % 2
        out[b, hf * HALF:(hf + 1) * HALF] = xT[k].T
    return out


TT = HALF + CTX
FT_L = [(0, 512, 0), (512, 512, 0), (1024, 512, 0), (1536, 512, 0)]
FT_LC = FT_L + [(HALF, CTX, 1)]
NCHK = TT // 128


class Shared:
    pass


def fused_common(P, R, W=512):
    R.sq = [P.sb("sq", [128, W], F32) for i in range(2)]
    R.sq_b = [P.buf() for i in range(2)]
    R.rstd = P.sb("rstd", [128, W], F32)
    R.rstd_b = P.buf()
    R.tmp = [P.sb("tmp", [128, W], F32) for i in range(2)]
    R.tmp_b = [P.buf() for i in range(2)]
    R.n_sq = 0
    R.n_tmp = 0
    R.P = P


def emit_vec_copy(P, R, dst, src, srcb):
    return P.op("dve", lambda e: e.tensor_copy(out=dst, in_=src), reads=[srcb], writes=[R.wvb])


def emit_ffn_phase(P, R, tiles, x_src, x_dst, sets, ffn_w, proj=None, h_dst=None, final_dst=None):
    W = 512
    fused_common(P, R)
    C = R
    n_sets = len(sets)
    NV = 11
    wv = P.sb("wv", [128, n_sets * NV * KC], F32)
    R.wvb = P.buf("wv")
    mvb = R.wvb

    def vec(s, k):
        o = (s * NV + k) * KC
        return wv[:, o:o + KC]

    for s, st in enumerate(sets):
        items = []
        if ffn_w is not None:
            items += list(zip(range(0, 4), st["ffn"]))
        if proj is not None:
            items += list(zip(range(4, 6), st["proj"]))
        if h_dst is not None:
            items += list(zip(range(6, 9), st["h"]))
        if final_dst is not None:
            items += [(9, st["fin"])]
        for k, src in items:
            emit_vec_copy(P, R, vec(s, k), src, R.persist_b)
        norm_slots = ([0] if ffn_w is not None else []) + ([6] if h_dst is not None else [])
        for k in norm_slots:
            P.op("dve", lambda e, s=s, k=k: e.scalar_tensor_tensor(
                out=vec(s, k), in0=vec(s, k + 2), scalar=1.0, in1=vec(s, k), op0=ALU.add, op1=ALU.mult),
                reads=[mvb], writes=[mvb])
        if ffn_w is not None:
            P.op("dve", lambda e, s=s: e.tensor_scalar(
                out=vec(s, 3), in0=vec(s, 3), scalar1=0.5, scalar2=None, op0=ALU.mult), reads=[mvb], writes=[mvb])

    x = P.sb("x", [128, KC, W], F32)
    xb = [P.buf() for c in range(KC)]
    h = P.sb("h", [128, KC, W], BF16)
    hb = [P.buf() for c in range(KC)]
    if ffn_w is not None:
        wg, wu, wd = ffn_w
        a = P.sb("a", [128, FC, W], BF16)
        ab = [P.buf() for c in range(FC)]
        sg = [P.sb("sg", [128, W], F32) for i in range(2)]
        sgb = [P.buf() for i in range(2)]
        FB, NWB, DFB, NDB = 2, 2, 4, 3
        wgt = [P.sb("wgt", [128, KC, FB * 128], BF16) for i in range(NWB)]
        wgb = [P.buf() for i in range(NWB)]
        wut = [P.sb("wut", [128, KC, FB * 128], BF16) for i in range(NWB)]
        wub = [P.buf() for i in range(NWB)]
        wdt = [P.sb("wdt", [128, DFB, 512], BF16) for i in range(NDB)]
        wdb = [P.buf() for i in range(NDB)]
    if h_dst is not None or final_dst is not None:
        ho = [P.sb("ho", [128, W], F32) for i in range(2)]
        hob = [P.buf() for i in range(2)]
    if proj is not None:
        wot = [P.sb("wot", [128, KC, 128], BF16) for i in range(2)]
        wotb = [P.buf() for i in range(2)]
        if proj["kind"] == "m":
            m32 = [P.sb("m32", [128, W], F32) for i in range(2)]
            m32b = [P.buf() for i in range(2)]
        else:
            roa = Rot(P, "oa", [128, 512], F32, 2)
            rob = Rot(P, "ob", [128, 512], F32, 2)
            rrs = Rot(P, "rs", [128, 512], F32, 2)
            rsq = Rot(P, "gsq", [128, 512], F32, 1)
            rss = Rot(P, "gss", [128, 2], F32, 2)
            rmb = Rot(P, "gmb", [128, 512], BF16, 2)
            bT = R.banks[7][:, 0:256].bitcast(BF16).rearrange("p (a b) -> p a b", b=128)
    cnt = {"w": 0, "d": 0, "sg": 0, "ho": 0, "m": 0, "wo": 0}

    for (t0, w, s) in tiles:
        for half in range(2):
            cs = slice(half * 8, half * 8 + 8)
            P.op("sp", lambda e, cs=cs, t0=t0, w=w: e.dma_start(
                out=x[:, cs, :w], in_=chunked(x_src)[:, cs, t0:t0 + w]), writes=xb[cs], dma=True)
        if proj is not None:
            if proj["kind"] == "m":
                mT = proj["m"]
                for c in range(KC):
                    i = cnt["m"] % 2
                    cnt["m"] += 1
                    P.op("sp", lambda e, c=c, i=i, t0=t0, w=w: e.dma_start(
                        out=m32[i][:, :w], in_=mT[c * 128:(c + 1) * 128, t0:t0 + w]), writes=[m32b[i]], dma=True)
                    P.op("act", lambda e, c=c, i=i, w=w: e.activation(out=h[:, c, :w], in_=m32[i][:, :w],
                                                                     func=AF.Identity),
                         reads=[m32b[i]], writes=[hb[c]])
            else:
                OA, OB, RS, gh = proj["oa"], proj["ob"], proj["rs"], proj["gh"]
                for tc in range(w // 128):
                    r0 = t0 + tc * 128
                    for hd in range(4):
                        oa, oab = roa.next(); ob_, obb = rob.next(); rs_, rsb = rrs.next()
                        P.op("sp", lambda e, oa=oa, r0=r0, hd=hd: e.dma_start(
                            out=oa[:], in_=OA[r0:r0 + 128, hd * 512:(hd + 1) * 512]), writes=[oab], dma=True)
                        P.op("sp", lambda e, ob_=ob_, r0=r0, hd=hd: e.dma_start(
                            out=ob_[:], in_=OB[r0:r0 + 128, hd * 512:(hd + 1) * 512]), writes=[obb], dma=True)
                        P.op("act", lambda e, rs_=rs_, r0=r0, hd=hd: e.dma_start(
                            out=rs_[:], in_=RS[r0:r0 + 128, hd * 512:(hd + 1) * 512]), writes=[rsb], dma=True)
                        P.op("dve", lambda e, oa=oa, ob_=ob_: e.tensor_tensor(out=oa[:], in0=oa[:], in1=ob_[:], op=ALU.add),
                             reads=[oab, obb], writes=[oab])
                        sqt, sqtb = rsq.next(); ss, ssb = rss.next()
                        P.op("act", lambda e, oa=oa, sqt=sqt: e.activation(out=sqt[:], in_=oa[:], func=AF.Square),
                             reads=[oab], writes=[sqtb])
                        P.op("dve", lambda e, sqt=sqt, ss=ss: e.reduce_sum(out=ss[:, 0:1], in_=sqt[:],
                                                                         axis=mybir.AxisListType.X),
                             reads=[sqtb], writes=[ssb])
                        P.op("act", lambda e, ss=ss: e.activation(out=ss[:, 1:2], in_=ss[:, 0:1], func=AF.Sqrt,
                                                                 bias=R.epsb[:, 0:1], scale=1.0 / 512.0),
                             reads=[ssb, R.persist_b], writes=[ssb])
                        P.op("dve", lambda e, ss=ss: e.reciprocal(out=ss[:, 1:2], in_=ss[:, 1:2]),
                             reads=[ssb], writes=[ssb])
                        P.op("dve", lambda e, oa=oa, ss=ss: e.scalar_tensor_tensor(
                            out=oa[:], in0=oa[:], scalar=ss[:, 1:2], in1=gh[:], op0=ALU.mult, op1=ALU.mult),
                            reads=[oab, ssb, R.persist_b], writes=[oab])
                        mb_, mbb = rmb.next()
                        P.op("dve", lambda e, oa=oa, rs_=rs_, mb_=mb_: e.tensor_tensor(
                            out=mb_[:], in0=oa[:], in1=rs_[:], op=ALU.mult), reads=[oab, rsb], writes=[mbb])
                        for cc in range(4):
                            P.op("pe", lambda e, mb_=mb_, cc=cc: e.transpose(
                                out=bT[:, cc, :], in_=mb_[:, cc * 128:(cc + 1) * 128], identity=R.ident[:]),
                                reads=[mbb, R.persist_b], writes=[R.bank_b[7]])
                        hsl = hb[4 * hd:4 * hd + 4]
                        P.op("act", lambda e, hd=hd, tc=tc: e.activation(
                            out=h[:, 4 * hd:4 * hd + 4, tc * 128:(tc + 1) * 128], in_=bT[:], func=AF.Identity),
                            reads=[R.bank_b[7]], writes=hsl)
            wo = proj["wo"]
            for dc in range(KC):
                i = cnt["wo"] % 2
                cnt["wo"] += 1
                P.op("pool", lambda e, dc=dc, i=i: e.dma_start(
                    out=wot[i][:], in_=chunked(wo)[:, :, dc * 128:(dc + 1) * 128]), writes=[wotb[i]], dma=True)
                bank = 4 + (dc % 3)
                for kc in range(KC):
                    P.op("pe", lambda e, kc=kc, i=i, bank=bank, w=w: e.matmul(
                        R.banks[bank][:, :w], wot[i][:, kc, :], h[:, kc, :w], start=(kc == 0), stop=(kc == KC - 1)),
                        reads=[wotb[i], hb[kc]], writes=[R.bank_b[bank]])
                ti = C.n_tmp % 2
                C.n_tmp += 1
                P.op("act", lambda e, dc=dc, ti=ti, bank=bank, w=w, s=s: e.activation(
                    out=C.tmp[ti][:, :w], in_=R.banks[bank][:, :w], func=AF.Identity,
                    bias=vec(s, 5)[:, dc:dc + 1], scale=1.0), reads=[R.bank_b[bank], mvb], writes=[C.tmp_b[ti]])
                P.op("dve", lambda e, dc=dc, ti=ti, w=w, s=s: e.scalar_tensor_tensor(
                    out=x[:, dc, :w], in0=C.tmp[ti][:, :w], scalar=vec(s, 4)[:, dc:dc + 1], in1=x[:, dc, :w],
                    op0=ALU.mult, op1=ALU.add), reads=[C.tmp_b[ti], mvb, xb[dc]], writes=[xb[dc]])

        if ffn_w is not None:
            emit_rstd(C, x, xb, w, 0)
            emit_prenorm(C, x, xb, w, vec(s, 0), vec(s, 1), mvb, lambda c, w=w: h[:, c, :w], hb)
            for fb in range(FC // FB):
                i = cnt["w"] % NWB
                cnt["w"] += 1
                fsl = slice(fb * FB * 128, (fb + 1) * FB * 128)
                P.op("pool", lambda e, i=i, fsl=fsl: e.dma_start(out=wgt[i][:], in_=chunked(wg)[:, :, fsl]),
                     writes=[wgb[i]], dma=True)
                P.op("pool", lambda e, i=i, fsl=fsl: e.dma_start(out=wut[i][:], in_=chunked(wu)[:, :, fsl]),
                     writes=[wub[i]], dma=True)
                for f in range(FB):
                    fc = fb * FB + f
                    par = fc % 2
                    gb, ub = 2 * par, 2 * par + 1
                    for kc in range(KC):
                        P.op("pe", lambda e, i=i, f=f, kc=kc, gb=gb, w=w: e.matmul(
                            R.banks[gb][:, :w], wgt[i][:, kc, f * 128:(f + 1) * 128], h[:, kc, :w],
                            start=(kc == 0), stop=(kc == KC - 1)), reads=[wgb[i], hb[kc]], writes=[R.bank_b[gb]])
                    for kc in range(KC):
                        P.op("pe", lambda e, i=i, f=f, kc=kc, ub=ub, w=w: e.matmul(
                            R.banks[ub][:, :w], wut[i][:, kc, f * 128:(f + 1) * 128], h[:, kc, :w],
                            start=(kc == 0), stop=(kc == KC - 1)), reads=[wub[i], hb[kc]], writes=[R.bank_b[ub]])
                    si = cnt["sg"] % 2
                    cnt["sg"] += 1
                    P.op("act", lambda e, si=si, gb=gb, w=w: e.activation(
                        out=sg[si][:, :w], in_=R.banks[gb][:, :w], func=AF.Silu),
                        reads=[R.bank_b[gb]], writes=[sgb[si]])
                    P.op("dve", lambda e, si=si, ub=ub, fc=fc, w=w: e.tensor_tensor(
                        out=a[:, fc, :w], in0=sg[si][:, :w], in1=R.banks[ub][:, :w], op=ALU.mult),
                        reads=[sgb[si], R.bank_b[ub]], writes=[ab[fc]])
            for dg in range(4):
                base = 4 if dg % 2 == 0 else 0
                for fb in range(FC // DFB):
                    i = cnt["d"] % NDB
                    cnt["d"] += 1
                    P.op("pool", lambda e, i=i, fb=fb, dg=dg: e.dma_start(
                        out=wdt[i][:], in_=wd[fb * DFB * 128:(fb + 1) * DFB * 128, dg * 512:(dg + 1) * 512]
                        .rearrange("(c p) n -> p c n", p=128)), writes=[wdb[i]], dma=True)
                    for f in range(DFB):
                        fc = fb * DFB + f
                        for dc in range(4):
                            P.op("pe", lambda e, i=i, f=f, fc=fc, dc=dc, base=base, w=w: e.matmul(
                                R.banks[base + dc][:, :w], wdt[i][:, f, dc * 128:(dc + 1) * 128], a[:, fc, :w],
                                start=(fc == 0), stop=(fc == FC - 1)),
                                reads=[wdb[i], ab[fc]], writes=[R.bank_b[base + dc]])
                for dc in range(4):
                    c = dg * 4 + dc
                    P.op("dve", lambda e, c=c, dc=dc, base=base, w=w, s=s: e.scalar_tensor_tensor(
                        out=x[:, c, :w], in0=R.banks[base + dc][:, :w], scalar=vec(s, 3)[:, c:c + 1],
                        in1=x[:, c, :w], op0=ALU.mult, op1=ALU.add),
                        reads=[R.bank_b[base + dc], mvb, xb[c]], writes=[xb[c]])

        if final_dst is not None:
            emit_rstd(C, x, xb, w, 0)
            for c in range(KC):
                i = cnt["ho"] % 2
                cnt["ho"] += 1
                P.op("dve", lambda e, c=c, i=i, w=w, s=s: e.scalar_tensor_tensor(
                    out=ho[i][:, :w], in0=x[:, c, :w], scalar=vec(s, 9)[:, c:c + 1], in1=C.rstd[:, :w],
                    op0=ALU.mult, op1=ALU.mult), reads=[xb[c], C.rstd_b, mvb], writes=[hob[i]])
                R.final_dmas.append(P.op("sp", lambda e, c=c, i=i, t0=t0, w=w: e.dma_start(
                    out=final_dst[c * 128:(c + 1) * 128, t0:t0 + w], in_=ho[i][:, :w]), reads=[hob[i]], dma=True))
        else:
            for half in range(2):
                cs = slice(half * 8, half * 8 + 8)
                P.op("sp", lambda e, cs=cs, t0=t0, w=w: e.dma_start(
                    out=chunked(x_dst)[:, cs, t0:t0 + w], in_=x[:, cs, :w]), reads=xb[cs], dma=True)
        if h_dst is not None:
            emit_rstd(C, x, xb, w, 0)
            for c in range(KC):
                i = cnt["ho"] % 2
                cnt["ho"] += 1
                ti = C.n_tmp % 2
                C.n_tmp += 1
                P.op("dve", lambda e, c=c, ti=ti, w=w, s=s: e.scalar_tensor_tensor(
                    out=C.tmp[ti][:, :w], in0=x[:, c, :w], scalar=vec(s, 6)[:, c:c + 1], in1=C.rstd[:, :w],
                    op0=ALU.mult, op1=ALU.mult), reads=[xb[c], C.rstd_b, mvb], writes=[C.tmp_b[ti]])
                P.op("act", lambda e, c=c, i=i, ti=ti, w=w, s=s: e.activation(
                    out=ho[i][:, :w], in_=C.tmp[ti][:, :w], func=AF.Identity, bias=vec(s, 7)[:, c:c + 1],
                    scale=1.0), reads=[C.tmp_b[ti], mvb], writes=[hob[i]])
                P.op("sp", lambda e, c=c, i=i, t0=t0, w=w: e.dma_start(
                    out=h_dst[c * 128:(c + 1) * 128, t0:t0 + w], in_=ho[i][:, :w]), reads=[hob[i]], dma=True)
    P.barrier()


def emit_mod_phase(P, R, sc2T_d, wm_d, bm_d, depth):
    sc = P.sb("sc", [128, KC * 2], F32)
    scb = P.buf()
    ones1 = P.sb("ones1", [1, 2], F32)
    onesb = P.buf()
    P.op("pool", lambda e: e.memset(ones1[:], 1.0), writes=[onesb])
    P.op("sp", lambda e: e.dma_start(out=sc[:], in_=sc2T_d), writes=[scb], dma=True)
    P.op("act", lambda e: e.activation(out=sc[:], in_=sc[:], func=AF.Silu), reads=[scb], writes=[scb])
    modrow = P.sb("modrow", [2, NMOD * D], F32)
    mrb = [P.buf() for i in range(36)]
    wt = [P.sb("wt", [128, KC, 512], F32) for i in range(2)]
    wtb = [[P.buf(), P.buf()] for i in range(2)]
    bt = [P.sb("bt", [1, 512], F32) for i in range(2)]
    btb = [P.buf() for i in range(2)]
    n = 0
    for l in range(depth):
        for blk in range(36):
            c0 = blk * 512
            i = n % 2
            n += 1
            for half in range(2):
                P.op("sp" if half == 0 else "act", lambda e, i=i, l=l, c0=c0, half=half: e.dma_start(
                    out=wt[i][:, half * 8:half * 8 + 8, :],
                    in_=wm_d[l].rearrange("(c p) n -> p c n", p=128)[:, half * 8:half * 8 + 8, c0:c0 + 512]),
                    writes=[wtb[i][half]], dma=True)
            P.op("sp", lambda e, i=i, l=l, c0=c0: e.dma_start(out=bt[i][:, :], in_=bm_d[l:l + 1, c0:c0 + 512]),
                 writes=[btb[i]], dma=True)
            bk = i
            for kc in range(KC):
                P.op("pe", lambda e, i=i, kc=kc, bk=bk: e.matmul(
                    R.banks[bk][:2, :], sc[:, kc * 2:(kc + 1) * 2], wt[i][:, kc, :], start=(kc == 0), stop=False),
                    reads=[wtb[i][kc // 8], scb], writes=[R.bank_b[bk]])
            P.op("pe", lambda e, i=i, bk=bk: e.matmul(R.banks[bk][:2, :], ones1[:], bt[i][:, :], start=False, stop=True),
                 reads=[btb[i], onesb], writes=[R.bank_b[bk]])
            P.op("act", lambda e, bk=bk, c0=c0: e.activation(out=modrow[:, c0:c0 + 512], in_=R.banks[bk][:2, :],
                                                           func=AF.Identity),
                 reads=[R.bank_b[bk]], writes=[mrb[blk]])
        for q in range(NMOD * KC):
            P.op("pe", lambda e, q=q: e.transpose(out=R.banks[2][:, q * 2:(q + 1) * 2],
                                                 in_=modrow[0:2, q * 128:(q + 1) * 128], identity=R.identf[0:2, 0:2]),
                 reads=[mrb[q // 4], R.persist_b], writes=[R.bank_b[2]])
        P.op("act", lambda e, l=l: e.activation(
            out=R.mvT[:, l].rearrange("p a b -> p (a b)"), in_=R.banks[2][:, 0:NMOD * KC * 2], func=AF.Identity),
            reads=[R.bank_b[2]], writes=[R.persist_b])
    P.barrier()


def emit_glaproj_phase(P, R, Hs, win, wgu_d, bgrow_d, QT, KT, V, RS, GA, GB):
    W = 512
    h = P.sb("h", [128, KC, W], BF16)
    hb = [P.buf() for c in range(KC)]
    wt = [P.sb("wt", [128, KC, 128], BF16) for i in range(3)]
    wtb = [P.buf() for i in range(3)]
    wbig = [P.sb("wbig", [128, KC, 512], BF16) for i in range(2)]
    wbigb = [P.buf() for i in range(2)]
    wz = P.sb("wz", [128, KC, 32], BF16)
    wzb = P.buf()
    P.op("pool", lambda e: e.dma_start(out=wz[:], in_=chunked(win)[:, :, 6144:6176]), writes=[wzb], dma=True)
    wg = P.sb("wg", [GLA_R, 2 * GLA_HK], F32)
    wgb = P.buf()
    P.op("sp", lambda e: e.dma_start(out=wg[:], in_=wgu_d), writes=[wgb], dma=True)
    bgr = P.sb("bgr", [1, 2 * GLA_HK], F32)
    bgb = P.buf()
    P.op("sp", lambda e: e.dma_start(out=bgr[:], in_=bgrow_d), writes=[bgb], dma=True)
    onesr = P.sb("onesr", [1, 128], F32)
    onesrb = P.buf()
    P.op("pool", lambda e: e.memset(onesr[:], 1.0), writes=[onesrb])
    z = [P.sb("z", [GLA_R, W], F32) for i in range(2)]
    zb = [P.buf() for i in range(2)]
    rot_o = Rot(P, "ot", [128, 512], F32, 3)
    rot_s = Rot(P, "sgx", [128, 512], F32, 2)
    n = {"w": 0, "b": 0, "wb": 0}
    G = [GA, GB]
    for (t0, w, _) in FT_LC:
        for half in range(2):
            cs = slice(half * 8, half * 8 + 8)
            P.op("pool", lambda e, cs=cs, t0=t0, w=w: e.dma_start(
                out=h[:, cs, :w], in_=chunked(Hs)[:, cs, t0:t0 + w]), writes=hb[cs], dma=True)
        for mc in range(16):
            i = n["w"] % 3
            n["w"] += 1
            P.op("pool", lambda e, i=i, mc=mc: e.dma_start(
                out=wt[i][:], in_=chunked(win)[:, :, mc * 128:(mc + 1) * 128]), writes=[wtb[i]], dma=True)
            bk = n["b"] % 4
            n["b"] += 1
            for kc in range(KC):
                P.op("pe", lambda e, i=i, kc=kc, bk=bk, w=w: e.matmul(
                    R.banks[bk][:, :w], wt[i][:, kc, :], h[:, kc, :w], start=(kc == 0), stop=(kc == KC - 1)),
                    reads=[wtb[i], hb[kc]], writes=[R.bank_b[bk]])
            ot, otb = rot_o.next()
            if mc < 8:
                P.op("act", lambda e, ot=ot, bk=bk, w=w: e.activation(
                    out=ot[:, :w], in_=R.banks[bk][:, :w], func=AF.Identity, scale=1.0 / 16.0),
                    reads=[R.bank_b[bk]], writes=[otb])
                dst = QT[mc * 128:(mc + 1) * 128, t0:t0 + w]
            else:
                P.op("dve", lambda e, ot=ot, bk=bk, w=w: e.tensor_copy(out=ot[:, :w], in_=R.banks[bk][:, :w]),
                     reads=[R.bank_b[bk]], writes=[otb])
                dst = KT[(mc - 8) * 128:(mc - 7) * 128, t0:t0 + w]
            P.op("sp", lambda e, ot=ot, dst=dst, w=w: e.dma_start(out=dst, in_=ot[:, :w]), reads=[otb], dma=True)
        for d in range(2):
            bk = 4 + d
            for kc in range(KC):
                P.op("pe", lambda e, d=d, kc=kc, bk=bk, w=w: e.matmul(
                    R.banks[bk][:GLA_R, :w], wz[:, kc, d * 16:(d + 1) * 16], h[:, kc, :w],
                    start=(kc == 0), stop=(kc == KC - 1)), reads=[wzb, hb[kc]], writes=[R.bank_b[bk]])
            P.op("dve", lambda e, d=d, bk=bk, w=w: e.tensor_copy(out=z[d][:, :w], in_=R.banks[bk][:GLA_R, :w]),
                 reads=[R.bank_b[bk]], writes=[zb[d]])
        for tc in range(w // 128):
            r0 = t0 + tc * 128
            for d in range(2):
                for cb in range(2):
                    bk = 6 + (cb % 2)
                    c0 = d * GLA_HK + cb * 512
                    P.op("pe", lambda e, d=d, tc=tc, bk=bk, c0=c0: e.matmul(
                        R.banks[bk][:, :], z[d][:, tc * 128:(tc + 1) * 128], wg[:, c0:c0 + 512],
                        start=True, stop=False), reads=[zb[d], wgb], writes=[R.bank_b[bk]])
                    P.op("pe", lambda e, bk=bk, c0=c0: e.matmul(
                        R.banks[bk][:, :], onesr[:], bgr[:, c0:c0 + 512], start=False, stop=True),
                        reads=[onesrb, bgb], writes=[R.bank_b[bk]])
                    sg_, sgb_ = rot_s.next()
                    ot, otb = rot_o.next()
                    P.op("act", lambda e, sg_=sg_, bk=bk: e.activation(out=sg_[:], in_=R.banks[bk][:, :], func=AF.Sigmoid),
                         reads=[R.bank_b[bk]], writes=[sgb_])
                    P.op("act", lambda e, sg_=sg_, ot=ot: e.activation(out=ot[:], in_=sg_[:], func=AF.Ln),
                         reads=[sgb_], writes=[otb])
                    P.op("sp", lambda e, ot=ot, d=d, r0=r0, cb=cb: e.dma_start(
                        out=G[d][r0:r0 + 128, cb * 512:(cb + 1) * 512], in_=ot[:]), reads=[otb], dma=True)
        for cb in range(8):
            i = n["wb"] % 2
            n["wb"] += 1
            P.op("pool", lambda e, i=i, cb=cb: e.dma_start(
                out=wbig[i][:], in_=chunked(win)[:, :, 2048 + cb * 512:2048 + (cb + 1) * 512]),
                writes=[wbigb[i]], dma=True)
            for tc in range(w // 128):
                r0 = t0 + tc * 128
                bk = n["b"] % 4
                n["b"] += 1
                for kc in range(KC):
                    P.op("pe", lambda e, i=i, kc=kc, bk=bk, tc=tc: e.matmul(
                        R.banks[bk][:, :], h[:, kc, tc * 128:(tc + 1) * 128], wbig[i][:, kc, :],
                        start=(kc == 0), stop=(kc == KC - 1)), reads=[wbigb[i], hb[kc]], writes=[R.bank_b[bk]])
                ot, otb = rot_o.next()
                if cb < 4:
                    P.op("dve", lambda e, ot=ot, bk=bk: e.tensor_copy(out=ot[:], in_=R.banks[bk][:, :]),
                         reads=[R.bank_b[bk]], writes=[otb])
                    dst = V[r0:r0 + 128, cb * 512:(cb + 1) * 512]
                else:
                    P.op("act", lambda e, ot=ot, bk=bk: e.activation(out=ot[:], in_=R.banks[bk][:, :], func=AF.Silu),
                         reads=[R.bank_b[bk]], writes=[otb])
                    dst = RS[r0:r0 + 128, (cb - 4) * 512:(cb - 3) * 512]
                P.op("sp", lambda e, ot=ot, dst=dst: e.dma_start(out=dst, in_=ot[:]), reads=[otb], dma=True)
    P.barrier()


def emit_scan_phase(P, R, QT, KT, V, GA, GB, OA, OB, cc_in, cc_out, emit_ctx):
    pairs = [[0, 1], [2, 3], [4, 5], [6, 7]]
    bankB = R.banks[0][:, 0:256].rearrange("p (a b) -> p a b", b=128); bBb = R.bank_b[0]
    bankA = R.banks[1][:, 0:128]; bAb = R.bank_b[1]
    bankO = [R.banks[2], R.banks[3]]; bOb = [R.bank_b[2], R.bank_b[3]]
    bankT = R.banks[4][:, 0:128].bitcast(BF16).rearrange("p (a b) -> p a b", b=128); bTb = R.bank_b[4]
    bankU = [R.banks[5], R.banks[6]]; bUb = [R.bank_b[5], R.bank_b[6]]
    ident = R.ident
    Sst = P.sb("Sst", [128, 2, 512], F32); Sb = [P.buf() for j in range(2)]
    Sp = P.sb("Sp", [128, 2, 512], BF16); Spb = [P.buf() for j in range(2)]
    S0 = P.sb("S0", [128, 2, 512], F32); S0b = P.buf()
    S1 = P.sb("S1", [128, 2, 512], F32); S1b = P.buf()
    rq = Rot(P, "q", [128, 2, 128], F32, 3)
    rk = Rot(P, "k", [128, 2, 128], F32, 3)
    rv = Rot(P, "v", [128, 512], BF16, 3)
    rg = Rot(P, "g", [128, 256], F32, 3)
    rB = Rot(P, "Bsb", [128, 2, 128], F32, 2)
    rs = Rot(P, "sc", [128, 5, 2], F32, 2)
    rEq = Rot(P, "Eq", [128, 2, 128], F32, 2)
    rEk = Rot(P, "Ek", [128, 2, 128], F32, 2)
    rqi = Rot(P, "qi", [128, 2, 128], BF16, 2)
    rki = Rot(P, "ki", [128, 2, 128], BF16, 2)
    ram = Rot(P, "am", [128, 128], BF16, 2)
    rkit = Rot(P, "kit", [128, 2, 128], BF16, 2)
    ro = Rot(P, "osb", [128, 512], F32, 2)
    rtu = Rot(P, "tu", [128, 512], F32, 2)
    st = {"no": 0}

    def chunk(hd, n, tri, mask, ref, last, G, O):
        ts = slice(n * 128, (n + 1) * 128)
        q, qb = rq.next(); k, kb = rk.next(); v, vb = rv.next(); g, gb = rg.next()
        P.op("sp", lambda e: e.dma_start(
            out=q[:], in_=QT[hd * 256:(hd + 1) * 256, :].rearrange("(j p) t -> p j t", p=128)[:, :, ts]),
            writes=[qb], dma=True)
        P.op("sp", lambda e: e.dma_start(
            out=k[:], in_=KT[hd * 256:(hd + 1) * 256, :].rearrange("(j p) t -> p j t", p=128)[:, :, ts]),
            writes=[kb], dma=True)
        P.op("pool", lambda e: e.dma_start(out=v[:], in_=V[ts, hd * 512:(hd + 1) * 512]), writes=[vb], dma=True)
        P.op("act", lambda e: e.dma_start(out=g[:], in_=G[ts, hd * 256:(hd + 1) * 256]), writes=[gb], dma=True)
        for j in range(2):
            P.op("pe", lambda e, j=j: e.matmul(bankB[:, j, :], g[:, j * 128:(j + 1) * 128], tri[:],
                                               start=True, stop=True), reads=[gb, R.persist_b], writes=[bBb])
        B, Bb = rB.next()
        P.op("act", lambda e: e.activation(out=B[:], in_=bankB[:], func=AF.Identity), reads=[bBb], writes=[Bb])
        sc, scb = rs.next()
        P.op("dve", lambda e: e.tensor_scalar(out=sc[:, 0, :], in0=B[:, :, ref], scalar1=-1.0, scalar2=None,
                                              op0=ALU.mult), reads=[Bb], writes=[scb])
        P.op("dve", lambda e: e.tensor_tensor(out=sc[:, 1, :], in0=B[:, :, last], in1=sc[:, 0, :], op=ALU.add),
             reads=[Bb, scb], writes=[scb])
        P.op("act", lambda e: e.activation(out=sc[:, 2, :], in_=B[:, :, ref], func=AF.Exp),
             reads=[Bb, scb], writes=[scb])
        P.op("act", lambda e: e.activation(out=sc[:, 3, :], in_=B[:, :, last], func=AF.Exp),
             reads=[Bb, scb], writes=[scb])
        P.op("act", lambda e: e.activation(out=sc[:, 4, :], in_=sc[:, 1, :], func=AF.Exp),
             reads=[scb], writes=[scb])
        Eq, Eqb = rEq.next(); Ek, Ekb = rEk.next()
        for j in range(2):
            P.op("act", lambda e, j=j: e.activation(out=Eq[:, j, :], in_=B[:, j, :], func=AF.Exp,
                                                    bias=sc[:, 0, j:j + 1], scale=1.0),
                 reads=[Bb, scb], writes=[Eqb])
            P.op("act", lambda e, j=j: e.activation(out=Ek[:, j, :], in_=B[:, j, :], func=AF.Exp,
                                                    bias=B[:, j, ref:ref + 1], scale=-1.0),
                 reads=[Bb], writes=[Ekb])
        qi, qib = rqi.next(); ki, kib = rki.next()
        P.op("dve", lambda e: e.tensor_tensor(out=qi[:], in0=q[:], in1=Eq[:], op=ALU.mult),
             reads=[qb, Eqb], writes=[qib])
        P.op("dve", lambda e: e.tensor_tensor(out=ki[:], in0=k[:], in1=Ek[:], op=ALU.mult),
             reads=[kb, Ekb], writes=[kib])
        for j in range(2):
            P.op("act", lambda e, j=j: e.activation(out=Sp[:, j, :], in_=Sst[:, j, :], func=AF.Identity,
                                                    scale=sc[:, 2, j:j + 1]),
                 reads=[Sb[j], scb], writes=[Spb[j]])
        for j in range(2):
            P.op("pe", lambda e, j=j: e.matmul(bankA, ki[:, j, :], qi[:, j, :], start=(j == 0), stop=(j == 1)),
                 reads=[kib, qib], writes=[bAb])
        am, amb = ram.next()
        P.op("dve", lambda e: e.tensor_tensor(out=am[:], in0=bankA, in1=mask[:], op=ALU.mult),
             reads=[bAb, R.persist_b], writes=[amb])
        oi = st["no"] % 2
        st["no"] += 1
        P.op("pe", lambda e: e.matmul(bankO[oi][:], am[:], v[:], start=True, stop=False),
             reads=[amb, vb], writes=[bOb[oi]])
        for j in range(2):
            P.op("pe", lambda e, j=j: e.matmul(bankO[oi][:], qi[:, j, :], Sp[:, j, :], start=False, stop=(j == 1)),
                 reads=[qib, Spb[j]], writes=[bOb[oi]])
        osb, osbb = ro.next()
        P.op("act", lambda e: e.activation(out=osb[:], in_=bankO[oi][:], func=AF.Identity),
             reads=[bOb[oi]], writes=[osbb])
        P.op("sp", lambda e: e.dma_start(out=O[ts, hd * 512:(hd + 1) * 512], in_=osb[:]), reads=[osbb], dma=True)
        for j in range(2):
            P.op("pe", lambda e, j=j: e.transpose(out=bankT[:, j, :], in_=ki[:, j, :], identity=ident[:]),
                 reads=[kib, R.persist_b], writes=[bTb])
        kit, kitb = rkit.next()
        P.op("dve", lambda e: e.tensor_copy(out=kit[:], in_=bankT[:]), reads=[bTb], writes=[kitb])
        for j in range(2):
            P.op("pe", lambda e, j=j: e.matmul(bankU[j][:], kit[:, j, :], v[:], start=True, stop=True),
                 reads=[kitb, vb], writes=[bUb[j]])
            tu, tub = rtu.next()
            P.op("act", lambda e, j=j, tu=tu: e.activation(out=tu[:], in_=bankU[j][:], func=AF.Identity,
                                                           scale=sc[:, 4, j:j + 1]),
                 reads=[bUb[j], scb], writes=[tub])
            P.op("dve", lambda e, j=j, tu=tu: e.scalar_tensor_tensor(
                out=Sst[:, j, :], in0=Sst[:, j, :], scalar=sc[:, 3, j:j + 1], in1=tu[:],
                op0=ALU.mult, op1=ALU.add), reads=[Sb[j], tub, scb], writes=[Sb[j]])

    def zero_state():
        for j in range(2):
            P.op("dve", lambda e, j=j: e.memset(Sst[:, j, :], 0.0), writes=[Sb[j]])

    orderA = [16, 17] + list(range(16))
    cc_writes = []
    for hd in range(4):
        zero_state()
        for n in orderA:
            chunk(hd, n, R.triA, R.maskA, 63, 127, GA, OA)
        cc_writes.append(P.op("sp", lambda e, hd=hd: e.dma_start(
            out=cc_in[hd * 256:(hd + 1) * 256, :].rearrange("(j p) n -> p j n", p=128), in_=Sst[:]),
            reads=Sb, dma=True))
    cc = P.cc_op(lambda e: e.collective_compute("AllGather", ALU.bypass, replica_groups=pairs,
                                                ins=[cc_in], outs=[cc_out]), extra=cc_writes)
    if emit_ctx:
        for hd in range(4):
            zero_state()
            for n in (17, 16):
                chunk(hd, n, R.triB, R.maskB, 64, 0, GB, OB)
    for hd in range(4):
        P.op("sp", lambda e, hd=hd: e.dma_start(
            out=S0[:], in_=cc_out[hd * 256:(hd + 1) * 256, :].rearrange("(j p) n -> p j n", p=128)),
            writes=[S0b], dma=True, extra=[cc])
        P.op("sp", lambda e, hd=hd: e.dma_start(
            out=S1[:], in_=cc_out[1024 + hd * 256:1024 + (hd + 1) * 256, :].rearrange("(j p) n -> p j n", p=128)),
            writes=[S1b], dma=True, extra=[cc])
        for j in range(2):
            P.op("dve", lambda e, j=j: e.tensor_scalar(out=Sst[:, j, :], in0=S0[:, j, :], scalar1=R.sel[:, 0:1],
                                                       scalar2=None, op0=ALU.mult),
                 reads=[S0b, R.persist_b], writes=[Sb[j]])
            P.op("dve", lambda e, j=j: e.scalar_tensor_tensor(
                out=Sst[:, j, :], in0=S1[:, j, :], scalar=R.sel[:, 1:2], in1=Sst[:, j, :],
                op0=ALU.mult, op1=ALU.add), reads=[S1b, Sb[j], R.persist_b], writes=[Sb[j]])
        for n in range(15, -1, -1):
            chunk(hd, n, R.triB, R.maskB, 64, 0, GB, OB)
    P.barrier()


def emit_fnet_phase(P, R, Hs, Ms, hcc_in, hcc_out, cw_d, cl_l, sl_l, cl_c, sl_c):
    pairs = [[0, 1], [2, 3], [4, 5], [6, 7]]
    ccs = []
    for g in range(NG):
        stg = P.op("sp", lambda e, g=g: e.dma_start(
            out=hcc_in[g], in_=Hs[g * 256:(g + 1) * 256, 0:HALF]), dma=True)
        ccs.append(P.cc_op(lambda e, g=g: e.collective_compute(
            "AllGather", ALU.bypass, replica_groups=pairs, ins=[hcc_in[g]], outs=[hcc_out[g]]), extra=[stg]))
    banks, bkb = R.banks, R.bank_b
    cw = P.sb("cw_sb", [128, 2, 512], BF16)
    cwb = P.buf()
    P.op("sp", lambda e: e.dma_start(out=cw[:].rearrange("p a b -> p (a b)"), in_=cw_d), writes=[cwb], dma=True)
    LMAX = SEQ
    hg = [P.sb("hg", [128, 2, LMAX], BF16) for i in range(2)]
    hgb = [[P.buf(), P.buf()] for i in range(2)]
    A = P.sb("A", [128, LMAX // 128, 512], BF16)
    Ab = [P.buf() for i in range(LMAX // 128)]
    NB = 3
    clt = [P.sb("clt", [128, 8, 512], BF16) for i in range(NB)]
    cltb = [P.buf() for i in range(NB)]
    slt = [P.sb("slt", [128, 8, 512], BF16) for i in range(NB)]
    sltb = [P.buf() for i in range(NB)]
    fo = [P.sb("fo", [128, 512], F32) for i in range(4)]
    fob = [P.buf() for i in range(4)]
    n = {"hg": 0, "m": 0, "fo": 0, "a": 0}
    for (name, L, NK, cl, sl, col0) in (("l", SEQ, HALF, cl_l, sl_l, 0), ("c", CTX, CTX, cl_c, sl_c, HALF)):
        NCH = L // 128
        for g in range(NG):
            gi = n["hg"] % 2
            n["hg"] += 1
            if name == "l":
                for rnk in range(2):
                    P.op("pool", lambda e, gi=gi, g=g, rnk=rnk: e.dma_start(
                        out=hg[gi][:, :, rnk * HALF:(rnk + 1) * HALF],
                        in_=hcc_out[g][rnk * 256:(rnk + 1) * 256, :].rearrange("(c p) t -> p c t", p=128)),
                        writes=[hgb[gi][rnk]], dma=True, extra=[ccs[g]])
            else:
                P.op("pool", lambda e, gi=gi, g=g, L=L: e.dma_start(
                    out=hg[gi][:, :, :L], in_=chunked(Hs)[:, 2 * g:2 * g + 2, HALF:HALF + L]),
                    writes=hgb[gi], dma=True)
            for nch in range(NCH):
                bk = n["a"] % 2
                n["a"] += 1
                for kc in range(2):
                    P.op("pe", lambda e, gi=gi, nch=nch, kc=kc, bk=bk: e.matmul(
                        banks[bk][:, :], hg[gi][:, kc, nch * 128:(nch + 1) * 128], cw[:, kc, :],
                        start=(kc == 0), stop=(kc == 1)), reads=[hgb[gi][nch // 16], cwb], writes=[bkb[bk]])
                if nch % 2 == 0:
                    P.op("act", lambda e, nch=nch, bk=bk: e.activation(out=A[:, nch, :], in_=banks[bk][:, :],
                                                                      func=AF.Identity),
                         reads=[bkb[bk]], writes=[Ab[nch]])
                else:
                    P.op("dve", lambda e, nch=nch, bk=bk: e.tensor_copy(out=A[:, nch, :], in_=banks[bk][:, :]),
                         reads=[bkb[bk]], writes=[Ab[nch]])
            kblocks = [(k0, min(512, NK - k0)) for k0 in range(0, NK, 512)]
            for (k0, kw) in kblocks:
                pb = [2 + 2 * (n["fo"] % 2), 3 + 2 * (n["fo"] % 2)]
                nsub = (NCH + 7) // 8
                for sb_ in range(nsub):
                    r0 = sb_ * 8
                    rn = min(8, NCH - r0)
                    mi = n["m"] % NB
                    n["m"] += 1
                    P.op("sp", lambda e, mi=mi, r0=r0, rn=rn, k0=k0, kw=kw, cl=cl: e.dma_start(
                        out=clt[mi][:, :rn, :kw],
                        in_=cl[r0 * 128:(r0 + rn) * 128, k0:k0 + kw].rearrange("(c p) n -> p c n", p=128)),
                        writes=[cltb[mi]], dma=True)
                    P.op("act", lambda e, mi=mi, r0=r0, rn=rn, k0=k0, kw=kw, sl=sl: e.dma_start(
                        out=slt[mi][:, :rn, :kw],
                        in_=sl[r0 * 128:(r0 + rn) * 128, k0:k0 + kw].rearrange("(c p) n -> p c n", p=128)),
                        writes=[sltb[mi]], dma=True)
                    for r in range(rn):
                        nch = r0 + r
                        for mcx in range(2):
                            P.op("pe", lambda e, mi=mi, r=r, nch=nch, mcx=mcx, kw=kw, pb=pb: e.matmul(
                                banks[pb[mcx]][:, :kw], A[:, nch, mcx * 128:(mcx + 1) * 128], clt[mi][:, r, :kw],
                                start=(nch == 0), stop=False), reads=[Ab[nch], cltb[mi]], writes=[bkb[pb[mcx]]])
                            P.op("pe", lambda e, mi=mi, r=r, nch=nch, mcx=mcx, kw=kw, pb=pb, NCH=NCH: e.matmul(
                                banks[pb[mcx]][:, :kw], A[:, nch, 256 + mcx * 128:256 + (mcx + 1) * 128],
                                slt[mi][:, r, :kw], start=False, stop=(nch == NCH - 1)),
                                reads=[Ab[nch], sltb[mi]], writes=[bkb[pb[mcx]]])
                for mcx in range(2):
                    fi = n["fo"] % 2
                    P.op("act", lambda e, fi=fi, mcx=mcx, kw=kw, pb=pb: e.activation(
                        out=fo2(fo, fi, mcx)[:, :kw], in_=banks[pb[mcx]][:, :kw], func=AF.Identity),
                        reads=[bkb[pb[mcx]]], writes=[fob2(fob, fi, mcx)])
                    c = 2 * g + mcx
                    P.op("sp", lambda e, fi=fi, mcx=mcx, c=c, k0=k0, kw=kw, col0=col0: e.dma_start(
                        out=Ms[c * 128:(c + 1) * 128, col0 + k0:col0 + k0 + kw], in_=fo2(fo, fi, mcx)[:, :kw]),
                        reads=[fob2(fob, fi, mcx)], dma=True)
                n["fo"] += 1
    P.barrier()


def emit_conv_phase(P, R, Hs, Ms, w1, vecs_d, wdw_d):
    W = 512
    fused_common(P, R)
    C = R
    tiles = [(t0, w, 64) for t0, w, _ in FT_L] + [(HALF, CTX, CTX)]
    vv = P.sb("vv", [128, 5 * KC], F32)
    vvb = P.buf()
    P.op("sp", lambda e: e.dma_start(out=vv[:], in_=vecs_d), writes=[vvb], dma=True)
    wk = P.sb("wk", [128, KC * CONV_W], F32)
    wkb = P.buf()
    P.op("sp", lambda e: e.dma_start(out=wk[:], in_=wdw_d), writes=[wkb], dma=True)

    def vec(k):
        return vv[:, k * KC:(k + 1) * KC]

    h = P.sb("h", [128, KC, W], BF16)
    hb = [P.buf() for c in range(KC)]
    y = P.sb("y", [128, KC, W], F32)
    yb = [P.buf() for c in range(KC)]
    u = [P.sb("u", [128, W], F32) for i in range(2)]
    ub = [P.buf() for i in range(2)]
    sgm = [P.sb("sgm", [128, W], F32) for i in range(2)]
    sgmb = [P.buf() for i in range(2)]
    wa = [P.sb("wa", [128, KC, 128], BF16) for i in range(2)]
    wab = [P.buf() for i in range(2)]
    wgx = [P.sb("wgx", [128, KC, 128], BF16) for i in range(2)]
    wgxb = [P.buf() for i in range(2)]
    mean = P.sb("mean", [128, W], F32)
    meanb = P.buf()
    var = P.sb("var", [128, W], F32)
    varb = P.buf()
    ho = [P.sb("ho", [128, W], F32) for i in range(2)]
    hob = [P.buf() for i in range(2)]
    n = {"w": 0, "u": 0, "ho": 0}
    for (t0, w, L) in tiles:
        for half in range(2):
            cs = slice(half * 8, half * 8 + 8)
            P.op("pool", lambda e, cs=cs, t0=t0, w=w: e.dma_start(
                out=h[:, cs, :w], in_=chunked(Hs)[:, cs, t0:t0 + w]), writes=hb[cs], dma=True)
        for mc in range(KC):
            i = n["w"] % 2
            n["w"] += 1
            P.op("pool", lambda e, i=i, mc=mc: e.dma_start(
                out=wa[i][:], in_=chunked(w1)[:, :, mc * 128:(mc + 1) * 128]), writes=[wab[i]], dma=True)
            P.op("pool", lambda e, i=i, mc=mc: e.dma_start(
                out=wgx[i][:], in_=chunked(w1)[:, :, D + mc * 128:D + (mc + 1) * 128]), writes=[wgxb[i]], dma=True)
            par = mc % 2
            ba, bg = 2 + 2 * par, 3 + 2 * par
            for kc in range(KC):
                P.op("pe", lambda e, i=i, kc=kc, ba=ba, w=w: e.matmul(
                    C.banks[ba][:, :w], wa[i][:, kc, :], h[:, kc, :w], start=(kc == 0), stop=(kc == KC - 1)),
                    reads=[wab[i], hb[kc]], writes=[C.bank_b[ba]])
            for kc in range(KC):
                P.op("pe", lambda e, i=i, kc=kc, bg=bg, w=w: e.matmul(
                    C.banks[bg][:, :w], wgx[i][:, kc, :], h[:, kc, :w], start=(kc == 0), stop=(kc == KC - 1)),
                    reads=[wgxb[i], hb[kc]], writes=[C.bank_b[bg]])
            ui = n["u"] % 2
            n["u"] += 1
            P.op("act", lambda e, ui=ui, bg=bg, mc=mc, w=w: e.activation(
                out=sgm[ui][:, :w], in_=C.banks[bg][:, :w], func=AF.Sigmoid, bias=vec(1)[:, mc:mc + 1], scale=1.0),
                reads=[C.bank_b[bg], vvb], writes=[sgmb[ui]])
            P.op("dve", lambda e, ui=ui, ba=ba, mc=mc, w=w: e.scalar_tensor_tensor(
                out=u[ui][:, :w], in0=C.banks[ba][:, :w], scalar=vec(0)[:, mc:mc + 1], in1=sgm[ui][:, :w],
                op0=ALU.add, op1=ALU.mult), reads=[C.bank_b[ba], sgmb[ui], vvb], writes=[ub[ui]])
            u3 = u[ui][:, :w].rearrange("p (s l) -> p s l", l=L)
            y3 = y[:, mc, :w].rearrange("p (s l) -> p s l", l=L)
            P.op("dve", lambda e, ui=ui, mc=mc, w=w: e.tensor_scalar(
                out=y[:, mc, :w], in0=u[ui][:, :w], scalar1=wk[:, mc * CONV_W + 15:mc * CONV_W + 16],
                scalar2=vec(2)[:, mc:mc + 1], op0=ALU.mult, op1=ALU.add),
                reads=[ub[ui], wkb, vvb], writes=[yb[mc]])
            for k in range(CONV_W):
                o = k - 15
                if o == 0 or abs(o) >= L:
                    continue
                a0, a1 = max(0, -o), min(L, L - o)
                P.op("dve", lambda e, u3=u3, y3=y3, mc=mc, k=k, a0=a0, a1=a1, o=o: e.scalar_tensor_tensor(
                    out=y3[:, :, a0:a1], in0=u3[:, :, a0 + o:a1 + o],
                    scalar=wk[:, mc * CONV_W + k:mc * CONV_W + k + 1], in1=y3[:, :, a0:a1],
                    op0=ALU.mult, op1=ALU.add), reads=[ub[ui], wkb, yb[mc]], writes=[yb[mc]])
            P.op("pe", lambda e, mc=mc, w=w: e.matmul(C.banks[0][:, :w], C.ones[:], y[:, mc, :w],
                                                     start=(mc == 0), stop=(mc == KC - 1)),
                 reads=[yb[mc], C.ones_b], writes=[C.bank_b[0]])
            si = C.n_sq % 2
            C.n_sq += 1
            P.op("act", lambda e, mc=mc, si=si, w=w: e.activation(out=C.sq[si][:, :w], in_=y[:, mc, :w], func=AF.Square),
                 reads=[yb[mc]], writes=[C.sq_b[si]])
            P.op("pe", lambda e, mc=mc, si=si, w=w: e.matmul(C.banks[1][:, :w], C.ones[:], C.sq[si][:, :w],
                                                            start=(mc == 0), stop=(mc == KC - 1)),
                 reads=[C.sq_b[si], C.ones_b], writes=[C.bank_b[1]])
        P.op("act", lambda e, w=w: e.activation(out=mean[:, :w], in_=C.banks[0][:, :w], func=AF.Identity,
                                                scale=1.0 / D), reads=[C.bank_b[0]], writes=[meanb])
        P.op("dve", lambda e, w=w: e.tensor_tensor(out=var[:, :w], in0=mean[:, :w], in1=mean[:, :w], op=ALU.mult),
             reads=[meanb], writes=[varb])
        P.op("dve", lambda e, w=w: e.scalar_tensor_tensor(
            out=var[:, :w], in0=C.banks[1][:, :w], scalar=1.0 / D, in1=var[:, :w],
            op0=ALU.mult, op1=ALU.subtract), reads=[C.bank_b[1], varb], writes=[varb])
        P.op("act", lambda e, w=w: e.activation(out=C.rstd[:, :w], in_=var[:, :w], func=AF.Sqrt,
                                                bias=C.epsb[:, 0:1], scale=1.0),
             reads=[varb, C.eps_bb], writes=[C.rstd_b])
        P.op("dve", lambda e, w=w: e.reciprocal(out=C.rstd[:, :w], in_=C.rstd[:, :w]),
             reads=[C.rstd_b], writes=[C.rstd_b])
        for c in range(KC):
            ti = C.n_tmp % 2
            C.n_tmp += 1
            i = n["ho"] % 2
            n["ho"] += 1
            P.op("dve", lambda e, c=c, ti=ti, w=w: e.tensor_tensor(
                out=C.tmp[ti][:, :w], in0=y[:, c, :w], in1=mean[:, :w], op=ALU.subtract),
                reads=[yb[c], meanb], writes=[C.tmp_b[ti]])
            P.op("dve", lambda e, c=c, ti=ti, w=w: e.scalar_tensor_tensor(
                out=C.tmp[ti][:, :w], in0=C.tmp[ti][:, :w], scalar=vec(3)[:, c:c + 1], in1=C.rstd[:, :w],
                op0=ALU.mult, op1=ALU.mult), reads=[C.tmp_b[ti], C.rstd_b, vvb], writes=[C.tmp_b[ti]])
            P.op("act", lambda e, c=c, ti=ti, i=i, w=w: e.activation(
                out=ho[i][:, :w], in_=C.tmp[ti][:, :w], func=AF.Silu, bias=vec(4)[:, c:c + 1], scale=1.0),
                reads=[C.tmp_b[ti], vvb], writes=[hob[i]])
            P.op("sp", lambda e, c=c, i=i, t0=t0, w=w: e.dma_start(
                out=Ms[c * 128:(c + 1) * 128, t0:t0 + w], in_=ho[i][:, :w]), reads=[hob[i]], dma=True)
    P.barrier()


def build_fused(depth=DEPTH, dbg=False):
    nc = bass.Bass("TRN2", target_bir_lowering=False)

    def din(name, shape, dt=F32):
        return nc.dram_tensor(name, list(shape), dt, kind="ExternalInput").ap()

    def dscr(name, shape, dt=F32):
        return nc.dram_tensor(name, list(shape), dt, kind="Internal").ap()

    xT_in = din("xT_in", [D, TT])
    sc2T = din("sc2T", [128, KC * 2])
    wm = din("wm", [DEPTH, D, NMOD * D])
    bm = din("bm", [DEPTH, NMOD * D])
    ngf_d = din("ngf", [128, DEPTH * 3 * KC])
    fing_d = din("fing", [128, KC])
    wg = din("wg", [DEPTH, 2, D, DFF])
    wu = din("wu", [DEPTH, 2, D, DFF])
    wd = din("wd", [DEPTH, 2, DFF, D])
    win = din("win", [2, D, GLA_IN])
    wgu = din("wgu", [2, GLA_R, 2 * GLA_HK])
    bgrow = din("bgrow", [2, 1, 2 * GLA_HK])
    ghbc_d = din("ghbc", [2, 128, 512])
    wout = din("wout", [2, D, D])
    cst_d = din("cst", [128, 5 * 128])
    ident_d = din("ident", [128, 128], BF16)
    sel_d = din("sel", [128, 2])
    cw_d = din("cw", [128, 1024], BF16)
    cl_l = din("cl_l", [SEQ, HALF], BF16)
    sl_l = din("sl_l", [SEQ, HALF], BF16)
    cl_c = din("cl_c", [CTX, CTX], BF16)
    sl_c = din("sl_c", [CTX, CTX], BF16)
    fwo = din("fwo", [D, D])
    cw1 = din("cw1", [D, 2 * D])
    cvecs = din("cvecs", [128, 5 * KC])
    cwdw = din("cwdw", [128, KC * CONV_W])
    cwo = din("cwo", [D, D])
    bvec_d = din("bvec", [128, 2 * KC])
    oT = nc.dram_tensor("oT", [D, HALF], F32, kind="ExternalOutput").ap()

    X = dscr("X", [D, TT]); Hs = dscr("Hs", [D, TT]); Ms = dscr("Ms", [D, TT])
    QT = dscr("QT", [GLA_HK, TT]); KT = dscr("KT", [GLA_HK, TT])
    V = dscr("V", [TT, GLA_HV]); RS = dscr("RS", [TT, GLA_HV])
    GA = dscr("GA", [TT, GLA_HK]); GB = dscr("GB", [TT, GLA_HK])
    OA = dscr("OA", [TT, GLA_HV]); OB = dscr("OB", [TT, GLA_HV])
    ccb = [(dscr(f"cc_in{j}", [1024, 512]), dscr(f"cc_out{j}", [2048, 512])) for j in range(2)]
    hcc_in = [dscr(f"hcc_in{g}", [256, HALF]) for g in range(NG)]
    hcc_out = [dscr(f"hcc_out{g}", [512, HALF]) for g in range(NG)]
    if dbg:
        dbgX = nc.dram_tensor("dbgX", [D, TT], F32, kind="ExternalOutput").ap()

    with ExitStack() as es:
        P = Prog(nc, es)
        R = Shared()
        R.banks = [P.ps(f"bank{i}", [128, 512]) for i in range(8)]
        R.bank_b = [P.buf(f"bank{i}") for i in range(8)]
        R.persist_b = P.buf("persist")
        R.ones_b = R.persist_b
        R.eps_bb = R.persist_b
        R.ones = P.sb("ones", [128, 128], F32)
        R.epsb = P.sb("epsb", [128, 1], F32)
        zero16 = P.sb("zero16", [128, KC], F32)
        cst = P.sb("cst_sb", [128, 5 * 128], F32)
        R.triA, R.maskA, R.triB, R.maskB, R.identf = [cst[:, i * 128:(i + 1) * 128] for i in range(5)]
        R.ident = P.sb("ident_sb", [128, 128], BF16)
        R.sel = P.sb("sel_sb", [128, 2], F32)
        R.mvT = P.sb("mvT", [128, DEPTH, NMOD * KC, 2], F32)
        ngf = P.sb("ngf_sb", [128, DEPTH * 3 * KC], F32)
        fing = P.sb("fing_sb", [128, KC], F32)
        bvec = P.sb("bvec_sb", [128, 2 * KC], F32)
        ghbc = [P.sb(f"ghbc{j}", [128, 512], F32) for j in range(2)]
        R.final_dmas = []
        P.op("pool", lambda e: e.memset(R.ones[:], 1.0), writes=[R.persist_b])
        P.op("pool", lambda e: e.memset(R.epsb[:], EPS), writes=[R.persist_b])
        P.op("pool", lambda e: e.memset(zero16[:], 0.0), writes=[R.persist_b])
        for dst, src in ((cst[:], cst_d), (R.ident[:], ident_d), (R.sel[:], sel_d), (ngf[:], ngf_d),
                         (fing[:], fing_d), (bvec[:], bvec_d), (ghbc[0][:], ghbc_d[0]), (ghbc[1][:], ghbc_d[1])):
            P.op("sp", lambda e, dst=dst, src=src: e.dma_start(out=dst, in_=src), writes=[R.persist_b], dma=True)
        rem = nc.sbuf_bytes_remaining
        print("sbuf remaining", rem)
        P.arena = Arena(P, (rem - 2048) // 64 * 64)
        P.barrier()

        emit_mod_phase(P, R, sc2T, wm, bm, depth)

        def mv(i, r, k):
            return R.mvT[:, i, k * KC:(k + 1) * KC, r]

        def ng(i, k):
            return ngf[:, (i * 3 + k) * KC:(i * 3 + k + 1) * KC]

        x_src = xT_in
        for i in range(depth):
            kind, j, last = i % 3, i // 3, i == DEPTH - 1
            sets = [{"ffn": (ng(i, 0), mv(i, r, 0), mv(i, r, 1), mv(i, r, 2)),
                     "h": (ng(i, 1), mv(i, r, 3), mv(i, r, 4))} for r in range(2)]
            emit_ffn_phase(P, R, FT_LC, x_src, X, sets, (wg[i, 0], wu[i, 0], wd[i, 0]), h_dst=Hs)
            x_src = X
            if kind == 0:
                emit_glaproj_phase(P, R, Hs, win[j], wgu[j], bgrow[j], QT, KT, V, RS, GA, GB)
                emit_scan_phase(P, R, QT, KT, V, GA, GB, OA, OB, ccb[j][0], ccb[j][1], emit_ctx=not last)
                proj = dict(kind="gla", oa=OA, ob=OB, rs=RS, wo=wout[j], gh=ghbc[j])
                bias = zero16[:]
            elif kind == 1:
                emit_fnet_phase(P, R, Hs, Ms, hcc_in, hcc_out, cw_d, cl_l, sl_l, cl_c, sl_c)
                proj = dict(kind="m", m=Ms, wo=fwo)
                bias = bvec[:, 0:KC]
            else:
                emit_conv_phase(P, R, Hs, Ms, cw1, cvecs, cwdw)
                proj = dict(kind="m", m=Ms, wo=cwo)
                bias = bvec[:, KC:2 * KC]
            sets = [{"ffn": (ng(i, 2), mv(i, r, 6), mv(i, r, 7), mv(i, r, 8)),
                     "proj": (mv(i, r, 5), bias), "fin": fing[:]} for r in range(2)]
            if last:
                emit_ffn_phase(P, R, FT_L, X, X, sets[:1], (wg[i, 1], wu[i, 1], wd[i, 1]), proj=proj, final_dst=oT)
            else:
                emit_ffn_phase(P, R, FT_LC, X, X, sets, (wg[i, 1], wu[i, 1], wd[i, 1]), proj=proj)
        if depth < DEPTH or dbg:
            pass
        if dbg:
            for c4 in range(4):
                R.final_dmas.append(P.op("sp", lambda e, c4=c4: e.dma_start(
                    out=dbgX[c4 * 512:(c4 + 1) * 512, :], in_=X[c4 * 512:(c4 + 1) * 512, :]), dma=True))
        P.join("sp", R.final_dmas)
        P.emit()
    return nc


def _fused_inputs(x, c, ctx, c_ctx, w_mod, b_mod, norm_g, ffn_w_gate, ffn_w_up, ffn_w_down,
                  gla_w_in, gla_w_gate_up, gla_b_gate, gla_g_head, gla_w_out, fnet_w_out, fnet_b_out,
                  cm_w_pw1, cm_b_pw1, cm_w_dw, cm_b_dw, cm_ln_g, cm_ln_b, cm_w_pw2, cm_b_pw2, final_g):
    bf = ml_dtypes.bfloat16
    s_, t_ = np.arange(128)[:, None], np.arange(128)[None, :]
    maskA = (s_ <= t_).astype(np.float32)
    maskB = (s_ >= t_).astype(np.float32)
    cst = np.ascontiguousarray(np.concatenate(
        [maskA / 16.0, maskA, maskB / 16.0, maskB, np.eye(128, dtype=np.float32)], axis=1).astype(np.float32))
    ident = np.eye(128).astype(bf)
    m_ = np.arange(GW)
    ang = 2 * np.pi * np.outer(m_, m_) / GW
    cwf = np.concatenate([np.cos(ang), np.sin(ang)], axis=1)
    cwc = np.ascontiguousarray(cwf.reshape(2, 128, 512).transpose(1, 0, 2)).reshape(128, 1024).astype(bf)
    ngf = np.ascontiguousarray(np.concatenate([_fm(norm_g[i, k]) for i in range(DEPTH) for k in range(3)], axis=1))
    fing = _fm(final_g)
    bvec = np.ascontiguousarray(np.concatenate([_fm(fnet_b_out[0]), _fm(cm_b_pw2[0])], axis=1))
    ghbc = np.ascontiguousarray(np.broadcast_to(gla_g_head[:, None, :], (2, 128, 512)))
    cvecs = np.ascontiguousarray(np.concatenate(
        [_fm(cm_b_pw1[0][:D]), _fm(cm_b_pw1[0][D:]), _fm(cm_b_dw[0]), _fm(cm_ln_g[0]), _fm(cm_ln_b[0])], axis=1))

    def taps(wdw):
        return np.ascontiguousarray(wdw.T.reshape(KC, 128, CONV_W).transpose(1, 0, 2)).reshape(128, KC * CONV_W)

    wdw_f = [taps(cm_w_dw[0]), taps(cm_w_dw[0][::-1])]
    nrm_l = 1.0 / np.sqrt(SEQ * GW)
    nrm_c = 1.0 / np.sqrt(CTX * GW)
    n_g = np.concatenate([np.arange(HALF), SEQ - 1 - np.arange(HALF)]).astype(np.int64)
    dft_l, dft_c = [], []
    for hf in range(2):
        k_loc = (np.arange(HALF) if hf == 0 else SEQ - 1 - np.arange(HALF)).astype(np.int64)
        a = 2 * np.pi * ((n_g[:, None] * k_loc[None, :]) % SEQ).astype(np.float64) / SEQ
        dft_l.append(((np.cos(a) * nrm_l).astype(bf), (-np.sin(a) * nrm_l).astype(bf)))
        nc_ = (np.arange(CTX) if hf == 0 else CTX - 1 - np.arange(CTX)).astype(np.int64)
        a = 2 * np.pi * ((nc_[:, None] * nc_[None, :]) % CTX).astype(np.float64) / CTX
        dft_c.append(((np.cos(a) * nrm_c).astype(bf), (-np.sin(a) * nrm_c).astype(bf)))
    win_p, wgu_p, bg_p = [], [], []
    for hf in range(2):
        da, db = (0, 1) if hf == 0 else (1, 0)
        w = gla_w_in.copy()
        if hf == 1:
            w[:, :, 6144:6160] = gla_w_in[:, :, 6160:6176]
            w[:, :, 6160:6176] = gla_w_in[:, :, 6144:6160]
        win_p.append(w)
        wgu_p.append(np.ascontiguousarray(np.concatenate([gla_w_gate_up[:, da], gla_w_gate_up[:, db]], axis=2)))
        bg_p.append(np.ascontiguousarray(np.concatenate([gla_b_gate[:, da], gla_b_gate[:, db]], axis=1)[:, None, :]))
    ims = []
    for k in CORES:
        b, hf = k // 2, k % 2
        xl = x[b, hf * HALF:(hf + 1) * HALF]
        xc = ctx[b]
        if hf == 1:
            xl, xc = xl[::-1], xc[::-1]
        sc2 = np.stack([c[b], c_ctx])
        sc2T = np.ascontiguousarray(sc2.reshape(2, KC, 128).transpose(2, 1, 0)).reshape(128, KC * 2)
        sel = np.zeros((128, 2), np.float32)
        sel[:, 1 - hf] = 1.0
        ims.append({
            "xT_in": _cat([xl.T, xc.T]), "sc2T": sc2T, "wm": w_mod, "bm": b_mod, "ngf": ngf, "fing": fing,
            "wg": ffn_w_gate, "wu": ffn_w_up, "wd": ffn_w_down,
            "win": win_p[hf], "wgu": wgu_p[hf], "bgrow": bg_p[hf], "ghbc": ghbc, "wout": gla_w_out,
            "cst": cst, "ident": ident, "sel": sel, "cw": cwc,
            "cl_l": dft_l[hf][0], "sl_l": dft_l[hf][1], "cl_c": dft_c[hf][0], "sl_c": dft_c[hf][1],
            "fwo": fnet_w_out[0], "cw1": cm_w_pw1[0], "cvecs": cvecs, "cwdw": wdw_f[hf], "cwo": cm_w_pw2[0],
            "bvec": bvec,
        })
    return ims


def kernel(x, c, ctx, c_ctx, w_mod, b_mod, norm_g, ffn_w_gate, ffn_w_up, ffn_w_down,
           gla_w_in, gla_w_gate_up, gla_b_gate, gla_g_head, gla_w_out,
           fnet_w_out, fnet_b_out,
           cm_w_pw1, cm_b_pw1, cm_w_dw, cm_b_dw, cm_ln_g, cm_ln_b, cm_w_pw2, cm_b_pw2,
           final_g):
    args = [np.asarray(a, dtype=np.float32) for a in (
        x, c, ctx, c_ctx, w_mod, b_mod, norm_g, ffn_w_gate, ffn_w_up, ffn_w_down,
        gla_w_in, gla_w_gate_up, gla_b_gate, gla_g_head, gla_w_out, fnet_w_out, fnet_b_out,
        cm_w_pw1, cm_b_pw1, cm_w_dw, cm_b_dw, cm_ln_g, cm_ln_b, cm_w_pw2, cm_b_pw2, final_g)]
    ims = _fused_inputs(*args)
    nc = _prog("fused", lambda: build_fused(DEPTH))
    r = _run(nc, ims)
    out = np.zeros((BATCH, SEQ, D), np.float32)
    for k in CORES:
        b, hf = k // 2, k % 2
        o = r[k]["oT"].T
        if hf == 1:
            o = o[::-1]
        out[b, hf * HALF:(hf + 1) * HALF] = o
    return out
```

```python
import numpy as np
from contextlib import ExitStack
import concourse.bass as bass
import concourse.mybir as mybir
from concourse.bass_utils import run_bass_kernel_spmd

F32 = mybir.dt.float32
BF16 = mybir.dt.bfloat16
AF = mybir.ActivationFunctionType
ALU = mybir.AluOpType

D = 2048
KC = D // 128
DFF = 5632
FC = DFF // 128
NMOD = 9
EPS = 1e-6
NCORES = 8


class Buf:
    __slots__ = ("name", "w", "r")

    def __init__(self, name=""):
        self.name = name
        self.w = None
        self.r = {}


class Op:
    __slots__ = ("eng", "fn", "deps", "dma", "ref", "sem", "val", "pos", "cc")

    def __init__(self, eng, fn, dma):
        self.eng = eng
        self.fn = fn
        self.dma = dma
        self.cc = False
        self.deps = []
        self.ref = False
        self.sem = None
        self.val = 0
        self.pos = 0


ENGS = ("pe", "act", "dve", "pool", "sp")
DMA_SLOTS = 8


class Prog:
    def __init__(self, nc, es):
        self.nc = nc
        self.es = es
        self.streams = {e: [] for e in ENGS}
        self.nbuf = 0
        self.arena = None

    def sb(self, name, shape, dt):
        if self.arena is not None:
            return self.arena.alloc(shape, dt)
        return self.es.enter_context(self.nc.sbuf_tensor(name, list(shape), dt))

    def barrier(self):
        deps = []
        for e in ENGS:
            st = self.streams[e]
            last_c = None
            dmas = []
            for o in reversed(st):
                if o.fn is None:
                    continue
                if o.dma or o.cc:
                    if len(dmas) < DMA_SLOTS + 2:
                        dmas.append(o)
                elif last_c is None:
                    last_c = o
                if last_c is not None and len(dmas) >= DMA_SLOTS + 2:
                    break
            if last_c is not None:
                deps.append(last_c)
            deps.extend(dmas)
        for e in ENGS:
            self.op(e, None, extra=list(deps))
        if self.arena is not None:
            self.arena.reset()

    def cc_op(self, fn, extra=()):
        o = self.op("pool", fn, extra=extra)
        o.cc = True
        return o

    def ps(self, name, shape, dt=F32):
        return self.es.enter_context(self.nc.psum_tensor(name, list(shape), dt))

    def buf(self, name=""):
        self.nbuf += 1
        return Buf(name or f"b{self.nbuf}")

    def op(self, eng, fn, reads=(), writes=(), dma=False, extra=()):
        o = Op(eng, fn, dma)
        deps = []
        for b in reads:
            if b.w is not None:
                deps.append(b.w)
        for b in writes:
            if b.w is not None:
                deps.append(b.w)
            for k, v in b.r.items():
                if k is None:
                    deps.extend(v)
                else:
                    deps.append(v)
        deps.extend(extra)
        seen = set()
        for d in deps:
            if d is o or id(d) in seen:
                continue
            if d.eng == "pe" and eng == "pe" and not d.dma and not dma:
                continue
            seen.add(id(d))
            o.deps.append(d)
        for b in reads:
            if dma:
                b.r.setdefault(None, []).append(o)
            else:
                b.r[eng] = o
        for b in writes:
            b.w = o
            b.r = {}
        o.pos = len(self.streams[eng])
        self.streams[eng].append(o)
        return o

    def join(self, eng, ops):
        return self.op(eng, None, extra=list(ops))

    def emit(self):
        nc = self.nc
        es = self.es
        for e in ENGS:
            for o in self.streams[e]:
                for d in o.deps:
                    d.ref = True
        csem = {e: es.enter_context(nc.semaphore(f"c_{e}")) for e in ENGS}
        dsem = {e: [es.enter_context(nc.semaphore(f"d_{e}{i}")) for i in range(DMA_SLOTS)]
                for e in ("act", "pool", "sp")}
        for e in ENGS:
            cc = 0
            dcount = 0
            duse = [0] * DMA_SLOTS
            hist = []
            for o in self.streams[e]:
                if o.fn is None:
                    continue
                if o.cc:
                    o.sem = es.enter_context(nc.semaphore(f"cc_{e}{o.pos}"))
                    o.val = 1
                    o.ref = True
                elif o.dma:
                    slot = dcount % DMA_SLOTS
                    duse[slot] += 1
                    o.sem = dsem[e][slot]
                    o.val = 16 * duse[slot]
                    if dcount >= DMA_SLOTS:
                        o.deps.append(hist[dcount - DMA_SLOTS])
                    hist.append(o)
                    dcount += 1
                    o.ref = True
                elif o.ref:
                    cc += 1
                    o.sem = csem[e]
                    o.val = cc
        block = es.enter_context(nc.Block())

        def run(e, eng):
            seen = {}
            for o in self.streams[e]:
                need = {}
                for d in o.deps:
                    k = id(d.sem)
                    if seen.get(k, 0) >= d.val:
                        continue
                    if k not in need or need[k][1] < d.val:
                        need[k] = (d.sem, d.val)
                for k, (s, v) in need.items():
                    eng.wait_ge(s, v)
                    seen[k] = v
                if o.fn is None:
                    continue
                ins = o.fn(eng)
                if o.sem is not None:
                    ins.then_inc(o.sem, 16 if o.dma else 1)

        @block.tensor
        def _(eng):
            run("pe", eng)

        @block.scalar
        def _(eng):
            run("act", eng)

        @block.vector
        def _(eng):
            run("dve", eng)

        @block.gpsimd
        def _(eng):
            run("pool", eng)

        @block.sync
        def _(eng):
            run("sp", eng)


class Arena:
    def __init__(self, P, nbytes):
        self.t = P.es.enter_context(P.nc.sbuf_tensor("arena", [128, nbytes // 4], F32))
        self.n = nbytes // 4
        self.off = 0

    def reset(self):
        self.off = 0

    def alloc(self, shape, dt):
        shape = list(shape)
        p = shape[0]
        nel = 1
        for d in shape[1:]:
            nel *= d
        esz = 4 if dt == F32 else 2
        words = (nel * esz + 3) // 4
        words = (words + 7) // 8 * 8
        assert self.off + words <= self.n, f"arena overflow {self.off}+{words}>{self.n}"
        ap = self.t[0:p, self.off:self.off + words]
        self.off += words
        if dt != F32:
            ap = ap.bitcast(dt)
        ap = ap[:, 0:nel]
        if len(shape) == 3:
            ap = ap.rearrange("p (a b) -> p a b", b=shape[2])
        elif len(shape) == 4:
            ap = ap.rearrange("p (a b c) -> p a b c", b=shape[2], c=shape[3])
        return ap


def chunked(ap2d):
    return ap2d.rearrange("(c p) t -> p c t", p=128)


class FFNCtx:
    def __init__(self, P, W):
        self.P = P
        nc = P.nc
        self.W = W
        self.ones = P.sb("ones", [128, 128], F32)
        self.ones_b = P.buf("ones")
        P.op("pool", lambda e: e.memset(self.ones[:], 1.0), writes=[self.ones_b])
        self.banks = [P.ps(f"bank{i}", [128, 512]) for i in range(8)]
        self.bank_b = [P.buf(f"bank{i}") for i in range(8)]
        self.sq = [P.sb(f"sq{i}", [128, W], F32) for i in range(2)]
        self.sq_b = [P.buf(f"sq{i}") for i in range(2)]
        self.rstd = P.sb("rstd", [128, W], F32)
        self.rstd_b = P.buf("rstd")
        self.tmp = [P.sb(f"tmp{i}", [128, W], F32) for i in range(2)]
        self.tmp_b = [P.buf(f"tmp{i}") for i in range(2)]
        self.n_sq = 0
        self.n_tmp = 0


def emit_rstd(C, x, xb, w, bank):
    P = C.P
    for c in range(KC):
        i = C.n_sq % 2
        C.n_sq += 1
        P.op("act", lambda e, c=c, i=i: e.activation(out=C.sq[i][:, :w], in_=x[:, c, :w], func=AF.Square),
             reads=[xb[c]], writes=[C.sq_b[i]])
        P.op("pe", lambda e, c=c, i=i: e.matmul(C.banks[bank][:, :w], C.ones[:], C.sq[i][:, :w],
                                               start=(c == 0), stop=(c == KC - 1)),
             reads=[C.sq_b[i], C.ones_b], writes=[C.bank_b[bank]])
    P.op("act", lambda e: e.activation(out=C.rstd[:, :w], in_=C.banks[bank][:, :w], func=AF.Sqrt,
                                       bias=C.epsb[:, 0:1], scale=1.0 / D),
         reads=[C.bank_b[bank], C.eps_bb], writes=[C.rstd_b])
    P.op("dve", lambda e: e.reciprocal(out=C.rstd[:, :w], in_=C.rstd[:, :w]),
         reads=[C.rstd_b], writes=[C.rstd_b])


def emit_prenorm(C, x, xb, w, gs, shift, msb, out_fn, out_bufs):
    P = C.P
    for c in range(KC):
        i = C.n_tmp % 2
        C.n_tmp += 1
        P.op("dve", lambda e, c=c, i=i: e.scalar_tensor_tensor(
            out=C.tmp[i][:, :w], in0=x[:, c, :w], scalar=gs[:, c:c + 1], in1=C.rstd[:, :w],
            op0=ALU.mult, op1=ALU.mult),
            reads=[xb[c], C.rstd_b, msb], writes=[C.tmp_b[i]])
        P.op("act", lambda e, c=c, i=i: e.activation(out=out_fn(c), in_=C.tmp[i][:, :w], func=AF.Identity,
                                                     bias=shift[:, c:c + 1], scale=1.0),
             reads=[C.tmp_b[i], msb], writes=[out_bufs[c]])


def build_ffn_program(T, tiles, n_sets, n_ffn, proj_in, emit_h, final_norm, gla_in=False):
    W = max(w for _, w, _ in tiles)
    nc = bass.Bass("TRN2", target_bir_lowering=False)
    NV = 4 * n_ffn + (2 if proj_in else 0) + (3 if emit_h else 0) + (1 if final_norm else 0)
    xT = nc.dram_tensor("xT", [D, T], F32, kind="ExternalInput").ap()
    mods = nc.dram_tensor("mods", [128, n_sets * NV * KC], F32, kind="ExternalInput").ap()
    wg = [nc.dram_tensor(f"wg{j}", [D, DFF], F32, kind="ExternalInput").ap() for j in range(n_ffn)]
    wu = [nc.dram_tensor(f"wu{j}", [D, DFF], F32, kind="ExternalInput").ap() for j in range(n_ffn)]
    wd = [nc.dram_tensor(f"wd{j}", [DFF, D], F32, kind="ExternalInput").ap() for j in range(n_ffn)]
    if proj_in:
        if gla_in:
            ofT = nc.dram_tensor("ofT", [D, T], F32, kind="ExternalInput").ap()
            obT = nc.dram_tensor("obT", [D, T], F32, kind="ExternalInput").ap()
            rsT = nc.dram_tensor("rsT", [D, T], F32, kind="ExternalInput").ap()
            gh_d = nc.dram_tensor("gh", [128, 4], F32, kind="ExternalInput").ap()
        else:
            mT = nc.dram_tensor("mT", [D, T], F32, kind="ExternalInput").ap()
        wo = nc.dram_tensor("wo", [D, D], F32, kind="ExternalInput").ap()
    oT = nc.dram_tensor("oT", [D, T], F32, kind="ExternalOutput").ap()
    if emit_h:
        hT = nc.dram_tensor("hT", [D, T], F32, kind="ExternalOutput").ap()

    with ExitStack() as es:
        P = Prog(nc, es)
        C = FFNCtx(P, W)
        C.epsb = P.sb("epsb", [128, 1], F32)
        C.eps_bb = P.buf("eps")
        P.op("pool", lambda e: e.memset(C.epsb[:], EPS), writes=[C.eps_bb])
        mv = P.sb("mv", [128, n_sets * NV * KC], F32)
        mvb = P.buf("mv")
        P.op("sp", lambda e: e.dma_start(out=mv[:], in_=mods), writes=[mvb], dma=True)

        def vec(s, k):
            o = (s * NV + k) * KC
            return mv[:, o:o + KC]

        slot = 0
        ffn_slots = []
        for j in range(n_ffn):
            ffn_slots.append(slot)
            slot += 4
        proj_slot = slot if proj_in else None
        slot += 2 if proj_in else 0
        h_slot = slot if emit_h else None
        slot += 3 if emit_h else 0
        fin_slot = slot if final_norm else None
        for s in range(n_sets):
            norm_slots = list(ffn_slots) + ([h_slot] if emit_h else [])
            for k in norm_slots:
                P.op("dve", lambda e, s=s, k=k: e.scalar_tensor_tensor(
                    out=vec(s, k), in0=vec(s, k + 2), scalar=1.0, in1=vec(s, k),
                    op0=ALU.add, op1=ALU.mult), reads=[mvb], writes=[mvb])
            for k in ffn_slots:
                P.op("dve", lambda e, s=s, k=k: e.tensor_scalar(
                    out=vec(s, k + 3), in0=vec(s, k + 3), scalar1=0.5, scalar2=None, op0=ALU.mult),
                    reads=[mvb], writes=[mvb])

        x = P.sb("x", [128, KC, W], F32)
        xb = [P.buf(f"x{c}") for c in range(KC)]
        h = P.sb("h", [128, KC, W], BF16)
        hb = [P.buf(f"h{c}") for c in range(KC)]
        a = P.sb("a", [128, FC, W], BF16)
        ab = [P.buf(f"a{c}") for c in range(FC)]
        sg = [P.sb(f"sg{i}", [128, W], F32) for i in range(2)]
        sgb = [P.buf(f"sg{i}") for i in range(2)]
        FB = 2
        NWB = 2
        wgt = [P.sb(f"wgt{i}", [128, KC, FB * 128], BF16) for i in range(NWB)]
        wgb = [P.buf(f"wgt{i}") for i in range(NWB)]
        wut = [P.sb(f"wut{i}", [128, KC, FB * 128], BF16) for i in range(NWB)]
        wub = [P.buf(f"wut{i}") for i in range(NWB)]
        DFB = 4
        NDB = 3
        wdt = [P.sb(f"wdt{i}", [128, DFB, 512], BF16) for i in range(NDB)]
        wdb = [P.buf(f"wdt{i}") for i in range(NDB)]
        if emit_h or final_norm:
            ho = [P.sb(f"ho{i}", [128, W], F32) for i in range(2)]
            hob = [P.buf(f"ho{i}") for i in range(2)]
        if proj_in:
            m32 = [P.sb(f"m32_{i}", [128, W], F32) for i in range(2)]
            m32b = [P.buf(f"m32_{i}") for i in range(2)]
            wot = [P.sb(f"wot{i}", [128, KC, 128], BF16) for i in range(2)]
            wotb = [P.buf(f"wot{i}") for i in range(2)]
            if gla_in:
                gh = P.sb("gh_sb", [128, 4], F32)
                ghb = P.buf("gh")
                P.op("sp", lambda e: e.dma_start(out=gh[:], in_=gh_d), writes=[ghb], dma=True)
                osum = P.sb("osum", [128, 4, W], F32)
                osumb = [P.buf(f"osum{i}") for i in range(4)]
                ob32 = [P.sb(f"ob32_{i}", [128, W], F32) for i in range(2)]
                ob32b = [P.buf(f"ob32_{i}") for i in range(2)]
                eps5 = P.sb("eps5", [128, 1], F32)
                eps5b = P.buf("eps5")
                P.op("pool", lambda e: e.memset(eps5[:], EPS), writes=[eps5b])
        cnt = {"w": 0, "d": 0, "sg": 0, "ho": 0, "m": 0, "wo": 0}
        out_dmas = []

        for (t0, w, s) in tiles:
            for half in range(2):
                cs = slice(half * 8, half * 8 + 8)
                P.op("sp", lambda e, cs=cs, t0=t0, w=w: e.dma_start(
                    out=x[:, cs, :w], in_=chunked(xT)[:, cs, t0:t0 + w]),
                    writes=xb[cs], dma=True)
            if proj_in:
                if gla_in:
                    for hd in range(4):
                        for cc in range(4):
                            c = 4 * hd + cc
                            i = cnt["m"] % 2
                            cnt["m"] += 1
                            P.op("sp", lambda e, c=c, cc=cc, t0=t0, w=w: e.dma_start(
                                out=osum[:, cc, :w], in_=ofT[c * 128:(c + 1) * 128, t0:t0 + w]),
                                writes=[osumb[cc]], dma=True)
                            P.op("sp", lambda e, c=c, i=i, t0=t0, w=w: e.dma_start(
                                out=ob32[i][:, :w], in_=obT[c * 128:(c + 1) * 128, t0:t0 + w]),
                                writes=[ob32b[i]], dma=True)
                            P.op("dve", lambda e, cc=cc, i=i, w=w: e.tensor_tensor(
                                out=osum[:, cc, :w], in0=osum[:, cc, :w], in1=ob32[i][:, :w], op=ALU.add),
                                reads=[osumb[cc], ob32b[i]], writes=[osumb[cc]])
                            si = C.n_sq % 2
                            C.n_sq += 1
                            P.op("act", lambda e, cc=cc, si=si, w=w: e.activation(
                                out=C.sq[si][:, :w], in_=osum[:, cc, :w], func=AF.Square),
                                reads=[osumb[cc]], writes=[C.sq_b[si]])
                            P.op("pe", lambda e, cc=cc, si=si, w=w: e.matmul(
                                C.banks[0][:, :w], C.ones[:], C.sq[si][:, :w], start=(cc == 0), stop=(cc == 3)),
                                reads=[C.sq_b[si], C.ones_b], writes=[C.bank_b[0]])
                        P.op("act", lambda e, w=w: e.activation(
                            out=C.rstd[:, :w], in_=C.banks[0][:, :w], func=AF.Sqrt, bias=eps5[:, 0:1],
                            scale=1.0 / 512.0), reads=[C.bank_b[0], eps5b], writes=[C.rstd_b])
                        P.op("dve", lambda e, w=w: e.reciprocal(out=C.rstd[:, :w], in_=C.rstd[:, :w]),
                             reads=[C.rstd_b], writes=[C.rstd_b])
                        for cc in range(4):
                            c = 4 * hd + cc
                            i = cnt["m"] % 2
                            cnt["m"] += 1
                            P.op("sp", lambda e, c=c, i=i, t0=t0, w=w: e.dma_start(
                                out=m32[i][:, :w], in_=rsT[c * 128:(c + 1) * 128, t0:t0 + w]),
                                writes=[m32b[i]], dma=True)
                            P.op("dve", lambda e, cc=cc, w=w: e.scalar_tensor_tensor(
                                out=osum[:, cc, :w], in0=osum[:, cc, :w], scalar=gh[:, cc:cc + 1],
                                in1=C.rstd[:, :w], op0=ALU.mult, op1=ALU.mult),
                                reads=[osumb[cc], ghb, C.rstd_b], writes=[osumb[cc]])
                            P.op("dve", lambda e, c=c, cc=cc, i=i, w=w: e.tensor_tensor(
                                out=h[:, c, :w], in0=osum[:, cc, :w], in1=m32[i][:, :w], op=ALU.mult),
                                reads=[osumb[cc], m32b[i]], writes=[hb[c]])
                else:
                    for c in range(KC):
                        i = cnt["m"] % 2
                        cnt["m"] += 1
                        P.op("sp", lambda e, c=c, i=i, t0=t0, w=w: e.dma_start(
                            out=m32[i][:, :w], in_=mT[c * 128:(c + 1) * 128, t0:t0 + w]),
                            writes=[m32b[i]], dma=True)
                        P.op("act", lambda e, c=c, i=i, w=w: e.activation(out=h[:, c, :w], in_=m32[i][:, :w],
                                                                         func=AF.Identity),
                             reads=[m32b[i]], writes=[hb[c]])
                for dc in range(KC):
                    i = cnt["wo"] % 2
                    cnt["wo"] += 1
                    P.op("pool", lambda e, dc=dc, i=i: e.dma_start(
                        out=wot[i][:], in_=chunked(wo)[:, :, dc * 128:(dc + 1) * 128]),
                        writes=[wotb[i]], dma=True)
                    bank = 4 + (dc % 4)
                    for kc in range(KC):
                        P.op("pe", lambda e, dc=dc, kc=kc, i=i, bank=bank, w=w: e.matmul(
                            C.banks[bank][:, :w], wot[i][:, kc, :], h[:, kc, :w],
                            start=(kc == 0), stop=(kc == KC - 1)),
                            reads=[wotb[i], hb[kc]], writes=[C.bank_b[bank]])
                    ti = C.n_tmp % 2
                    C.n_tmp += 1
                    P.op("act", lambda e, dc=dc, ti=ti, bank=bank, w=w, s=s: e.activation(
                        out=C.tmp[ti][:, :w], in_=C.banks[bank][:, :w], func=AF.Identity,
                        bias=vec(s, proj_slot + 1)[:, dc:dc + 1], scale=1.0),
                        reads=[C.bank_b[bank], mvb], writes=[C.tmp_b[ti]])
                    P.op("dve", lambda e, dc=dc, ti=ti, w=w, s=s: e.scalar_tensor_tensor(
                        out=x[:, dc, :w], in0=C.tmp[ti][:, :w], scalar=vec(s, proj_slot)[:, dc:dc + 1],
                        in1=x[:, dc, :w], op0=ALU.mult, op1=ALU.add),
                        reads=[C.tmp_b[ti], mvb, xb[dc]], writes=[xb[dc]])

            for j in range(n_ffn):
                k0 = ffn_slots[j]
                emit_rstd(C, x, xb, w, 0)
                emit_prenorm(C, x, xb, w, vec(s, k0), vec(s, k0 + 1), mvb,
                             lambda c, w=w: h[:, c, :w], hb)
                for fb in range(FC // FB):
                    i = cnt["w"] % NWB
                    cnt["w"] += 1
                    fsl = slice(fb * FB * 128, (fb + 1) * FB * 128)
                    P.op("pool", lambda e, i=i, fsl=fsl, j=j: e.dma_start(
                        out=wgt[i][:], in_=chunked(wg[j])[:, :, fsl]), writes=[wgb[i]], dma=True)
                    P.op("pool", lambda e, i=i, fsl=fsl, j=j: e.dma_start(
                        out=wut[i][:], in_=chunked(wu[j])[:, :, fsl]), writes=[wub[i]], dma=True)
                    for f in range(FB):
                        fc = fb * FB + f
                        par = fc % 2
                        gb, ub = 2 * par, 2 * par + 1
                        for kc in range(KC):
                            P.op("pe", lambda e, i=i, f=f, kc=kc, gb=gb, w=w: e.matmul(
                                C.banks[gb][:, :w], wgt[i][:, kc, f * 128:(f + 1) * 128], h[:, kc, :w],
                                start=(kc == 0), stop=(kc == KC - 1)),
                                reads=[wgb[i], hb[kc]], writes=[C.bank_b[gb]])
                        for kc in range(KC):
                            P.op("pe", lambda e, i=i, f=f, kc=kc, ub=ub, w=w: e.matmul(
                                C.banks[ub][:, :w], wut[i][:, kc, f * 128:(f + 1) * 128], h[:, kc, :w],
                                start=(kc == 0), stop=(kc == KC - 1)),
                                reads=[wub[i], hb[kc]], writes=[C.bank_b[ub]])
                        si = cnt["sg"] % 2
                        cnt["sg"] += 1
                        P.op("act", lambda e, si=si, gb=gb, w=w: e.activation(
                            out=sg[si][:, :w], in_=C.banks[gb][:, :w], func=AF.Silu),
                            reads=[C.bank_b[gb]], writes=[sgb[si]])
                        P.op("dve", lambda e, si=si, ub=ub, fc=fc, w=w: e.tensor_tensor(
                            out=a[:, fc, :w], in0=sg[si][:, :w], in1=C.banks[ub][:, :w], op=ALU.mult),
                            reads=[sgb[si], C.bank_b[ub]], writes=[ab[fc]])
                for dg in range(4):
                    base = 4 if dg % 2 == 0 else 0
                    for fb in range(FC // DFB):
                        i = cnt["d"] % NDB
                        cnt["d"] += 1
                        P.op("pool", lambda e, i=i, fb=fb, dg=dg, j=j: e.dma_start(
                            out=wdt[i][:],
                            in_=wd[j][fb * DFB * 128:(fb + 1) * DFB * 128, dg * 512:(dg + 1) * 512]
                            .rearrange("(c p) n -> p c n", p=128)), writes=[wdb[i]], dma=True)
                        for f in range(DFB):
                            fc = fb * DFB + f
                            for dc in range(4):
                                P.op("pe", lambda e, i=i, f=f, fc=fc, dc=dc, base=base, w=w: e.matmul(
                                    C.banks[base + dc][:, :w], wdt[i][:, f, dc * 128:(dc + 1) * 128],
                                    a[:, fc, :w], start=(fc == 0), stop=(fc == FC - 1)),
                                    reads=[wdb[i], ab[fc]], writes=[C.bank_b[base + dc]])
                    for dc in range(4):
                        c = dg * 4 + dc
                        P.op("dve", lambda e, c=c, dc=dc, base=base, w=w, s=s, k0=k0: e.scalar_tensor_tensor(
                            out=x[:, c, :w], in0=C.banks[base + dc][:, :w],
                            scalar=vec(s, k0 + 3)[:, c:c + 1], in1=x[:, c, :w],
                            op0=ALU.mult, op1=ALU.add),
                            reads=[C.bank_b[base + dc], mvb, xb[c]], writes=[xb[c]])

            if final_norm:
                emit_rstd(C, x, xb, w, 0)
                for c in range(KC):
                    i = cnt["ho"] % 2
                    cnt["ho"] += 1
                    P.op("dve", lambda e, c=c, i=i, w=w, s=s: e.scalar_tensor_tensor(
                        out=ho[i][:, :w], in0=x[:, c, :w], scalar=vec(s, fin_slot)[:, c:c + 1],
                        in1=C.rstd[:, :w], op0=ALU.mult, op1=ALU.mult),
                        reads=[xb[c], C.rstd_b, mvb], writes=[hob[i]])
                    out_dmas.append(P.op("sp", lambda e, c=c, i=i, t0=t0, w=w: e.dma_start(
                        out=oT[c * 128:(c + 1) * 128, t0:t0 + w], in_=ho[i][:, :w]),
                        reads=[hob[i]], dma=True))
            else:
                for half in range(2):
                    cs = slice(half * 8, half * 8 + 8)
                    out_dmas.append(P.op("sp", lambda e, cs=cs, t0=t0, w=w: e.dma_start(
                        out=chunked(oT)[:, cs, t0:t0 + w], in_=x[:, cs, :w]),
                        reads=xb[cs], dma=True))
            if emit_h:
                emit_rstd(C, x, xb, w, 0)
                for c in range(KC):
                    i = cnt["ho"] % 2
                    cnt["ho"] += 1
                    ti = C.n_tmp % 2
                    C.n_tmp += 1
                    P.op("dve", lambda e, c=c, ti=ti, w=w, s=s: e.scalar_tensor_tensor(
                        out=C.tmp[ti][:, :w], in0=x[:, c, :w], scalar=vec(s, h_slot)[:, c:c + 1],
                        in1=C.rstd[:, :w], op0=ALU.mult, op1=ALU.mult),
                        reads=[xb[c], C.rstd_b, mvb], writes=[C.tmp_b[ti]])
                    P.op("act", lambda e, c=c, i=i, ti=ti, w=w, s=s: e.activation(
                        out=ho[i][:, :w], in_=C.tmp[ti][:, :w], func=AF.Identity,
                        bias=vec(s, h_slot + 1)[:, c:c + 1], scale=1.0),
                        reads=[C.tmp_b[ti], mvb], writes=[hob[i]])
                    out_dmas.append(P.op("sp", lambda e, c=c, i=i, t0=t0, w=w: e.dma_start(
                        out=hT[c * 128:(c + 1) * 128, t0:t0 + w], in_=ho[i][:, :w]),
                        reads=[hob[i]], dma=True))
        P.join("sp", out_dmas)
        P.emit()
    return nc


MODC = NMOD * D // NCORES
NROW = 5


def build_mod_program(depth):
    nc = bass.Bass("TRN2", target_bir_lowering=False)
    scT = nc.dram_tensor("scT", [128, KC * NROW], F32, kind="ExternalInput").ap()
    wm = nc.dram_tensor("wm", [depth, D, MODC], F32, kind="ExternalInput").ap()
    bm = nc.dram_tensor("bm", [depth, MODC], F32, kind="ExternalInput").ap()
    out = nc.dram_tensor("out", [depth, NROW, MODC], F32, kind="ExternalOutput").ap()
    blocks = [(0, 512), (512, 512), (1024, 512), (1536, 512), (2048, 256)]
    with ExitStack() as es:
        P = Prog(nc, es)
        sc = P.sb("sc", [128, KC * NROW], F32)
        scb = P.buf()
        ones = P.sb("ones1", [1, NROW], F32)
        onesb = P.buf()
        P.op("pool", lambda e: e.memset(ones[:], 1.0), writes=[onesb])
        P.op("sp", lambda e: e.dma_start(out=sc[:], in_=scT), writes=[scb], dma=True)
        P.op("act", lambda e: e.activation(out=sc[:], in_=sc[:], func=AF.Silu), reads=[scb], writes=[scb])
        wt = [P.sb(f"wt{i}", [128, KC, 512], F32) for i in range(2)]
        wtb = [[P.buf(), P.buf()] for i in range(2)]
        bt = [P.sb(f"bt{i}", [1, 512], F32) for i in range(2)]
        btb = [P.buf() for i in range(2)]
        ot = [P.sb(f"ot{i}", [NROW, 512], F32) for i in range(2)]
        otb = [P.buf() for i in range(2)]
        banks = [P.ps(f"bk{i}", [128, 512]) for i in range(2)]
        bkb = [P.buf() for i in range(2)]
        n = 0
        outs = []
        for l in range(depth):
            for (c0, cw) in blocks:
                i = n % 2
                n += 1
                for half in range(2):
                    P.op("sp" if half == 0 else "act", lambda e, i=i, l=l, c0=c0, cw=cw, half=half: e.dma_start(
                        out=wt[i][:, half * 8:half * 8 + 8, :cw],
                        in_=wm[l].rearrange("(c p) n -> p c n", p=128)[:, half * 8:half * 8 + 8, c0:c0 + cw]),
                        writes=[wtb[i][half]], dma=True)
                P.op("sp", lambda e, i=i, l=l, c0=c0, cw=cw: e.dma_start(out=bt[i][:, :cw], in_=bm[l:l + 1, c0:c0 + cw]),
                     writes=[btb[i]], dma=True)
                for kc in range(KC):
                    P.op("pe", lambda e, i=i, kc=kc, cw=cw: e.matmul(
                        banks[i][:NROW, :cw], sc[:, kc * NROW:(kc + 1) * NROW], wt[i][:, kc, :cw],
                        start=(kc == 0), stop=False), reads=[wtb[i][kc // 8], scb], writes=[bkb[i]])
                P.op("pe", lambda e, i=i, cw=cw: e.matmul(banks[i][:NROW, :cw], ones[:], bt[i][:, :cw],
                                                         start=False, stop=True),
                     reads=[btb[i], onesb], writes=[bkb[i]])
                P.op("act", lambda e, i=i, cw=cw: e.activation(out=ot[i][:, :cw], in_=banks[i][:NROW, :cw],
                                                              func=AF.Identity),
                     reads=[bkb[i]], writes=[otb[i]])
                outs.append(P.op("sp", lambda e, i=i, l=l, c0=c0, cw=cw: e.dma_start(
                    out=out[l, :, c0:c0 + cw], in_=ot[i][:, :cw]), reads=[otb[i]], dma=True))
        P.join("sp", outs)
        P.emit()
    return nc


CONV_W = 31


def build_conv_program(T, tiles):
    W = max(w for _, w, _ in tiles)
    nc = bass.Bass("TRN2", target_bir_lowering=False)
    hT = nc.dram_tensor("hT", [D, T], F32, kind="ExternalInput").ap()
    w1 = nc.dram_tensor("w1", [D, 2 * D], F32, kind="ExternalInput").ap()
    vecs = nc.dram_tensor("vecs", [128, 5 * KC], F32, kind="ExternalInput").ap()
    wdw = nc.dram_tensor("wdw", [128, KC * CONV_W], F32, kind="ExternalInput").ap()
    mT = nc.dram_tensor("mT", [D, T], F32, kind="ExternalOutput").ap()
    with ExitStack() as es:
        P = Prog(nc, es)
        C = FFNCtx(P, W)
        C.epsb = P.sb("epsb", [128, 1], F32)
        C.eps_bb = P.buf("eps")
        P.op("pool", lambda e: e.memset(C.epsb[:], EPS), writes=[C.eps_bb])
        vv = P.sb("vv", [128, 5 * KC], F32)
        vvb = P.buf()
        P.op("sp", lambda e: e.dma_start(out=vv[:], in_=vecs), writes=[vvb], dma=True)
        wk = P.sb("wk", [128, KC * CONV_W], F32)
        wkb = P.buf()
        P.op("sp", lambda e: e.dma_start(out=wk[:], in_=wdw), writes=[wkb], dma=True)

        def vec(k):
            return vv[:, k * KC:(k + 1) * KC]

        h = P.sb("h", [128, KC, W], BF16)
        hb = [P.buf() for c in range(KC)]
        y = P.sb("y", [128, KC, W], F32)
        yb = [P.buf() for c in range(KC)]
        u = [P.sb(f"u{i}", [128, W], F32) for i in range(2)]
        ub = [P.buf() for i in range(2)]
        sgm = [P.sb(f"sgm{i}", [128, W], F32) for i in range(2)]
        sgmb = [P.buf() for i in range(2)]
        wa = [P.sb(f"wa{i}", [128, KC, 128], BF16) for i in range(2)]
        wab = [P.buf() for i in range(2)]
        wgx = [P.sb(f"wgx{i}", [128, KC, 128], BF16) for i in range(2)]
        wgxb = [P.buf() for i in range(2)]
        mean = P.sb("mean", [128, W], F32)
        meanb = P.buf()
        var = P.sb("var", [128, W], F32)
        varb = P.buf()
        ho = [P.sb(f"ho{i}", [128, W], F32) for i in range(2)]
        hob = [P.buf() for i in range(2)]
        n = {"w": 0, "u": 0, "ho": 0}
        outs = []
        for (t0, w, L) in tiles:
            ns = w // L
            for half in range(2):
                cs = slice(half * 8, half * 8 + 8)
                P.op("pool", lambda e, cs=cs, t0=t0, w=w: e.dma_start(
                    out=h[:, cs, :w], in_=chunked(hT)[:, cs, t0:t0 + w]), writes=hb[cs], dma=True)
            for mc in range(KC):
                i = n["w"] % 2
                n["w"] += 1
                P.op("pool", lambda e, i=i, mc=mc: e.dma_start(
                    out=wa[i][:], in_=chunked(w1)[:, :, mc * 128:(mc + 1) * 128]), writes=[wab[i]], dma=True)
                P.op("pool", lambda e, i=i, mc=mc: e.dma_start(
                    out=wgx[i][:], in_=chunked(w1)[:, :, D + mc * 128:D + (mc + 1) * 128]),
                    writes=[wgxb[i]], dma=True)
                par = mc % 2
                ba, bg = 2 + 2 * par, 3 + 2 * par
                for kc in range(KC):
                    P.op("pe", lambda e, i=i, kc=kc, ba=ba, w=w: e.matmul(
                        C.banks[ba][:, :w], wa[i][:, kc, :], h[:, kc, :w], start=(kc == 0), stop=(kc == KC - 1)),
                        reads=[wab[i], hb[kc]], writes=[C.bank_b[ba]])
                for kc in range(KC):
                    P.op("pe", lambda e, i=i, kc=kc, bg=bg, w=w: e.matmul(
                        C.banks[bg][:, :w], wgx[i][:, kc, :], h[:, kc, :w], start=(kc == 0), stop=(kc == KC - 1)),
                        reads=[wgxb[i], hb[kc]], writes=[C.bank_b[bg]])
                ui = n["u"] % 2
                n["u"] += 1
                P.op("act", lambda e, ui=ui, bg=bg, mc=mc, w=w: e.activation(
                    out=sgm[ui][:, :w], in_=C.banks[bg][:, :w], func=AF.Sigmoid,
                    bias=vec(1)[:, mc:mc + 1], scale=1.0), reads=[C.bank_b[bg], vvb], writes=[sgmb[ui]])
                P.op("dve", lambda e, ui=ui, ba=ba, mc=mc, w=w: e.scalar_tensor_tensor(
                    out=u[ui][:, :w], in0=C.banks[ba][:, :w], scalar=vec(0)[:, mc:mc + 1], in1=sgm[ui][:, :w],
                    op0=ALU.add, op1=ALU.mult), reads=[C.bank_b[ba], sgmb[ui], vvb], writes=[ub[ui]])
                u3 = u[ui][:, :w].rearrange("p (s l) -> p s l", l=L)
                y3 = y[:, mc, :w].rearrange("p (s l) -> p s l", l=L)
                P.op("dve", lambda e, ui=ui, mc=mc, w=w: e.tensor_scalar(
                    out=y[:, mc, :w], in0=u[ui][:, :w], scalar1=wk[:, mc * CONV_W + 15:mc * CONV_W + 16],
                    scalar2=vec(2)[:, mc:mc + 1], op0=ALU.mult, op1=ALU.add),
                    reads=[ub[ui], wkb, vvb], writes=[yb[mc]])
                for k in range(CONV_W):
                    o = k - 15
                    if o == 0 or abs(o) >= L:
                        continue
                    a0, a1 = max(0, -o), min(L, L - o)
                    P.op("dve", lambda e, u3=u3, y3=y3, mc=mc, k=k, a0=a0, a1=a1, o=o: e.scalar_tensor_tensor(
                        out=y3[:, :, a0:a1], in0=u3[:, :, a0 + o:a1 + o],
                        scalar=wk[:, mc * CONV_W + k:mc * CONV_W + k + 1], in1=y3[:, :, a0:a1],
                        op0=ALU.mult, op1=ALU.add), reads=[ub[ui], wkb, yb[mc]], writes=[yb[mc]])
                P.op("pe", lambda e, mc=mc, w=w: e.matmul(C.banks[0][:, :w], C.ones[:], y[:, mc, :w],
                                                         start=(mc == 0), stop=(mc == KC - 1)),
                     reads=[yb[mc], C.ones_b], writes=[C.bank_b[0]])
                si = C.n_sq % 2
                C.n_sq += 1
                P.op("act", lambda e, mc=mc, si=si, w=w: e.activation(out=C.sq[si][:, :w], in_=y[:, mc, :w],
                                                                     func=AF.Square),
                     reads=[yb[mc]], writes=[C.sq_b[si]])
                P.op("pe", lambda e, mc=mc, si=si, w=w: e.matmul(C.banks[1][:, :w], C.ones[:], C.sq[si][:, :w],
                                                                start=(mc == 0), stop=(mc == KC - 1)),
                     reads=[C.sq_b[si], C.ones_b], writes=[C.bank_b[1]])
            P.op("act", lambda e, w=w: e.activation(out=mean[:, :w], in_=C.banks[0][:, :w], func=AF.Identity,
                                                    scale=1.0 / D), reads=[C.bank_b[0]], writes=[meanb])
            P.op("dve", lambda e, w=w: e.tensor_tensor(out=var[:, :w], in0=mean[:, :w], in1=mean[:, :w],
                                                       op=ALU.mult), reads=[meanb], writes=[varb])
            P.op("dve", lambda e, w=w: e.scalar_tensor_tensor(
                out=var[:, :w], in0=C.banks[1][:, :w], scalar=1.0 / D, in1=var[:, :w],
                op0=ALU.mult, op1=ALU.subtract), reads=[C.bank_b[1], varb], writes=[varb])
            P.op("act", lambda e, w=w: e.activation(out=C.rstd[:, :w], in_=var[:, :w], func=AF.Sqrt,
                                                    bias=C.epsb[:, 0:1], scale=1.0),
                 reads=[varb, C.eps_bb], writes=[C.rstd_b])
            P.op("dve", lambda e, w=w: e.reciprocal(out=C.rstd[:, :w], in_=C.rstd[:, :w]),
                 reads=[C.rstd_b], writes=[C.rstd_b])
            for c in range(KC):
                ti = C.n_tmp % 2
                C.n_tmp += 1
                i = n["ho"] % 2
                n["ho"] += 1
                P.op("dve", lambda e, c=c, ti=ti, w=w: e.tensor_tensor(
                    out=C.tmp[ti][:, :w], in0=y[:, c, :w], in1=mean[:, :w], op=ALU.subtract),
                    reads=[yb[c], meanb], writes=[C.tmp_b[ti]])
                P.op("dve", lambda e, c=c, ti=ti, w=w: e.scalar_tensor_tensor(
                    out=C.tmp[ti][:, :w], in0=C.tmp[ti][:, :w], scalar=vec(3)[:, c:c + 1], in1=C.rstd[:, :w],
                    op0=ALU.mult, op1=ALU.mult), reads=[C.tmp_b[ti], C.rstd_b, vvb], writes=[C.tmp_b[ti]])
                P.op("act", lambda e, c=c, ti=ti, i=i, w=w: e.activation(
                    out=ho[i][:, :w], in_=C.tmp[ti][:, :w], func=AF.Silu, bias=vec(4)[:, c:c + 1], scale=1.0),
                    reads=[C.tmp_b[ti], vvb], writes=[hob[i]])
                outs.append(P.op("sp", lambda e, c=c, i=i, t0=t0, w=w: e.dma_start(
                    out=mT[c * 128:(c + 1) * 128, t0:t0 + w], in_=ho[i][:, :w]), reads=[hob[i]], dma=True))
        P.join("sp", outs)
        P.emit()
    return nc


GW = 256
NG = 8


def build_fnet_program(seqs):
    nc = bass.Bass("TRN2", target_bir_lowering=False)
    cw_d = nc.dram_tensor("cw", [128, 2 * 512], BF16, kind="ExternalInput").ap()
    io = {}
    for (name, L, NK) in seqs:
        io[name] = (
            nc.dram_tensor(f"hT_{name}", [D, L], F32, kind="ExternalInput").ap(),
            nc.dram_tensor(f"cl_{name}", [L, NK], BF16, kind="ExternalInput").ap(),
            nc.dram_tensor(f"sl_{name}", [L, NK], BF16, kind="ExternalInput").ap(),
            nc.dram_tensor(f"fT_{name}", [D, NK], F32, kind="ExternalOutput").ap(),
        )
    LMAX = max(L for _, L, _ in seqs)
    with ExitStack() as es:
        P = Prog(nc, es)
        banks = [P.ps(f"bank{i}", [128, 512]) for i in range(8)]
        bkb = [P.buf() for i in range(8)]
        cw = P.sb("cw_sb", [128, 2, 512], BF16)
        cwb = P.buf()
        P.op("sp", lambda e: e.dma_start(out=cw[:].rearrange("p a b -> p (a b)"), in_=cw_d), writes=[cwb], dma=True)
        hg = [P.sb(f"hg{i}", [128, 2, LMAX], BF16) for i in range(2)]
        hgb = [P.buf() for i in range(2)]
        A = P.sb("A", [128, LMAX // 128, 512], BF16)
        Ab = [P.buf() for i in range(LMAX // 128)]
        NB = 4
        clt = [P.sb(f"clt{i}", [128, 8, 512], BF16) for i in range(NB)]
        cltb = [P.buf() for i in range(NB)]
        slt = [P.sb(f"slt{i}", [128, 8, 512], BF16) for i in range(NB)]
        sltb = [P.buf() for i in range(NB)]
        fo = [P.sb(f"fo{i}", [128, 512], F32) for i in range(4)]
        fob = [P.buf() for i in range(4)]
        n = {"hg": 0, "m": 0, "fo": 0, "a": 0}
        outs = []
        for (name, L, NK) in seqs:
            hT, cl, sl, fT = io[name]
            NCH = L // 128
            for g in range(NG):
                gi = n["hg"] % 2
                n["hg"] += 1
                P.op("pool", lambda e, gi=gi, g=g, L=L, hT=hT: e.dma_start(
                    out=hg[gi][:, :, :L], in_=chunked(hT)[:, 2 * g:2 * g + 2, :]), writes=[hgb[gi]], dma=True)
                for nch in range(NCH):
                    bk = n["a"] % 2
                    n["a"] += 1
                    for kc in range(2):
                        P.op("pe", lambda e, gi=gi, nch=nch, kc=kc, bk=bk: e.matmul(
                            banks[bk][:, :], hg[gi][:, kc, nch * 128:(nch + 1) * 128], cw[:, kc, :],
                            start=(kc == 0), stop=(kc == 1)), reads=[hgb[gi], cwb], writes=[bkb[bk]])
                    P.op("act" if nch % 2 == 0 else "dve",
                         (lambda e, nch=nch, bk=bk: e.activation(out=A[:, nch, :], in_=banks[bk][:, :], func=AF.Identity))
                         if nch % 2 == 0 else
                         (lambda e, nch=nch, bk=bk: e.tensor_copy(out=A[:, nch, :], in_=banks[bk][:, :])),
                         reads=[bkb[bk]], writes=[Ab[nch]])
                kblocks = [(k0, min(512, NK - k0)) for k0 in range(0, NK, 512)]
                for (k0, kw) in kblocks:
                    pb = [2 + 2 * (n["fo"] % 2), 3 + 2 * (n["fo"] % 2)]
                    nsub = (NCH + 7) // 8
                    for sb_ in range(nsub):
                        r0 = sb_ * 8
                        rn = min(8, NCH - r0)
                        mi = n["m"] % NB
                        n["m"] += 1
                        P.op("sp", lambda e, mi=mi, r0=r0, rn=rn, k0=k0, kw=kw, cl=cl: e.dma_start(
                            out=clt[mi][:, :rn, :kw],
                            in_=cl[r0 * 128:(r0 + rn) * 128, k0:k0 + kw].rearrange("(c p) n -> p c n", p=128)),
                            writes=[cltb[mi]], dma=True)
                        P.op("act", lambda e, mi=mi, r0=r0, rn=rn, k0=k0, kw=kw, sl=sl: e.dma_start(
                            out=slt[mi][:, :rn, :kw],
                            in_=sl[r0 * 128:(r0 + rn) * 128, k0:k0 + kw].rearrange("(c p) n -> p c n", p=128)),
                            writes=[sltb[mi]], dma=True)
                        for r in range(rn):
                            nch = r0 + r
                            for mcx in range(2):
                                P.op("pe", lambda e, mi=mi, r=r, nch=nch, mcx=mcx, kw=kw, pb=pb: e.matmul(
                                    banks[pb[mcx]][:, :kw], A[:, nch, mcx * 128:(mcx + 1) * 128], clt[mi][:, r, :kw],
                                    start=(nch == 0), stop=False), reads=[Ab[nch], cltb[mi]], writes=[bkb[pb[mcx]]])
                                P.op("pe", lambda e, mi=mi, r=r, nch=nch, mcx=mcx, kw=kw, pb=pb, NCH=NCH: e.matmul(
                                    banks[pb[mcx]][:, :kw], A[:, nch, 256 + mcx * 128:256 + (mcx + 1) * 128],
                                    slt[mi][:, r, :kw], start=False, stop=(nch == NCH - 1)),
                                    reads=[Ab[nch], sltb[mi]], writes=[bkb[pb[mcx]]])
                    for mcx in range(2):
                        fi = n["fo"] % 2
                        P.op("act", lambda e, fi=fi, mcx=mcx, kw=kw, pb=pb: e.activation(
                            out=fo2(fo, fi, mcx)[:, :kw], in_=banks[pb[mcx]][:, :kw],
                            func=AF.Identity), reads=[bkb[pb[mcx]]], writes=[fob2(fob, fi, mcx)])
                        c = 2 * g + mcx
                        outs.append(P.op("sp", lambda e, fi=fi, mcx=mcx, c=c, k0=k0, kw=kw, fT=fT: e.dma_start(
                            out=fT[c * 128:(c + 1) * 128, k0:k0 + kw], in_=fo2(fo, fi, mcx)[:, :kw]),
                            reads=[fob2(fob, fi, mcx)], dma=True))
                    n["fo"] += 1
        P.join("sp", outs)
        P.emit()
    return nc


def fo2(fo, fi, mcx):
    return fo[(2 * fi + mcx) % len(fo)]


def fob2(fob, fi, mcx):
    return fob[(2 * fi + mcx) % len(fob)]


GLA_HK = 1024
GLA_HV = 2048
GLA_IN = 6176
GLA_R = 16


def build_glaproj_program(T, tiles):
    W = max(w for _, w in tiles)
    nc = bass.Bass("TRN2", target_bir_lowering=False)
    hT = nc.dram_tensor("hT", [D, T], F32, kind="ExternalInput").ap()
    win = nc.dram_tensor("win", [D, GLA_IN], F32, kind="ExternalInput").ap()
    wgu = nc.dram_tensor("wgu", [GLA_R, 2 * GLA_HK], F32, kind="ExternalInput").ap()
    bg = nc.dram_tensor("bg", [128, 16], F32, kind="ExternalInput").ap()
    pT = nc.dram_tensor("pT", [6144, T], F32, kind="ExternalOutput").ap()
    gT = nc.dram_tensor("gT", [2 * GLA_HK, T], F32, kind="ExternalOutput").ap()
    with ExitStack() as es:
        P = Prog(nc, es)
        banks = [P.ps(f"bank{i}", [128, 512]) for i in range(8)]
        bkb = [P.buf() for i in range(8)]
        h = P.sb("h", [128, KC, W], BF16)
        hb = [P.buf() for c in range(KC)]
        wt = [P.sb(f"wt{i}", [128, KC, 128], BF16) for i in range(3)]
        wtb = [P.buf() for i in range(3)]
        wz = P.sb("wz", [128, KC, 32], BF16)
        wzb = P.buf()
        P.op("pool", lambda e: e.dma_start(out=wz[:], in_=chunked(win)[:, :, 6144:6176]), writes=[wzb], dma=True)
        wg = P.sb("wg", [GLA_R, 2 * GLA_HK], F32)
        wgb = P.buf()
        P.op("sp", lambda e: e.dma_start(out=wg[:], in_=wgu), writes=[wgb], dma=True)
        bgs = P.sb("bgs", [128, 16], F32)
        bgb = P.buf()
        P.op("sp", lambda e: e.dma_start(out=bgs[:], in_=bg), writes=[bgb], dma=True)
        z = [P.sb(f"z{i}", [GLA_R, W], F32) for i in range(2)]
        zb = [P.buf() for i in range(2)]
        ot = [P.sb(f"ot{i}", [128, W], F32) for i in range(3)]
        otb = [P.buf() for i in range(3)]
        sg = [P.sb(f"sgx{i}", [128, W], F32) for i in range(2)]
        sgb = [P.buf() for i in range(2)]
        n = {"w": 0, "o": 0, "b": 0, "s": 0}
        outs = []
        for (t0, w) in tiles:
            for half in range(2):
                cs = slice(half * 8, half * 8 + 8)
                P.op("pool", lambda e, cs=cs, t0=t0, w=w: e.dma_start(
                    out=h[:, cs, :w], in_=chunked(hT)[:, cs, t0:t0 + w]), writes=hb[cs], dma=True)
            for mc in range(48):
                i = n["w"] % 3
                n["w"] += 1
                P.op("pool", lambda e, i=i, mc=mc: e.dma_start(
                    out=wt[i][:], in_=chunked(win)[:, :, mc * 128:(mc + 1) * 128]), writes=[wtb[i]], dma=True)
                bk = n["b"] % 4
                n["b"] += 1
                for kc in range(KC):
                    P.op("pe", lambda e, i=i, kc=kc, bk=bk, w=w: e.matmul(
                        banks[bk][:, :w], wt[i][:, kc, :], h[:, kc, :w], start=(kc == 0), stop=(kc == KC - 1)),
                        reads=[wtb[i], hb[kc]], writes=[bkb[bk]])
                oi = n["o"] % 3
                n["o"] += 1
                if mc < 8:
                    fn = lambda e, oi=oi, bk=bk, w=w: e.activation(out=ot[oi][:, :w], in_=banks[bk][:, :w],
                                                                   func=AF.Identity, scale=1.0 / 16.0)
                    eng = "act"
                elif mc < 32:
                    fn = lambda e, oi=oi, bk=bk, w=w: e.tensor_copy(out=ot[oi][:, :w], in_=banks[bk][:, :w])
                    eng = "dve"
                else:
                    fn = lambda e, oi=oi, bk=bk, w=w: e.activation(out=ot[oi][:, :w], in_=banks[bk][:, :w],
                                                                   func=AF.Silu)
                    eng = "act"
                P.op(eng, fn, reads=[bkb[bk]], writes=[otb[oi]])
                outs.append(P.op("sp", lambda e, oi=oi, mc=mc, t0=t0, w=w: e.dma_start(
                    out=pT[mc * 128:(mc + 1) * 128, t0:t0 + w], in_=ot[oi][:, :w]), reads=[otb[oi]], dma=True))
            for d in range(2):
                bk = 4 + d
                for kc in range(KC):
                    P.op("pe", lambda e, d=d, kc=kc, bk=bk, w=w: e.matmul(
                        banks[bk][:GLA_R, :w], wz[:, kc, d * 16:(d + 1) * 16], h[:, kc, :w],
                        start=(kc == 0), stop=(kc == KC - 1)), reads=[wzb, hb[kc]], writes=[bkb[bk]])
                P.op("dve", lambda e, d=d, bk=bk, w=w: e.tensor_copy(out=z[d][:, :w], in_=banks[bk][:GLA_R, :w]),
                     reads=[bkb[bk]], writes=[zb[d]])
            for d in range(2):
                for j in range(8):
                    bk = 6 + (j % 2)
                    P.op("pe", lambda e, d=d, j=j, bk=bk, w=w: e.matmul(
                        banks[bk][:, :w], wg[:, d * GLA_HK + j * 128:d * GLA_HK + (j + 1) * 128], z[d][:, :w],
                        start=True, stop=True), reads=[wgb, zb[d]], writes=[bkb[bk]])
                    si = n["s"] % 2
                    n["s"] += 1
                    oi = n["o"] % 3
                    n["o"] += 1
                    P.op("act", lambda e, d=d, j=j, bk=bk, si=si, w=w: e.activation(
                        out=sg[si][:, :w], in_=banks[bk][:, :w], func=AF.Sigmoid,
                        bias=bgs[:, d * 8 + j:d * 8 + j + 1], scale=1.0), reads=[bkb[bk], bgb], writes=[sgb[si]])
                    P.op("act", lambda e, si=si, oi=oi, w=w: e.activation(
                        out=ot[oi][:, :w], in_=sg[si][:, :w], func=AF.Ln), reads=[sgb[si]], writes=[otb[oi]])
                    r = d * 8 + j
                    outs.append(P.op("sp", lambda e, oi=oi, r=r, t0=t0, w=w: e.dma_start(
                        out=gT[r * 128:(r + 1) * 128, t0:t0 + w], in_=ot[oi][:, :w]), reads=[otb[oi]], dma=True))
        P.join("sp", outs)
        P.emit()
    return nc


class Rot:
    def __init__(self, P, name, shape, dt, n):
        self.t = [P.sb(f"{name}{i}", shape, dt) for i in range(n)]
        self.b = [P.buf(f"{name}{i}") for i in range(n)]
        self.n = n
        self.k = 0

    def next(self):
        i = self.k % self.n
        self.k += 1
        return self.t[i], self.b[i]


def build_glascan_program(NU, NCH):
    S_ = NCH * 128
    nc = bass.Bass("TRN2", target_bir_lowering=False)
    qT = nc.dram_tensor("qT", [NU, 256, S_], F32, kind="ExternalInput").ap()
    kT = nc.dram_tensor("kT", [NU, 256, S_], F32, kind="ExternalInput").ap()
    vv = nc.dram_tensor("v", [NU, S_, 512], F32, kind="ExternalInput").ap()
    gg = nc.dram_tensor("g", [NU, S_, 256], F32, kind="ExternalInput").ap()
    tri_d = nc.dram_tensor("tri", [128, 128], F32, kind="ExternalInput").ap()
    mask_d = nc.dram_tensor("mask", [128, 128], F32, kind="ExternalInput").ap()
    ident_d = nc.dram_tensor("ident", [128, 128], BF16, kind="ExternalInput").ap()
    oo = nc.dram_tensor("o", [NU, S_, 512], F32, kind="ExternalOutput").ap()
    with ExitStack() as es:
        P = Prog(nc, es)
        bankB = P.ps("bankB", [128, 2, 128]); bBb = P.buf()
        bankA = P.ps("bankA", [128, 128]); bAb = P.buf()
        bankO = [P.ps(f"bankO{i}", [128, 512]) for i in range(2)]; bOb = [P.buf() for i in range(2)]
        bankT = P.ps("bankT", [128, 2, 128], BF16); bTb = P.buf()
        bankU = [P.ps(f"bankU{j}", [128, 512]) for j in range(2)]; bUb = [P.buf() for j in range(2)]
        tri = P.sb("tri_sb", [128, 128], F32); trib = P.buf()
        mask = P.sb("mask_sb", [128, 128], F32); maskb = P.buf()
        ident = P.sb("ident_sb", [128, 128], BF16); identb = P.buf()
        P.op("sp", lambda e: e.dma_start(out=tri[:], in_=tri_d), writes=[trib], dma=True)
        P.op("sp", lambda e: e.dma_start(out=mask[:], in_=mask_d), writes=[maskb], dma=True)
        P.op("sp", lambda e: e.dma_start(out=ident[:], in_=ident_d), writes=[identb], dma=True)
        Sst = P.sb("Sst", [128, 2, 512], F32); Sb = [P.buf() for j in range(2)]
        Sp = P.sb("Sp", [128, 2, 512], BF16); Spb = [P.buf() for j in range(2)]
        rq = Rot(P, "q", [128, 2, 128], F32, 3)
        rk = Rot(P, "k", [128, 2, 128], F32, 3)
        rv = Rot(P, "v", [128, 512], BF16, 3)
        rg = Rot(P, "g", [128, 256], F32, 3)
        rB = Rot(P, "Bsb", [128, 2, 128], F32, 2)
        rs = Rot(P, "sc", [128, 5, 2], F32, 2)
        rEq = Rot(P, "Eq", [128, 2, 128], F32, 2)
        rEk = Rot(P, "Ek", [128, 2, 128], F32, 2)
        rqi = Rot(P, "qi", [128, 2, 128], BF16, 2)
        rki = Rot(P, "ki", [128, 2, 128], BF16, 2)
        ram = Rot(P, "am", [128, 128], BF16, 2)
        rkit = Rot(P, "kit", [128, 2, 128], BF16, 2)
        ro = Rot(P, "osb", [128, 512], F32, 2)
        rtu = Rot(P, "tu", [128, 512], F32, 2)
        outs = []
        no = 0
        for u in range(NU):
            for j in range(2):
                P.op("dve", lambda e, j=j: e.memset(Sst[:, j, :], 0.0), writes=[Sb[j]])
            for n in range(NCH):
                ts = slice(n * 128, (n + 1) * 128)
                q, qb = rq.next(); k, kb = rk.next(); v, vb = rv.next(); g, gb = rg.next()
                P.op("sp", lambda e, q=q, u=u, ts=ts: e.dma_start(
                    out=q[:], in_=qT[u].rearrange("(j p) t -> p j t", p=128)[:, :, ts]), writes=[qb], dma=True)
                P.op("sp", lambda e, k=k, u=u, ts=ts: e.dma_start(
                    out=k[:], in_=kT[u].rearrange("(j p) t -> p j t", p=128)[:, :, ts]), writes=[kb], dma=True)
                P.op("pool", lambda e, v=v, u=u, ts=ts: e.dma_start(out=v[:], in_=vv[u, ts, :]), writes=[vb], dma=True)
                P.op("sp", lambda e, g=g, u=u, ts=ts: e.dma_start(out=g[:], in_=gg[u, ts, :]), writes=[gb], dma=True)
                for j in range(2):
                    P.op("pe", lambda e, g=g, j=j: e.matmul(bankB[:, j, :], g[:, j * 128:(j + 1) * 128], tri[:],
                                                           start=True, stop=True),
                         reads=[gb, trib], writes=[bBb])
                B, Bb = rB.next()
                P.op("act", lambda e, B=B: e.activation(out=B[:], in_=bankB[:], func=AF.Identity),
                     reads=[bBb], writes=[Bb])
                sc, scb = rs.next()
                P.op("dve", lambda e, B=B, sc=sc: e.tensor_scalar(
                    out=sc[:, 0, :], in0=B[:, :, 63], scalar1=-1.0, scalar2=None, op0=ALU.mult),
                    reads=[Bb], writes=[scb])
                P.op("dve", lambda e, B=B, sc=sc: e.tensor_tensor(
                    out=sc[:, 1, :], in0=B[:, :, 127], in1=sc[:, 0, :], op=ALU.add), reads=[Bb, scb], writes=[scb])
                P.op("act", lambda e, B=B, sc=sc: e.activation(out=sc[:, 2, :], in_=B[:, :, 63], func=AF.Exp),
                     reads=[Bb, scb], writes=[scb])
                P.op("act", lambda e, B=B, sc=sc: e.activation(out=sc[:, 3, :], in_=B[:, :, 127], func=AF.Exp),
                     reads=[Bb, scb], writes=[scb])
                P.op("act", lambda e, sc=sc: e.activation(out=sc[:, 4, :], in_=sc[:, 1, :], func=AF.Exp),
                     reads=[scb], writes=[scb])
                Eq, Eqb = rEq.next(); Ek, Ekb = rEk.next()
                for j in range(2):
                    P.op("act", lambda e, B=B, sc=sc, Eq=Eq, j=j: e.activation(
                        out=Eq[:, j, :], in_=B[:, j, :], func=AF.Exp, bias=sc[:, 0, j:j + 1], scale=1.0),
                        reads=[Bb, scb], writes=[Eqb])
                    P.op("act", lambda e, B=B, Ek=Ek, j=j: e.activation(
                        out=Ek[:, j, :], in_=B[:, j, :], func=AF.Exp, bias=B[:, j, 63:64], scale=-1.0),
                        reads=[Bb], writes=[Ekb])
                qi, qib = rqi.next(); ki, kib = rki.next()
                P.op("dve", lambda e, q=q, Eq=Eq, qi=qi: e.tensor_tensor(out=qi[:], in0=q[:], in1=Eq[:], op=ALU.mult),
                     reads=[qb, Eqb], writes=[qib])
                P.op("dve", lambda e, k=k, Ek=Ek, ki=ki: e.tensor_tensor(out=ki[:], in0=k[:], in1=Ek[:], op=ALU.mult),
                     reads=[kb, Ekb], writes=[kib])
                for j in range(2):
                    P.op("act", lambda e, sc=sc, j=j: e.activation(
                        out=Sp[:, j, :], in_=Sst[:, j, :], func=AF.Identity, scale=sc[:, 2, j:j + 1]),
                        reads=[Sb[j], scb], writes=[Spb[j]])
                for j in range(2):
                    P.op("pe", lambda e, ki=ki, qi=qi, j=j: e.matmul(bankA[:], ki[:, j, :], qi[:, j, :],
                                                                    start=(j == 0), stop=(j == 1)),
                         reads=[kib, qib], writes=[bAb])
                am, amb = ram.next()
                P.op("dve", lambda e, am=am: e.tensor_tensor(out=am[:], in0=bankA[:], in1=mask[:], op=ALU.mult),
                     reads=[bAb, maskb], writes=[amb])
                oi = no % 2
                no += 1
                P.op("pe", lambda e, am=am, v=v, oi=oi: e.matmul(bankO[oi][:], am[:], v[:], start=True, stop=False),
                     reads=[amb, vb], writes=[bOb[oi]])
                for j in range(2):
                    P.op("pe", lambda e, qi=qi, j=j, oi=oi: e.matmul(bankO[oi][:], qi[:, j, :], Sp[:, j, :],
                                                                    start=False, stop=(j == 1)),
                         reads=[qib, Spb[j]], writes=[bOb[oi]])
                osb, osbb = ro.next()
                P.op("act", lambda e, osb=osb, oi=oi: e.activation(out=osb[:], in_=bankO[oi][:], func=AF.Identity),
                     reads=[bOb[oi]], writes=[osbb])
                outs.append(P.op("sp", lambda e, osb=osb, u=u, ts=ts: e.dma_start(out=oo[u, ts, :], in_=osb[:]),
                                 reads=[osbb], dma=True))
                for j in range(2):
                    P.op("pe", lambda e, ki=ki, j=j: e.transpose(out=bankT[:, j, :], in_=ki[:, j, :], identity=ident[:]),
                         reads=[kib, identb], writes=[bTb])
                kit, kitb = rkit.next()
                P.op("dve", lambda e, kit=kit: e.tensor_copy(out=kit[:], in_=bankT[:]), reads=[bTb], writes=[kitb])
                for j in range(2):
                    P.op("pe", lambda e, kit=kit, v=v, j=j: e.matmul(bankU[j][:], kit[:, j, :], v[:],
                                                                    start=True, stop=True),
                         reads=[kitb, vb], writes=[bUb[j]])
                    tu, tub = rtu.next()
                    P.op("act", lambda e, tu=tu, sc=sc, j=j: e.activation(
                        out=tu[:], in_=bankU[j][:], func=AF.Identity, scale=sc[:, 4, j:j + 1]),
                        reads=[bUb[j], scb], writes=[tub])
                    P.op("dve", lambda e, tu=tu, sc=sc, j=j: e.scalar_tensor_tensor(
                        out=Sst[:, j, :], in0=Sst[:, j, :], scalar=sc[:, 3, j:j + 1], in1=tu[:],
                        op0=ALU.mult, op1=ALU.add), reads=[Sb[j], tub, scb], writes=[Sb[j]])
        P.join("sp", outs)
        P.emit()
    return nc


import ml_dtypes

BATCH = 4
SEQ = 4096
CTX = 256
DEPTH = 4
HALF = SEQ // 2
CH = CTX // 2
TL = HALF + CH
CORES = list(range(NCORES))
_PROGS = {}
_DUMP = None


def _prog(key, builder):
    if key not in _PROGS:
        _PROGS[key] = builder()
    return _PROGS[key]


def _run(nc, in_maps):
    res = run_bass_kernel_spmd(nc, in_maps, core_ids=CORES)
    return res.results


def _fm(v):
    return np.ascontiguousarray(np.asarray(v, np.float32).reshape(-1, 128).T)


def _mods(sets):
    return np.ascontiguousarray(
        np.concatenate([_fm(v) for s in sets for v in s], axis=1))


def _cat(parts):
    return np.ascontiguousarray(np.concatenate(parts, axis=1))


TILES_L = [(0, 512, 0), (512, 512, 0), (1024, 512, 0), (1536, 512, 0)]
TILES_LC = TILES_L + [(2048, 128, 1)]


def kernel_unfused(x, c, ctx, c_ctx, w_mod, b_mod, norm_g, ffn_w_gate, ffn_w_up, ffn_w_down,
           gla_w_in, gla_w_gate_up, gla_b_gate, gla_g_head, gla_w_out,
           fnet_w_out, fnet_b_out,
           cm_w_pw1, cm_b_pw1, cm_w_dw, cm_b_dw, cm_ln_g, cm_ln_b, cm_w_pw2, cm_b_pw2,
           final_g):
    f32 = lambda a: np.asarray(a, dtype=np.float32)
    x, c, ctx, c_ctx = f32(x), f32(c), f32(ctx), f32(c_ctx)
    w_mod, b_mod, norm_g = f32(w_mod), f32(b_mod), f32(norm_g)
    ffn_w_gate, ffn_w_up, ffn_w_down = f32(ffn_w_gate), f32(ffn_w_up), f32(ffn_w_down)
    gla_w_in, gla_w_gate_up, gla_b_gate = f32(gla_w_in), f32(gla_w_gate_up), f32(gla_b_gate)
    gla_g_head, gla_w_out = f32(gla_g_head), f32(gla_w_out)
    fnet_w_out, fnet_b_out = f32(fnet_w_out), f32(fnet_b_out)
    cm_w_pw1, cm_b_pw1, cm_w_dw, cm_b_dw = f32(cm_w_pw1), f32(cm_b_pw1), f32(cm_w_dw), f32(cm_b_dw)
    cm_ln_g, cm_ln_b, cm_w_pw2, cm_b_pw2 = f32(cm_ln_g), f32(cm_ln_b), f32(cm_w_pw2), f32(cm_b_pw2)
    final_g = f32(final_g)
    zeros_d = np.zeros(D, np.float32)

    sc = np.concatenate([c, c_ctx[None]], axis=0)
    scT = np.ascontiguousarray(sc.reshape(NROW, KC, 128).transpose(2, 1, 0)).reshape(128, KC * NROW)
    nc_mod = _prog("mod", lambda: build_mod_program(DEPTH))
    ims = [{"scT": scT,
            "wm": np.ascontiguousarray(w_mod[:, :, k * MODC:(k + 1) * MODC]),
            "bm": np.ascontiguousarray(b_mod[:, k * MODC:(k + 1) * MODC])} for k in CORES]
    r = _run(nc_mod, ims)
    mod = np.concatenate([r[k]["out"] for k in CORES], axis=2).reshape(DEPTH, NROW, NMOD, D)

    xT = []
    for k in CORES:
        b, hf = k // 2, k % 2
        xT.append(_cat([x[b, hf * HALF:(hf + 1) * HALF].T, ctx[b, hf * CH:(hf + 1) * CH].T]))

    out = np.zeros((BATCH, SEQ, D), np.float32)
    for i in range(DEPTH):
        kind, j, last = i % 3, i // 3, i == DEPTH - 1
        rows = lambda k: (k // 2, BATCH)
        nc_head = _prog("head", lambda: build_ffn_program(TL, TILES_LC, 2, 1, False, True, False))
        ims = []
        for k in CORES:
            sets = [[norm_g[i, 0], mod[i, rw, 0], mod[i, rw, 1], mod[i, rw, 2],
                     norm_g[i, 1], mod[i, rw, 3], mod[i, rw, 4]] for rw in rows(k)]
            ims.append({"xT": xT[k], "mods": _mods(sets), "wg0": ffn_w_gate[i, 0], "wu0": ffn_w_up[i, 0],
                        "wd0": ffn_w_down[i, 0]})
        r = _run(nc_head, ims)
        xT = [r[k]["oT"] for k in CORES]
        hT = [r[k]["hT"] for k in CORES]
        H_c = [_cat([hT[2 * b][:, HALF:], hT[2 * b + 1][:, HALF:]]) for b in range(BATCH)]

        tail_extra = [dict() for _ in CORES]
        if kind == 0:
            nc_gp = _prog("gproj", lambda: build_glaproj_program(TL, [(t0, w) for t0, w, _ in TILES_LC]))
            wgu = _cat([gla_w_gate_up[j, 0], gla_w_gate_up[j, 1]])
            bg = _fm(gla_b_gate[j].reshape(-1))
            r = _run(nc_gp, [{"hT": hT[k], "win": gla_w_in[j], "wgu": wgu, "bg": bg} for k in CORES])
            pT = [r[k]["pT"] for k in CORES]
            gT = [r[k]["gT"] for k in CORES]
            P_l = [_cat([pT[2 * b][:, :HALF], pT[2 * b + 1][:, :HALF]]) for b in range(BATCH)]
            P_c = [_cat([pT[2 * b][:, HALF:], pT[2 * b + 1][:, HALF:]]) for b in range(BATCH)]
            G_l = [_cat([gT[2 * b][:, :HALF], gT[2 * b + 1][:, :HALF]]) for b in range(BATCH)]
            G_c = [_cat([gT[2 * b][:, HALF:], gT[2 * b + 1][:, HALF:]]) for b in range(BATCH)]
            NCH = (CTX + SEQ) // 128
            nc_sc = _prog("gscan", lambda: build_glascan_program(4, NCH))
            s_, t_ = np.arange(128)[:, None], np.arange(128)[None, :]
            maskc = (s_ <= t_).astype(np.float32)
            tric = (maskc / 16.0).astype(np.float32)
            identc = np.eye(128).astype(ml_dtypes.bfloat16)

            def seq(ac, al, d):
                if d == 0:
                    return np.concatenate([ac, al], axis=1)
                return np.concatenate([ac[:, ::-1], al[:, ::-1]], axis=1)

            ims = []
            for k in CORES:
                b, hf = k // 2, k % 2
                qs, ks, vs, gs = [], [], [], []
                for hl in range(2):
                    h = 2 * hf + hl
                    for d in range(2):
                        qs.append(seq(P_c[b][h * 256:(h + 1) * 256], P_l[b][h * 256:(h + 1) * 256], d))
                        ks.append(seq(P_c[b][1024 + h * 256:1024 + (h + 1) * 256],
                                      P_l[b][1024 + h * 256:1024 + (h + 1) * 256], d))
                        vs.append(seq(P_c[b][2048 + h * 512:2048 + (h + 1) * 512],
                                      P_l[b][2048 + h * 512:2048 + (h + 1) * 512], d).T)
                        gs.append(seq(G_c[b][d * 1024 + h * 256:d * 1024 + (h + 1) * 256],
                                      G_l[b][d * 1024 + h * 256:d * 1024 + (h + 1) * 256], d).T)
                ims.append({"qT": np.ascontiguousarray(np.stack(qs)), "kT": np.ascontiguousarray(np.stack(ks)),
                            "v": np.ascontiguousarray(np.stack(vs)), "g": np.ascontiguousarray(np.stack(gs)),
                            "tri": tric, "mask": maskc, "ident": identc})
            r = _run(nc_sc, ims)
            O_l = [[np.zeros((D, SEQ), np.float32) for _ in range(2)] for _ in range(BATCH)]
            O_c = [[np.zeros((D, CTX), np.float32) for _ in range(2)] for _ in range(BATCH)]
            for k in CORES:
                b, hf = k // 2, k % 2
                o = r[k]["o"]
                for hl in range(2):
                    h = 2 * hf + hl
                    for d in range(2):
                        ou = o[hl * 2 + d]
                        oc, ol = ou[:CTX], ou[CTX:]
                        if d == 1:
                            oc, ol = oc[::-1], ol[::-1]
                        O_l[b][d][h * 512:(h + 1) * 512] = ol.T
                        O_c[b][d][h * 512:(h + 1) * 512] = oc.T
            for k in CORES:
                b, hf = k // 2, k % 2
                def tok(al, ac):
                    if last:
                        return np.ascontiguousarray(al[:, hf * HALF:(hf + 1) * HALF])
                    return _cat([al[:, hf * HALF:(hf + 1) * HALF], ac[:, hf * CH:(hf + 1) * CH]])
                rs = pT[k][4096:6144]
                tail_extra[k] = {"ofT": tok(O_l[b][0], O_c[b][0]), "obT": tok(O_l[b][1], O_c[b][1]),
                                 "rsT": np.ascontiguousarray(rs[:, :HALF] if last else rs),
                                 "gh": _fm(gla_g_head[j]), "wo": gla_w_out[j]}
            bias_vec = zeros_d
        elif kind == 1:
            seqs = [("l", SEQ, HALF), ("c", CTX, CTX)]
            nc_fn = _prog("fnet", lambda: build_fnet_program(seqs))
            m_ = np.arange(GW)
            ang = 2 * np.pi * np.outer(m_, m_) / GW
            cwf = np.concatenate([np.cos(ang), np.sin(ang)], axis=1)
            cwc = np.ascontiguousarray(cwf.reshape(2, 128, 512).transpose(1, 0, 2)).reshape(128, 1024) \
                .astype(ml_dtypes.bfloat16)

            def dft(L, k0, nk):
                n_ = np.arange(L, dtype=np.int64)[:, None]
                k_ = (np.arange(nk, dtype=np.int64) + k0)[None, :]
                a = 2 * np.pi * ((n_ * k_) % L).astype(np.float64) / L
                nrm = 1.0 / np.sqrt(L * GW)
                return (np.cos(a) * nrm).astype(ml_dtypes.bfloat16), (-np.sin(a) * nrm).astype(ml_dtypes.bfloat16)

            dl = [dft(SEQ, hf * HALF, HALF) for hf in range(2)]
            dc = dft(CTX, 0, CTX)
            H_l = [_cat([hT[2 * b][:, :HALF], hT[2 * b + 1][:, :HALF]]) for b in range(BATCH)]
            ims = []
            for k in CORES:
                b, hf = k // 2, k % 2
                ims.append({"cw": cwc, "hT_l": H_l[b], "cl_l": dl[hf][0], "sl_l": dl[hf][1],
                            "hT_c": H_c[b], "cl_c": dc[0], "sl_c": dc[1]})
            r = _run(nc_fn, ims)
            for k in CORES:
                hf = k % 2
                tail_extra[k] = {"mT": _cat([r[k]["fT_l"], r[k]["fT_c"][:, hf * CH:(hf + 1) * CH]]),
                                 "wo": fnet_w_out[j]}
            bias_vec = fnet_b_out[j]
        else:
            tiles_cv = [(t0, w, 64) for t0, w, _ in TILES_L] + [(HALF, CTX, CTX)]
            nc_cv = _prog("conv", lambda: build_conv_program(HALF + CTX, tiles_cv))
            vecs = np.ascontiguousarray(np.concatenate(
                [_fm(cm_b_pw1[j][:D]), _fm(cm_b_pw1[j][D:]), _fm(cm_b_dw[j]), _fm(cm_ln_g[j]), _fm(cm_ln_b[j])], axis=1))
            wk = np.ascontiguousarray(cm_w_dw[j].T.reshape(KC, 128, CONV_W).transpose(1, 0, 2)).reshape(128, KC * CONV_W)
            ims = []
            for k in CORES:
                b = k // 2
                ims.append({"hT": _cat([hT[k][:, :HALF], H_c[b]]), "w1": cm_w_pw1[j], "vecs": vecs, "wdw": wk})
            r = _run(nc_cv, ims)
            for k in CORES:
                hf = k % 2
                m = r[k]["mT"]
                tail_extra[k] = {"mT": _cat([m[:, :HALF], m[:, HALF + hf * CH:HALF + (hf + 1) * CH]]),
                                 "wo": cm_w_pw2[j]}
            bias_vec = cm_b_pw2[j]

        gla = kind == 0
        if last:
            nc_tail = _prog(("tail_last", gla), lambda: build_ffn_program(HALF, TILES_L, 1, 1, True, False, True, gla_in=gla))
        else:
            nc_tail = _prog(("tail", gla), lambda: build_ffn_program(TL, TILES_LC, 2, 1, True, False, False, gla_in=gla))
        ims = []
        for k in CORES:
            rws = rows(k)[:1] if last else rows(k)
            sets = []
            for rw in rws:
                s = [norm_g[i, 2], mod[i, rw, 6], mod[i, rw, 7], mod[i, rw, 8], mod[i, rw, 5], bias_vec]
                if last:
                    s.append(final_g)
                sets.append(s)
            im = {"xT": np.ascontiguousarray(xT[k][:, :HALF]) if last else xT[k], "mods": _mods(sets),
                  "wg0": ffn_w_gate[i, 1], "wu0": ffn_w_up[i, 1], "wd0": ffn_w_down[i, 1]}
            im.update(tail_extra[k])
            ims.append(im)
        r = _run(nc_tail, ims)
        xT = [r[k]["oT"] for k in CORES]
        if _DUMP is not None:
            _DUMP(i, xT)

    for k in CORES:
        b, hf = k // 2, k % 2
        out[b, hf * HALF:(hf + 1) * HALF] = xT[k].T
    return out


TT = HALF + CTX
FT_L = [(0, 512, 0), (512, 512, 0), (1024, 512, 0), (1536, 512, 0)]
FT_LC = FT_L + [(HALF, CTX, 1)]
NCHK = TT // 128


class Shared:
    pass


def fused_common(P, R, W=512):
    R.sq = [P.sb("sq", [128, W], F32) for i in range(2)]
    R.sq_b = [P.buf() for i in range(2)]
    R.rstd = P.sb("rstd", [128, W], F32)
    R.rstd_b = P.buf()
    R.tmp = [P.sb("tmp", [128, W], F32) for i in range(2)]
    R.tmp_b = [P.buf() for i in range(2)]
    R.n_sq = 0
    R.n_tmp = 0
    R.P = P


def emit_vec_copy(P, R, dst, src, srcb):
    return P.op("dve", lambda e: e.tensor_copy(out=dst, in_=src), reads=[srcb], writes=[R.wvb])


def emit_ffn_phase(P, R, tiles, x_src, x_dst, sets, ffn_w, proj=None, h_dst=None, final_dst=None):
    W = 512
    fused_common(P, R)
    C = R
    n_sets = len(sets)
    NV = 11
    wv = P.sb("wv", [128, n_sets * NV * KC], F32)
    R.wvb = P.buf("wv")
    mvb = R.wvb

    def vec(s, k):
        o = (s * NV + k) * KC
        return wv[:, o:o + KC]

    for s, st in enumerate(sets):
        items = []
        if ffn_w is not None:
            items += list(zip(range(0, 4), st["ffn"]))
        if proj is not None:
            items += list(zip(range(4, 6), st["proj"]))
        if h_dst is not None:
            items += list(zip(range(6, 9), st["h"]))
        if final_dst is not None:
            items += [(9, st["fin"])]
        for k, src in items:
            emit_vec_copy(P, R, vec(s, k), src, R.persist_b)
        norm_slots = ([0] if ffn_w is not None else []) + ([6] if h_dst is not None else [])
        for k in norm_slots:
            P.op("dve", lambda e, s=s, k=k: e.scalar_tensor_tensor(
                out=vec(s, k), in0=vec(s, k + 2), scalar=1.0, in1=vec(s, k), op0=ALU.add, op1=ALU.mult),
                reads=[mvb], writes=[mvb])
        if ffn_w is not None:
            P.op("dve", lambda e, s=s: e.tensor_scalar(
                out=vec(s, 3), in0=vec(s, 3), scalar1=0.5, scalar2=None, op0=ALU.mult), reads=[mvb], writes=[mvb])

    x = P.sb("x", [128, KC, W], F32)
    xb = [P.buf() for c in range(KC)]
    h = P.sb("h", [128, KC, W], BF16)
    hb = [P.buf() for c in range(KC)]
    if ffn_w is not None:
        wg, wu, wd = ffn_w
        a = P.sb("a", [128, FC, W], BF16)
        ab = [P.buf() for c in range(FC)]
        sg = [P.sb("sg", [128, W], F32) for i in range(2)]
        sgb = [P.buf() for i in range(2)]
        FB, NWB, DFB, NDB = 2, 2, 4, 3
        wgt = [P.sb("wgt", [128, KC, FB * 128], BF16) for i in range(NWB)]
        wgb = [P.buf() for i in range(NWB)]
        wut = [P.sb("wut", [128, KC, FB * 128], BF16) for i in range(NWB)]
        wub = [P.buf() for i in range(NWB)]
        wdt = [P.sb("wdt", [128, DFB, 512], BF16) for i in range(NDB)]
        wdb = [P.buf() for i in range(NDB)]
    if h_dst is not None or final_dst is not None:
        ho = [P.sb("ho", [128, W], F32) for i in range(2)]
        hob = [P.buf() for i in range(2)]
    if proj is not None:
        wot = [P.sb("wot", [128, KC, 128], BF16) for i in range(2)]
        wotb = [P.buf() for i in range(2)]
        if proj["kind"] == "m":
            m32 = [P.sb("m32", [128, W], F32) for i in range(2)]
            m32b = [P.buf() for i in range(2)]
        else:
            roa = Rot(P, "oa", [128, 512], F32, 2)
            rob = Rot(P, "ob", [128, 512], F32, 2)
            rrs = Rot(P, "rs", [128, 512], F32, 2)
            rsq = Rot(P, "gsq", [128, 512], F32, 1)
            rss = Rot(P, "gss", [128, 2], F32, 2)
            rmb = Rot(P, "gmb", [128, 512], BF16, 2)
            bT = R.banks[7][:, 0:256].bitcast(BF16).rearrange("p (a b) -> p a b", b=128)
    cnt = {"w": 0, "d": 0, "sg": 0, "ho": 0, "m": 0, "wo": 0}

    for (t0, w, s) in tiles:
        for half in range(2):
            cs = slice(half * 8, half * 8 + 8)
            P.op("sp", lambda e, cs=cs, t0=t0, w=w: e.dma_start(
                out=x[:, cs, :w], in_=chunked(x_src)[:, cs, t0:t0 + w]), writes=xb[cs], dma=True)
        if proj is not None:
            if proj["kind"] == "m":
                mT = proj["m"]
                for c in range(KC):
                    i = cnt["m"] % 2
                    cnt["m"] += 1
                    P.op("sp", lambda e, c=c, i=i, t0=t0, w=w: e.dma_start(
                        out=m32[i][:, :w], in_=mT[c * 128:(c + 1) * 128, t0:t0 + w]), writes=[m32b[i]], dma=True)
                    P.op("act", lambda e, c=c, i=i, w=w: e.activation(out=h[:, c, :w], in_=m32[i][:, :w],
                                                                     func=AF.Identity),
                         reads=[m32b[i]], writes=[hb[c]])
            else:
                OA, OB, RS, gh = proj["oa"], proj["ob"], proj["rs"], proj["gh"]
                for tc in range(w // 128):
                    r0 = t0 + tc * 128
                    for hd in range(4):
                        oa, oab = roa.next(); ob_, obb = rob.next(); rs_, rsb = rrs.next()
                        P.op("sp", lambda e, oa=oa, r0=r0, hd=hd: e.dma_start(
                            out=oa[:], in_=OA[r0:r0 + 128, hd * 512:(hd + 1) * 512]), writes=[oab], dma=True)
                        P.op("sp", lambda e, ob_=ob_, r0=r0, hd=hd: e.dma_start(
                            out=ob_[:], in_=OB[r0:r0 + 128, hd * 512:(hd + 1) * 512]), writes=[obb], dma=True)
                        P.op("act", lambda e, rs_=rs_, r0=r0, hd=hd: e.dma_start(
                            out=rs_[:], in_=RS[r0:r0 + 128, hd * 512:(hd + 1) * 512]), writes=[rsb], dma=True)
                        P.op("dve", lambda e, oa=oa, ob_=ob_: e.tensor_tensor(out=oa[:], in0=oa[:], in1=ob_[:], op=ALU.add),
                             reads=[oab, obb], writes=[oab])
                        sqt, sqtb = rsq.next(); ss, ssb = rss.next()
                        P.op("act", lambda e, oa=oa, sqt=sqt: e.activation(out=sqt[:], in_=oa[:], func=AF.Square),
                             reads=[oab], writes=[sqtb])
                        P.op("dve", lambda e, sqt=sqt, ss=ss: e.reduce_sum(out=ss[:, 0:1], in_=sqt[:],
                                                                         axis=mybir.AxisListType.X),
                             reads=[sqtb], writes=[ssb])
                        P.op("act", lambda e, ss=ss: e.activation(out=ss[:, 1:2], in_=ss[:, 0:1], func=AF.Sqrt,
                                                                 bias=R.epsb[:, 0:1], scale=1.0 / 512.0),
                             reads=[ssb, R.persist_b], writes=[ssb])
                        P.op("dve", lambda e, ss=ss: e.reciprocal(out=ss[:, 1:2], in_=ss[:, 1:2]),
                             reads=[ssb], writes=[ssb])
                        P.op("dve", lambda e, oa=oa, ss=ss: e.scalar_tensor_tensor(
                            out=oa[:], in0=oa[:], scalar=ss[:, 1:2], in1=gh[:], op0=ALU.mult, op1=ALU.mult),
                            reads=[oab, ssb, R.persist_b], writes=[oab])
                        mb_, mbb = rmb.next()
                        P.op("dve", lambda e, oa=oa, rs_=rs_, mb_=mb_: e.tensor_tensor(
                            out=mb_[:], in0=oa[:], in1=rs_[:], op=ALU.mult), reads=[oab, rsb], writes=[mbb])
                        for cc in range(4):
                            P.op("pe", lambda e, mb_=mb_, cc=cc: e.transpose(
                                out=bT[:, cc, :], in_=mb_[:, cc * 128:(cc + 1) * 128], identity=R.ident[:]),
                                reads=[mbb, R.persist_b], writes=[R.bank_b[7]])
                        hsl = hb[4 * hd:4 * hd + 4]
                        P.op("act", lambda e, hd=hd, tc=tc: e.activation(
                            out=h[:, 4 * hd:4 * hd + 4, tc * 128:(tc + 1) * 128], in_=bT[:], func=AF.Identity),
                            reads=[R.bank_b[7]], writes=hsl)
            wo = proj["wo"]
            for dc in range(KC):
                i = cnt["wo"] % 2
                cnt["wo"] += 1
                P.op("pool", lambda e, dc=dc, i=i: e.dma_start(
                    out=wot[i][:], in_=wo[dc].rearrange("p (c n) -> p c n", n=128)), writes=[wotb[i]], dma=True)
                bank = 4 + (dc % 3)
                for kc in range(KC):
                    P.op("pe", lambda e, kc=kc, i=i, bank=bank, w=w: e.matmul(
                        R.banks[bank][:, :w], wot[i][:, kc, :], h[:, kc, :w], start=(kc == 0), stop=(kc == KC - 1)),
                        reads=[wotb[i], hb[kc]], writes=[R.bank_b[bank]])
                ti = C.n_tmp % 2
                C.n_tmp += 1
                P.op("act", lambda e, dc=dc, ti=ti, bank=bank, w=w, s=s: e.activation(
                    out=C.tmp[ti][:, :w], in_=R.banks[bank][:, :w], func=AF.Identity,
                    bias=vec(s, 5)[:, dc:dc + 1], scale=1.0), reads=[R.bank_b[bank], mvb], writes=[C.tmp_b[ti]])
                P.op("dve", lambda e, dc=dc, ti=ti, w=w, s=s: e.scalar_tensor_tensor(
                    out=x[:, dc, :w], in0=C.tmp[ti][:, :w], scalar=vec(s, 4)[:, dc:dc + 1], in1=x[:, dc, :w],
                    op0=ALU.mult, op1=ALU.add), reads=[C.tmp_b[ti], mvb, xb[dc]], writes=[xb[dc]])

        if ffn_w is not None:
            emit_rstd(C, x, xb, w, 0)
            emit_prenorm(C, x, xb, w, vec(s, 0), vec(s, 1), mvb, lambda c, w=w: h[:, c, :w], hb)
            for fb in range(FC // FB):
                i = cnt["w"] % NWB
                cnt["w"] += 1
                fsl = slice(fb * FB * 128, (fb + 1) * FB * 128)
                P.op("pool", lambda e, i=i, fb=fb: e.dma_start(
                    out=wgt[i][:], in_=wg[fb].rearrange("p (c n) -> p c n", n=FB * 128)), writes=[wgb[i]], dma=True)
                P.op("pool", lambda e, i=i, fb=fb: e.dma_start(
                    out=wut[i][:], in_=wu[fb].rearrange("p (c n) -> p c n", n=FB * 128)), writes=[wub[i]], dma=True)
                for f in range(FB):
                    fc = fb * FB + f
                    par = fc % 2
                    gb, ub = 2 * par, 2 * par + 1
                    for kc in range(KC):
                        P.op("pe", lambda e, i=i, f=f, kc=kc, gb=gb, w=w: e.matmul(
                            R.banks[gb][:, :w], wgt[i][:, kc, f * 128:(f + 1) * 128], h[:, kc, :w],
                            start=(kc == 0), stop=(kc == KC - 1)), reads=[wgb[i], hb[kc]], writes=[R.bank_b[gb]])
                    for kc in range(KC):
                        P.op("pe", lambda e, i=i, f=f, kc=kc, ub=ub, w=w: e.matmul(
                            R.banks[ub][:, :w], wut[i][:, kc, f * 128:(f + 1) * 128], h[:, kc, :w],
                            start=(kc == 0), stop=(kc == KC - 1)), reads=[wub[i], hb[kc]], writes=[R.bank_b[ub]])
                    si = cnt["sg"] % 2
                    cnt["sg"] += 1
                    P.op("act", lambda e, si=si, gb=gb, w=w: e.activation(
                        out=sg[si][:, :w], in_=R.banks[gb][:, :w], func=AF.Silu),
                        reads=[R.bank_b[gb]], writes=[sgb[si]])
                    P.op("dve", lambda e, si=si, ub=ub, fc=fc, w=w: e.tensor_tensor(
                        out=a[:, fc, :w], in0=sg[si][:, :w], in1=R.banks[ub][:, :w], op=ALU.mult),
                        reads=[sgb[si], R.bank_b[ub]], writes=[ab[fc]])
            for dg in range(4):
                base = 4 if dg % 2 == 0 else 0
                for fb in range(FC // DFB):
                    i = cnt["d"] % NDB
                    cnt["d"] += 1
                    P.op("pool", lambda e, i=i, fb=fb, dg=dg: e.dma_start(
                        out=wdt[i][:], in_=wd[dg, fb].rearrange("p (c n) -> p c n", n=512)),
                        writes=[wdb[i]], dma=True)
                    for f in range(DFB):
                        fc = fb * DFB + f
                        for dc in range(4):
                            P.op("pe", lambda e, i=i, f=f, fc=fc, dc=dc, base=base, w=w: e.matmul(
                                R.banks[base + dc][:, :w], wdt[i][:, f, dc * 128:(dc + 1) * 128], a[:, fc, :w],
                                start=(fc == 0), stop=(fc == FC - 1)),
                                reads=[wdb[i], ab[fc]], writes=[R.bank_b[base + dc]])
                for dc in range(4):
                    c = dg * 4 + dc
                    P.op("dve", lambda e, c=c, dc=dc, base=base, w=w, s=s: e.scalar_tensor_tensor(
                        out=x[:, c, :w], in0=R.banks[base + dc][:, :w], scalar=vec(s, 3)[:, c:c + 1],
                        in1=x[:, c, :w], op0=ALU.mult, op1=ALU.add),
                        reads=[R.bank_b[base + dc], mvb, xb[c]], writes=[xb[c]])

        if final_dst is not None:
            emit_rstd(C, x, xb, w, 0)
            for c in range(KC):
                i = cnt["ho"] % 2
                cnt["ho"] += 1
                P.op("dve", lambda e, c=c, i=i, w=w, s=s: e.scalar_tensor_tensor(
                    out=ho[i][:, :w], in0=x[:, c, :w], scalar=vec(s, 9)[:, c:c + 1], in1=C.rstd[:, :w],
                    op0=ALU.mult, op1=ALU.mult), reads=[xb[c], C.rstd_b, mvb], writes=[hob[i]])
                R.final_dmas.append(P.op("sp", lambda e, c=c, i=i, t0=t0, w=w: e.dma_start(
                    out=final_dst[c * 128:(c + 1) * 128, t0:t0 + w], in_=ho[i][:, :w]), reads=[hob[i]], dma=True))
        else:
            for half in range(2):
                cs = slice(half * 8, half * 8 + 8)
                P.op("sp", lambda e, cs=cs, t0=t0, w=w: e.dma_start(
                    out=chunked(x_dst)[:, cs, t0:t0 + w], in_=x[:, cs, :w]), reads=xb[cs], dma=True)
        if h_dst is not None:
            emit_rstd(C, x, xb, w, 0)
            for c in range(KC):
                i = cnt["ho"] % 2
                cnt["ho"] += 1
                ti = C.n_tmp % 2
                C.n_tmp += 1
                P.op("dve", lambda e, c=c, ti=ti, w=w, s=s: e.scalar_tensor_tensor(
                    out=C.tmp[ti][:, :w], in0=x[:, c, :w], scalar=vec(s, 6)[:, c:c + 1], in1=C.rstd[:, :w],
                    op0=ALU.mult, op1=ALU.mult), reads=[xb[c], C.rstd_b, mvb], writes=[C.tmp_b[ti]])
                P.op("act", lambda e, c=c, i=i, ti=ti, w=w, s=s: e.activation(
                    out=ho[i][:, :w], in_=C.tmp[ti][:, :w], func=AF.Identity, bias=vec(s, 7)[:, c:c + 1],
                    scale=1.0), reads=[C.tmp_b[ti], mvb], writes=[hob[i]])
                P.op("sp", lambda e, c=c, i=i, t0=t0, w=w: e.dma_start(
                    out=h_dst[c * 128:(c + 1) * 128, t0:t0 + w], in_=ho[i][:, :w]), reads=[hob[i]], dma=True)
    P.barrier()


def emit_mod_phase(P, R, sc2T_d, wm_d, bm_d, depth):
    sc = P.sb("sc", [128, KC * 2], F32)
    scb = P.buf()
    ones1 = P.sb("ones1", [1, 2], F32)
    onesb = P.buf()
    P.op("pool", lambda e: e.memset(ones1[:], 1.0), writes=[onesb])
    P.op("sp", lambda e: e.dma_start(out=sc[:], in_=sc2T_d), writes=[scb], dma=True)
    P.op("act", lambda e: e.activation(out=sc[:], in_=sc[:], func=AF.Silu), reads=[scb], writes=[scb])
    modrow = P.sb("modrow", [2, NMOD * D], F32)
    mrb = [P.buf() for i in range(36)]
    wt = [P.sb("wt", [128, KC, 512], F32) for i in range(2)]
    wtb = [[P.buf(), P.buf()] for i in range(2)]
    bt = [P.sb("bt", [1, 512], F32) for i in range(2)]
    btb = [P.buf() for i in range(2)]
    n = 0
    for l in range(depth):
        for blk in range(36):
            c0 = blk * 512
            i = n % 2
            n += 1
            for half in range(2):
                P.op("sp" if half == 0 else "act", lambda e, i=i, l=l, blk=blk, half=half: e.dma_start(
                    out=wt[i][:, half * 8:half * 8 + 8, :],
                    in_=wm_d[l, blk].rearrange("p (c n) -> p c n", n=512)[:, half * 8:half * 8 + 8, :]),
                    writes=[wtb[i][half]], dma=True)
            P.op("sp", lambda e, i=i, l=l, c0=c0: e.dma_start(out=bt[i][:, :], in_=bm_d[l:l + 1, c0:c0 + 512]),
                 writes=[btb[i]], dma=True)
            bk = i
            for kc in range(KC):
                P.op("pe", lambda e, i=i, kc=kc, bk=bk: e.matmul(
                    R.banks[bk][:2, :], sc[:, kc * 2:(kc + 1) * 2], wt[i][:, kc, :], start=(kc == 0), stop=False),
                    reads=[wtb[i][kc // 8], scb], writes=[R.bank_b[bk]])
            P.op("pe", lambda e, i=i, bk=bk: e.matmul(R.banks[bk][:2, :], ones1[:], bt[i][:, :], start=False, stop=True),
                 reads=[btb[i], onesb], writes=[R.bank_b[bk]])
            P.op("act", lambda e, bk=bk, c0=c0: e.activation(out=modrow[:, c0:c0 + 512], in_=R.banks[bk][:2, :],
                                                           func=AF.Identity),
                 reads=[R.bank_b[bk]], writes=[mrb[blk]])
        for q in range(NMOD * KC):
            P.op("pe", lambda e, q=q: e.transpose(out=R.banks[2][:, q * 2:(q + 1) * 2],
                                                 in_=modrow[0:2, q * 128:(q + 1) * 128], identity=R.identf[0:2, 0:2]),
                 reads=[mrb[q // 4], R.persist_b], writes=[R.bank_b[2]])
        P.op("act", lambda e, l=l: e.activation(
            out=R.mvT[:, l].rearrange("p a b -> p (a b)"), in_=R.banks[2][:, 0:NMOD * KC * 2], func=AF.Identity),
            reads=[R.bank_b[2]], writes=[R.persist_b])
    P.barrier()


def emit_glaproj_phase(P, R, Hs, win_qk, win_vr, win_z, wgu_d, bgrow_d, QT, KT, V, RS, GA, GB):
    W = 512
    h = P.sb("h", [128, KC, W], BF16)
    hb = [P.buf() for c in range(KC)]
    wt = [P.sb("wt", [128, KC, 128], BF16) for i in range(3)]
    wtb = [P.buf() for i in range(3)]
    wbig = [P.sb("wbig", [128, KC, 512], BF16) for i in range(2)]
    wbigb = [P.buf() for i in range(2)]
    wz = P.sb("wz", [128, KC, 32], BF16)
    wzb = P.buf()
    P.op("pool", lambda e: e.dma_start(out=wz[:], in_=win_z.rearrange("p (c n) -> p c n", n=32)),
         writes=[wzb], dma=True)
    wg = P.sb("wg", [GLA_R, 2 * GLA_HK], F32)
    wgb = P.buf()
    P.op("sp", lambda e: e.dma_start(out=wg[:], in_=wgu_d), writes=[wgb], dma=True)
    bgr = P.sb("bgr", [1, 2 * GLA_HK], F32)
    bgb = P.buf()
    P.op("sp", lambda e: e.dma_start(out=bgr[:], in_=bgrow_d), writes=[bgb], dma=True)
    onesr = P.sb("onesr", [1, 128], F32)
    onesrb = P.buf()
    P.op("pool", lambda e: e.memset(onesr[:], 1.0), writes=[onesrb])
    z = [P.sb("z", [GLA_R, W], F32) for i in range(2)]
    zb = [P.buf() for i in range(2)]
    rot_o = Rot(P, "ot", [128, 512], F32, 3)
    rot_s = Rot(P, "sgx", [128, 512], F32, 2)
    n = {"w": 0, "b": 0, "wb": 0}
    G = [GA, GB]
    for (t0, w, _) in FT_LC:
        for half in range(2):
            cs = slice(half * 8, half * 8 + 8)
            P.op("pool", lambda e, cs=cs, t0=t0, w=w: e.dma_start(
                out=h[:, cs, :w], in_=chunked(Hs)[:, cs, t0:t0 + w]), writes=hb[cs], dma=True)
        for mc in range(16):
            i = n["w"] % 3
            n["w"] += 1
            P.op("pool", lambda e, i=i, mc=mc: e.dma_start(
                out=wt[i][:], in_=win_qk[mc].rearrange("p (c n) -> p c n", n=128)), writes=[wtb[i]], dma=True)
            bk = n["b"] % 4
            n["b"] += 1
            for kc in range(KC):
                P.op("pe", lambda e, i=i, kc=kc, bk=bk, w=w: e.matmul(
                    R.banks[bk][:, :w], wt[i][:, kc, :], h[:, kc, :w], start=(kc == 0), stop=(kc == KC - 1)),
                    reads=[wtb[i], hb[kc]], writes=[R.bank_b[bk]])
            ot, otb = rot_o.next()
            if mc < 8:
                P.op("act", lambda e, ot=ot, bk=bk, w=w: e.activation(
                    out=ot[:, :w], in_=R.banks[bk][:, :w], func=AF.Identity, scale=1.0 / 16.0),
                    reads=[R.bank_b[bk]], writes=[otb])
                dst = QT[mc * 128:(mc + 1) * 128, t0:t0 + w]
            else:
                P.op("dve", lambda e, ot=ot, bk=bk, w=w: e.tensor_copy(out=ot[:, :w], in_=R.banks[bk][:, :w]),
                     reads=[R.bank_b[bk]], writes=[otb])
                dst = KT[(mc - 8) * 128:(mc - 7) * 128, t0:t0 + w]
            P.op("sp", lambda e, ot=ot, dst=dst, w=w: e.dma_start(out=dst, in_=ot[:, :w]), reads=[otb], dma=True)
        for d in range(2):
            bk = 4 + d
            for kc in range(KC):
                P.op("pe", lambda e, d=d, kc=kc, bk=bk, w=w: e.matmul(
                    R.banks[bk][:GLA_R, :w], wz[:, kc, d * 16:(d + 1) * 16], h[:, kc, :w],
                    start=(kc == 0), stop=(kc == KC - 1)), reads=[wzb, hb[kc]], writes=[R.bank_b[bk]])
            P.op("dve", lambda e, d=d, bk=bk, w=w: e.tensor_copy(out=z[d][:, :w], in_=R.banks[bk][:GLA_R, :w]),
                 reads=[R.bank_b[bk]], writes=[zb[d]])
        for tc in range(w // 128):
            r0 = t0 + tc * 128
            for d in range(2):
                for cb in range(2):
                    bk = 6 + (cb % 2)
                    c0 = d * GLA_HK + cb * 512
                    P.op("pe", lambda e, d=d, tc=tc, bk=bk, c0=c0: e.matmul(
                        R.banks[bk][:, :], z[d][:, tc * 128:(tc + 1) * 128], wg[:, c0:c0 + 512],
                        start=True, stop=False), reads=[zb[d], wgb], writes=[R.bank_b[bk]])
                    P.op("pe", lambda e, bk=bk, c0=c0: e.matmul(
                        R.banks[bk][:, :], onesr[:], bgr[:, c0:c0 + 512], start=False, stop=True),
                        reads=[onesrb, bgb], writes=[R.bank_b[bk]])
                    sg_, sgb_ = rot_s.next()
                    ot, otb = rot_o.next()
                    P.op("act", lambda e, sg_=sg_, bk=bk: e.activation(out=sg_[:], in_=R.banks[bk][:, :], func=AF.Sigmoid),
                         reads=[R.bank_b[bk]], writes=[sgb_])
                    P.op("act", lambda e, sg_=sg_, ot=ot: e.activation(out=ot[:], in_=sg_[:], func=AF.Ln),
                         reads=[sgb_], writes=[otb])
                    P.op("sp", lambda e, ot=ot, d=d, r0=r0, cb=cb: e.dma_start(
                        out=G[d][r0:r0 + 128, cb * 512:(cb + 1) * 512], in_=ot[:]), reads=[otb], dma=True)
        for cb in range(8):
            i = n["wb"] % 2
            n["wb"] += 1
            P.op("pool", lambda e, i=i, cb=cb: e.dma_start(
                out=wbig[i][:], in_=win_vr[cb].rearrange("p (c n) -> p c n", n=512)),
                writes=[wbigb[i]], dma=True)
            for tc in range(w // 128):
                r0 = t0 + tc * 128
                bk = n["b"] % 4
                n["b"] += 1
                for kc in range(KC):
                    P.op("pe", lambda e, i=i, kc=kc, bk=bk, tc=tc: e.matmul(
                        R.banks[bk][:, :], h[:, kc, tc * 128:(tc + 1) * 128], wbig[i][:, kc, :],
                        start=(kc == 0), stop=(kc == KC - 1)), reads=[wbigb[i], hb[kc]], writes=[R.bank_b[bk]])
                ot, otb = rot_o.next()
                if cb < 4:
                    P.op("dve", lambda e, ot=ot, bk=bk: e.tensor_copy(out=ot[:], in_=R.banks[bk][:, :]),
                         reads=[R.bank_b[bk]], writes=[otb])
                    dst = V[r0:r0 + 128, cb * 512:(cb + 1) * 512]
                else:
                    P.op("act", lambda e, ot=ot, bk=bk: e.activation(out=ot[:], in_=R.banks[bk][:, :], func=AF.Silu),
                         reads=[R.bank_b[bk]], writes=[otb])
                    dst = RS[r0:r0 + 128, (cb - 4) * 512:(cb - 3) * 512]
                P.op("sp", lambda e, ot=ot, dst=dst: e.dma_start(out=dst, in_=ot[:]), reads=[otb], dma=True)
    P.barrier()


def emit_scan_phase(P, R, QT, KT, V, GA, GB, OA, OB, cc_in, cc_out, emit_ctx):
    pairs = [[0, 1], [2, 3], [4, 5], [6, 7]]
    bankB = R.banks[0][:, 0:256].rearrange("p (a b) -> p a b", b=128); bBb = R.bank_b[0]
    bankA = R.banks[1][:, 0:128]; bAb = R.bank_b[1]
    bankO = [R.banks[2], R.banks[3]]; bOb = [R.bank_b[2], R.bank_b[3]]
    bankT = R.banks[4][:, 0:128].bitcast(BF16).rearrange("p (a b) -> p a b", b=128); bTb = R.bank_b[4]
    bankU = [R.banks[5], R.banks[6]]; bUb = [R.bank_b[5], R.bank_b[6]]
    ident = R.ident
    Sst = P.sb("Sst", [128, 2, 512], F32); Sb = [P.buf() for j in range(2)]
    Sp = P.sb("Sp", [128, 2, 512], BF16); Spb = [P.buf() for j in range(2)]
    S0 = P.sb("S0", [128, 2, 512], F32); S0b = P.buf()
    S1 = P.sb("S1", [128, 2, 512], F32); S1b = P.buf()
    rq = Rot(P, "q", [128, 2, 128], F32, 3)
    rk = Rot(P, "k", [128, 2, 128], F32, 3)
    rv = Rot(P, "v", [128, 512], BF16, 3)
    rg = Rot(P, "g", [128, 256], F32, 3)
    rB = Rot(P, "Bsb", [128, 2, 128], F32, 2)
    rs = Rot(P, "sc", [128, 5, 2], F32, 2)
    rEq = Rot(P, "Eq", [128, 2, 128], F32, 2)
    rEk = Rot(P, "Ek", [128, 2, 128], F32, 2)
    rqi = Rot(P, "qi", [128, 2, 128], BF16, 2)
    rki = Rot(P, "ki", [128, 2, 128], BF16, 2)
    ram = Rot(P, "am", [128, 128], BF16, 2)
    rkit = Rot(P, "kit", [128, 2, 128], BF16, 2)
    ro = Rot(P, "osb", [128, 512], F32, 2)
    rtu = Rot(P, "tu", [128, 512], F32, 2)
    st = {"no": 0}

    def chunk(hd, n, tri, mask, ref, last, G, O):
        ts = slice(n * 128, (n + 1) * 128)
        q, qb = rq.next(); k, kb = rk.next(); v, vb = rv.next(); g, gb = rg.next()
        P.op("sp", lambda e: e.dma_start(
            out=q[:], in_=QT[hd * 256:(hd + 1) * 256, :].rearrange("(j p) t -> p j t", p=128)[:, :, ts]),
            writes=[qb], dma=True)
        P.op("sp", lambda e: e.dma_start(
            out=k[:], in_=KT[hd * 256:(hd + 1) * 256, :].rearrange("(j p) t -> p j t", p=128)[:, :, ts]),
            writes=[kb], dma=True)
        P.op("pool", lambda e: e.dma_start(out=v[:], in_=V[ts, hd * 512:(hd + 1) * 512]), writes=[vb], dma=True)
        P.op("act", lambda e: e.dma_start(out=g[:], in_=G[ts, hd * 256:(hd + 1) * 256]), writes=[gb], dma=True)
        for j in range(2):
            P.op("pe", lambda e, j=j: e.matmul(bankB[:, j, :], g[:, j * 128:(j + 1) * 128], tri[:],
                                               start=True, stop=True), reads=[gb, R.persist_b], writes=[bBb])
        B, Bb = rB.next()
        P.op("act", lambda e: e.activation(out=B[:], in_=bankB[:], func=AF.Identity), reads=[bBb], writes=[Bb])
        sc, scb = rs.next()
        P.op("dve", lambda e: e.tensor_scalar(out=sc[:, 0, :], in0=B[:, :, ref], scalar1=-1.0, scalar2=None,
                                              op0=ALU.mult), reads=[Bb], writes=[scb])
        P.op("dve", lambda e: e.tensor_tensor(out=sc[:, 1, :], in0=B[:, :, last], in1=sc[:, 0, :], op=ALU.add),
             reads=[Bb, scb], writes=[scb])
        P.op("act", lambda e: e.activation(out=sc[:, 2, :], in_=B[:, :, ref], func=AF.Exp),
             reads=[Bb, scb], writes=[scb])
        P.op("act", lambda e: e.activation(out=sc[:, 3, :], in_=B[:, :, last], func=AF.Exp),
             reads=[Bb, scb], writes=[scb])
        P.op("act", lambda e: e.activation(out=sc[:, 4, :], in_=sc[:, 1, :], func=AF.Exp),
             reads=[scb], writes=[scb])
        Eq, Eqb = rEq.next(); Ek, Ekb = rEk.next()
        for j in range(2):
            P.op("act", lambda e, j=j: e.activation(out=Eq[:, j, :], in_=B[:, j, :], func=AF.Exp,
                                                    bias=sc[:, 0, j:j + 1], scale=1.0),
                 reads=[Bb, scb], writes=[Eqb])
            P.op("act", lambda e, j=j: e.activation(out=Ek[:, j, :], in_=B[:, j, :], func=AF.Exp,
                                                    bias=B[:, j, ref:ref + 1], scale=-1.0),
                 reads=[Bb], writes=[Ekb])
        qi, qib = rqi.next(); ki, kib = rki.next()
        P.op("dve", lambda e: e.tensor_tensor(out=qi[:], in0=q[:], in1=Eq[:], op=ALU.mult),
             reads=[qb, Eqb], writes=[qib])
        P.op("dve", lambda e: e.tensor_tensor(out=ki[:], in0=k[:], in1=Ek[:], op=ALU.mult),
             reads=[kb, Ekb], writes=[kib])
        for j in range(2):
            P.op("act", lambda e, j=j: e.activation(out=Sp[:, j, :], in_=Sst[:, j, :], func=AF.Identity,
                                                    scale=sc[:, 2, j:j + 1]),
                 reads=[Sb[j], scb], writes=[Spb[j]])
        for j in range(2):
            P.op("pe", lambda e, j=j: e.matmul(bankA, ki[:, j, :], qi[:, j, :], start=(j == 0), stop=(j == 1)),
                 reads=[kib, qib], writes=[bAb])
        am, amb = ram.next()
        P.op("dve", lambda e: e.tensor_tensor(out=am[:], in0=bankA, in1=mask[:], op=ALU.mult),
             reads=[bAb, R.persist_b], writes=[amb])
        oi = st["no"] % 2
        st["no"] += 1
        P.op("pe", lambda e: e.matmul(bankO[oi][:], am[:], v[:], start=True, stop=False),
             reads=[amb, vb], writes=[bOb[oi]])
        for j in range(2):
            P.op("pe", lambda e, j=j: e.matmul(bankO[oi][:], qi[:, j, :], Sp[:, j, :], start=False, stop=(j == 1)),
                 reads=[qib, Spb[j]], writes=[bOb[oi]])
        osb, osbb = ro.next()
        P.op("act", lambda e: e.activation(out=osb[:], in_=bankO[oi][:], func=AF.Identity),
             reads=[bOb[oi]], writes=[osbb])
        P.op("sp", lambda e: e.dma_start(out=O[ts, hd * 512:(hd + 1) * 512], in_=osb[:]), reads=[osbb], dma=True)
        for j in range(2):
            P.op("pe", lambda e, j=j: e.transpose(out=bankT[:, j, :], in_=ki[:, j, :], identity=ident[:]),
                 reads=[kib, R.persist_b], writes=[bTb])
        kit, kitb = rkit.next()
        P.op("dve", lambda e: e.tensor_copy(out=kit[:], in_=bankT[:]), reads=[bTb], writes=[kitb])
        for j in range(2):
            P.op("pe", lambda e, j=j: e.matmul(bankU[j][:], kit[:, j, :], v[:], start=True, stop=True),
                 reads=[kitb, vb], writes=[bUb[j]])
            tu, tub = rtu.next()
            P.op("act", lambda e, j=j, tu=tu: e.activation(out=tu[:], in_=bankU[j][:], func=AF.Identity,
                                                           scale=sc[:, 4, j:j + 1]),
                 reads=[bUb[j], scb], writes=[tub])
            P.op("dve", lambda e, j=j, tu=tu: e.scalar_tensor_tensor(
                out=Sst[:, j, :], in0=Sst[:, j, :], scalar=sc[:, 3, j:j + 1], in1=tu[:],
                op0=ALU.mult, op1=ALU.add), reads=[Sb[j], tub, scb], writes=[Sb[j]])

    def zero_state():
        for j in range(2):
            P.op("dve", lambda e, j=j: e.memset(Sst[:, j, :], 0.0), writes=[Sb[j]])

    orderA = [16, 17] + list(range(16))
    cc_writes = []
    for hd in range(4):
        zero_state()
        for n in orderA:
            chunk(hd, n, R.triA, R.maskA, 63, 127, GA, OA)
        cc_writes.append(P.op("sp", lambda e, hd=hd: e.dma_start(
            out=cc_in[hd * 256:(hd + 1) * 256, :].rearrange("(j p) n -> p j n", p=128), in_=Sst[:]),
            reads=Sb, dma=True))
    cc = P.cc_op(lambda e: e.collective_compute("AllGather", ALU.bypass, replica_groups=pairs,
                                                ins=[cc_in], outs=[cc_out]), extra=cc_writes)
    if emit_ctx:
        for hd in range(4):
            zero_state()
            for n in (17, 16):
                chunk(hd, n, R.triB, R.maskB, 64, 0, GB, OB)
    for hd in range(4):
        P.op("sp", lambda e, hd=hd: e.dma_start(
            out=S0[:], in_=cc_out[hd * 256:(hd + 1) * 256, :].rearrange("(j p) n -> p j n", p=128)),
            writes=[S0b], dma=True, extra=[cc])
        P.op("sp", lambda e, hd=hd: e.dma_start(
            out=S1[:], in_=cc_out[1024 + hd * 256:1024 + (hd + 1) * 256, :].rearrange("(j p) n -> p j n", p=128)),
            writes=[S1b], dma=True, extra=[cc])
        for j in range(2):
            P.op("dve", lambda e, j=j: e.tensor_scalar(out=Sst[:, j, :], in0=S0[:, j, :], scalar1=R.sel[:, 0:1],
                                                       scalar2=None, op0=ALU.mult),
                 reads=[S0b, R.persist_b], writes=[Sb[j]])
            P.op("dve", lambda e, j=j: e.scalar_tensor_tensor(
                out=Sst[:, j, :], in0=S1[:, j, :], scalar=R.sel[:, 1:2], in1=Sst[:, j, :],
                op0=ALU.mult, op1=ALU.add), reads=[S1b, Sb[j], R.persist_b], writes=[Sb[j]])
        for n in range(15, -1, -1):
            chunk(hd, n, R.triB, R.maskB, 64, 0, GB, OB)
    P.barrier()


def emit_fnet_phase(P, R, Hs, Ms, hcc_in, hcc_out, cw_d, cl_l, sl_l, cl_c, sl_c):
    pairs = [[0, 1], [2, 3], [4, 5], [6, 7]]
    ccs = []
    for g in range(NG):
        stg = P.op("sp", lambda e, g=g: e.dma_start(
            out=hcc_in[g], in_=Hs[g * 256:(g + 1) * 256, 0:HALF]), dma=True)
        ccs.append(P.cc_op(lambda e, g=g: e.collective_compute(
            "AllGather", ALU.bypass, replica_groups=pairs, ins=[hcc_in[g]], outs=[hcc_out[g]]), extra=[stg]))
    banks, bkb = R.banks, R.bank_b
    cw = P.sb("cw_sb", [128, 2, 512], BF16)
    cwb = P.buf()
    P.op("sp", lambda e: e.dma_start(out=cw[:].rearrange("p a b -> p (a b)"), in_=cw_d), writes=[cwb], dma=True)
    LMAX = SEQ
    hg = [P.sb("hg", [128, 2, LMAX], BF16) for i in range(2)]
    hgb = [[P.buf(), P.buf()] for i in range(2)]
    A = P.sb("A", [128, LMAX // 128, 512], BF16)
    Ab = [P.buf() for i in range(LMAX // 128)]
    NB = 3
    clt = [P.sb("clt", [128, 8, 512], BF16) for i in range(NB)]
    cltb = [P.buf() for i in range(NB)]
    slt = [P.sb("slt", [128, 8, 512], BF16) for i in range(NB)]
    sltb = [P.buf() for i in range(NB)]
    fo = [P.sb("fo", [128, 512], F32) for i in range(4)]
    fob = [P.buf() for i in range(4)]
    n = {"hg": 0, "m": 0, "fo": 0, "a": 0}
    for (name, L, NK, cl, sl, col0) in (("l", SEQ, HALF, cl_l, sl_l, 0), ("c", CTX, CTX, cl_c, sl_c, HALF)):
        NCH = L // 128
        for g in range(NG):
            gi = n["hg"] % 2
            n["hg"] += 1
            if name == "l":
                for rnk in range(2):
                    P.op("pool", lambda e, gi=gi, g=g, rnk=rnk: e.dma_start(
                        out=hg[gi][:, :, rnk * HALF:(rnk + 1) * HALF],
                        in_=hcc_out[g][rnk * 256:(rnk + 1) * 256, :].rearrange("(c p) t -> p c t", p=128)),
                        writes=[hgb[gi][rnk]], dma=True, extra=[ccs[g]])
            else:
                P.op("pool", lambda e, gi=gi, g=g, L=L: e.dma_start(
                    out=hg[gi][:, :, :L], in_=chunked(Hs)[:, 2 * g:2 * g + 2, HALF:HALF + L]),
                    writes=hgb[gi], dma=True)
            for nch in range(NCH):
                bk = n["a"] % 2
                n["a"] += 1
                for kc in range(2):
                    P.op("pe", lambda e, gi=gi, nch=nch, kc=kc, bk=bk: e.matmul(
                        banks[bk][:, :], hg[gi][:, kc, nch * 128:(nch + 1) * 128], cw[:, kc, :],
                        start=(kc == 0), stop=(kc == 1)), reads=[hgb[gi][nch // 16], cwb], writes=[bkb[bk]])
                if nch % 2 == 0:
                    P.op("act", lambda e, nch=nch, bk=bk: e.activation(out=A[:, nch, :], in_=banks[bk][:, :],
                                                                      func=AF.Identity),
                         reads=[bkb[bk]], writes=[Ab[nch]])
                else:
                    P.op("dve", lambda e, nch=nch, bk=bk: e.tensor_copy(out=A[:, nch, :], in_=banks[bk][:, :]),
                         reads=[bkb[bk]], writes=[Ab[nch]])
            kblocks = [(k0, min(512, NK - k0)) for k0 in range(0, NK, 512)]
            for (k0, kw) in kblocks:
                pb = [2 + 2 * (n["fo"] % 2), 3 + 2 * (n["fo"] % 2)]
                nsub = (NCH + 7) // 8
                for sb_ in range(nsub):
                    r0 = sb_ * 8
                    rn = min(8, NCH - r0)
                    mi = n["m"] % NB
                    n["m"] += 1
                    P.op("sp", lambda e, mi=mi, r0=r0, rn=rn, k0=k0, kw=kw, cl=cl: e.dma_start(
                        out=clt[mi][:, :rn, :kw],
                        in_=cl[r0 * 128:(r0 + rn) * 128, k0:k0 + kw].rearrange("(c p) n -> p c n", p=128)),
                        writes=[cltb[mi]], dma=True)
                    P.op("act", lambda e, mi=mi, r0=r0, rn=rn, k0=k0, kw=kw, sl=sl: e.dma_start(
                        out=slt[mi][:, :rn, :kw],
                        in_=sl[r0 * 128:(r0 + rn) * 128, k0:k0 + kw].rearrange("(c p) n -> p c n", p=128)),
                        writes=[sltb[mi]], dma=True)
                    for r in range(rn):
                        nch = r0 + r
                        for mcx in range(2):
                            P.op("pe", lambda e, mi=mi, r=r, nch=nch, mcx=mcx, kw=kw, pb=pb: e.matmul(
                                banks[pb[mcx]][:, :kw], A[:, nch, mcx * 128:(mcx + 1) * 128], clt[mi][:, r, :kw],
                                start=(nch == 0), stop=False), reads=[Ab[nch], cltb[mi]], writes=[bkb[pb[mcx]]])
                            P.op("pe", lambda e, mi=mi, r=r, nch=nch, mcx=mcx, kw=kw, pb=pb, NCH=NCH: e.matmul(
                                banks[pb[mcx]][:, :kw], A[:, nch, 256 + mcx * 128:256 + (mcx + 1) * 128],
                                slt[mi][:, r, :kw], start=False, stop=(nch == NCH - 1)),
                                reads=[Ab[nch], sltb[mi]], writes=[bkb[pb[mcx]]])
                for mcx in range(2):
                    fi = n["fo"] % 2
                    P.op("act", lambda e, fi=fi, mcx=mcx, kw=kw, pb=pb: e.activation(
                        out=fo2(fo, fi, mcx)[:, :kw], in_=banks[pb[mcx]][:, :kw], func=AF.Identity),
                        reads=[bkb[pb[mcx]]], writes=[fob2(fob, fi, mcx)])
                    c = 2 * g + mcx
                    P.op("sp", lambda e, fi=fi, mcx=mcx, c=c, k0=k0, kw=kw, col0=col0: e.dma_start(
                        out=Ms[c * 128:(c + 1) * 128, col0 + k0:col0 + k0 + kw], in_=fo2(fo, fi, mcx)[:, :kw]),
                        reads=[fob2(fob, fi, mcx)], dma=True)
                n["fo"] += 1
    P.barrier()


def emit_conv_phase(P, R, Hs, Ms, w1, vecs_d, wdw_d):
    W = 512
    fused_common(P, R)
    C = R
    tiles = [(t0, w, 64) for t0, w, _ in FT_L] + [(HALF, CTX, CTX)]
    vv = P.sb("vv", [128, 5 * KC], F32)
    vvb = P.buf()
    P.op("sp", lambda e: e.dma_start(out=vv[:], in_=vecs_d), writes=[vvb], dma=True)
    wk = P.sb("wk", [128, KC * CONV_W], F32)
    wkb = P.buf()
    P.op("sp", lambda e: e.dma_start(out=wk[:], in_=wdw_d), writes=[wkb], dma=True)

    def vec(k):
        return vv[:, k * KC:(k + 1) * KC]

    h = P.sb("h", [128, KC, W], BF16)
    hb = [P.buf() for c in range(KC)]
    y = P.sb("y", [128, KC, W], F32)
    yb = [P.buf() for c in range(KC)]
    u = [P.sb("u", [128, W], F32) for i in range(2)]
    ub = [P.buf() for i in range(2)]
    sgm = [P.sb("sgm", [128, W], F32) for i in range(2)]
    sgmb = [P.buf() for i in range(2)]
    wa = [P.sb("wa", [128, KC, 128], BF16) for i in range(2)]
    wab = [P.buf() for i in range(2)]
    wgx = [P.sb("wgx", [128, KC, 128], BF16) for i in range(2)]
    wgxb = [P.buf() for i in range(2)]
    mean = P.sb("mean", [128, W], F32)
    meanb = P.buf()
    var = P.sb("var", [128, W], F32)
    varb = P.buf()
    ho = [P.sb("ho", [128, W], F32) for i in range(2)]
    hob = [P.buf() for i in range(2)]
    n = {"w": 0, "u": 0, "ho": 0}
    for (t0, w, L) in tiles:
        for half in range(2):
            cs = slice(half * 8, half * 8 + 8)
            P.op("pool", lambda e, cs=cs, t0=t0, w=w: e.dma_start(
                out=h[:, cs, :w], in_=chunked(Hs)[:, cs, t0:t0 + w]), writes=hb[cs], dma=True)
        for mc in range(KC):
            i = n["w"] % 2
            n["w"] += 1
            P.op("pool", lambda e, i=i, mc=mc: e.dma_start(
                out=wa[i][:], in_=w1[mc].rearrange("p (c n) -> p c n", n=128)), writes=[wab[i]], dma=True)
            P.op("pool", lambda e, i=i, mc=mc: e.dma_start(
                out=wgx[i][:], in_=w1[KC + mc].rearrange("p (c n) -> p c n", n=128)), writes=[wgxb[i]], dma=True)
            par = mc % 2
            ba, bg = 2 + 2 * par, 3 + 2 * par
            for kc in range(KC):
                P.op("pe", lambda e, i=i, kc=kc, ba=ba, w=w: e.matmul(
                    C.banks[ba][:, :w], wa[i][:, kc, :], h[:, kc, :w], start=(kc == 0), stop=(kc == KC - 1)),
                    reads=[wab[i], hb[kc]], writes=[C.bank_b[ba]])
            for kc in range(KC):
                P.op("pe", lambda e, i=i, kc=kc, bg=bg, w=w: e.matmul(
                    C.banks[bg][:, :w], wgx[i][:, kc, :], h[:, kc, :w], start=(kc == 0), stop=(kc == KC - 1)),
                    reads=[wgxb[i], hb[kc]], writes=[C.bank_b[bg]])
            ui = n["u"] % 2
            n["u"] += 1
            P.op("act", lambda e, ui=ui, bg=bg, mc=mc, w=w: e.activation(
                out=sgm[ui][:, :w], in_=C.banks[bg][:, :w], func=AF.Sigmoid, bias=vec(1)[:, mc:mc + 1], scale=1.0),
                reads=[C.bank_b[bg], vvb], writes=[sgmb[ui]])
            P.op("dve", lambda e, ui=ui, ba=ba, mc=mc, w=w: e.scalar_tensor_tensor(
                out=u[ui][:, :w], in0=C.banks[ba][:, :w], scalar=vec(0)[:, mc:mc + 1], in1=sgm[ui][:, :w],
                op0=ALU.add, op1=ALU.mult), reads=[C.bank_b[ba], sgmb[ui], vvb], writes=[ub[ui]])
            u3 = u[ui][:, :w].rearrange("p (s l) -> p s l", l=L)
            y3 = y[:, mc, :w].rearrange("p (s l) -> p s l", l=L)
            P.op("dve", lambda e, ui=ui, mc=mc, w=w: e.tensor_scalar(
                out=y[:, mc, :w], in0=u[ui][:, :w], scalar1=wk[:, mc * CONV_W + 15:mc * CONV_W + 16],
                scalar2=vec(2)[:, mc:mc + 1], op0=ALU.mult, op1=ALU.add),
                reads=[ub[ui], wkb, vvb], writes=[yb[mc]])
            for k in range(CONV_W):
                o = k - 15
                if o == 0 or abs(o) >= L:
                    continue
                a0, a1 = max(0, -o), min(L, L - o)
                P.op("dve", lambda e, u3=u3, y3=y3, mc=mc, k=k, a0=a0, a1=a1, o=o: e.scalar_tensor_tensor(
                    out=y3[:, :, a0:a1], in0=u3[:, :, a0 + o:a1 + o],
                    scalar=wk[:, mc * CONV_W + k:mc * CONV_W + k + 1], in1=y3[:, :, a0:a1],
                    op0=ALU.mult, op1=ALU.add), reads=[ub[ui], wkb, yb[mc]], writes=[yb[mc]])
            P.op("pe", lambda e, mc=mc, w=w: e.matmul(C.banks[0][:, :w], C.ones[:], y[:, mc, :w],
                                                     start=(mc == 0), stop=(mc == KC - 1)),
                 reads=[yb[mc], C.ones_b], writes=[C.bank_b[0]])
            si = C.n_sq % 2
            C.n_sq += 1
            P.op("act", lambda e, mc=mc, si=si, w=w: e.activation(out=C.sq[si][:, :w], in_=y[:, mc, :w], func=AF.Square),
                 reads=[yb[mc]], writes=[C.sq_b[si]])
            P.op("pe", lambda e, mc=mc, si=si, w=w: e.matmul(C.banks[1][:, :w], C.ones[:], C.sq[si][:, :w],
                                                            start=(mc == 0), stop=(mc == KC - 1)),
                 reads=[C.sq_b[si], C.ones_b], writes=[C.bank_b[1]])
        P.op("act", lambda e, w=w: e.activation(out=mean[:, :w], in_=C.banks[0][:, :w], func=AF.Identity,
                                                scale=1.0 / D), reads=[C.bank_b[0]], writes=[meanb])
        P.op("dve", lambda e, w=w: e.tensor_tensor(out=var[:, :w], in0=mean[:, :w], in1=mean[:, :w], op=ALU.mult),
             reads=[meanb], writes=[varb])
        P.op("dve", lambda e, w=w: e.scalar_tensor_tensor(
            out=var[:, :w], in0=C.banks[1][:, :w], scalar=1.0 / D, in1=var[:, :w],
            op0=ALU.mult, op1=ALU.subtract), reads=[C.bank_b[1], varb], writes=[varb])
        P.op("act", lambda e, w=w: e.activation(out=C.rstd[:, :w], in_=var[:, :w], func=AF.Sqrt,
                                                bias=C.epsb[:, 0:1], scale=1.0),
             reads=[varb, C.eps_bb], writes=[C.rstd_b])
        P.op("dve", lambda e, w=w: e.reciprocal(out=C.rstd[:, :w], in_=C.rstd[:, :w]),
             reads=[C.rstd_b], writes=[C.rstd_b])
        for c in range(KC):
            ti = C.n_tmp % 2
            C.n_tmp += 1
            i = n["ho"] % 2
            n["ho"] += 1
            P.op("dve", lambda e, c=c, ti=ti, w=w: e.tensor_tensor(
                out=C.tmp[ti][:, :w], in0=y[:, c, :w], in1=mean[:, :w], op=ALU.subtract),
                reads=[yb[c], meanb], writes=[C.tmp_b[ti]])
            P.op("dve", lambda e, c=c, ti=ti, w=w: e.scalar_tensor_tensor(
                out=C.tmp[ti][:, :w], in0=C.tmp[ti][:, :w], scalar=vec(3)[:, c:c + 1], in1=C.rstd[:, :w],
                op0=ALU.mult, op1=ALU.mult), reads=[C.tmp_b[ti], C.rstd_b, vvb], writes=[C.tmp_b[ti]])
            P.op("act", lambda e, c=c, ti=ti, i=i, w=w: e.activation(
                out=ho[i][:, :w], in_=C.tmp[ti][:, :w], func=AF.Silu, bias=vec(4)[:, c:c + 1], scale=1.0),
                reads=[C.tmp_b[ti], vvb], writes=[hob[i]])
            P.op("sp", lambda e, c=c, i=i, t0=t0, w=w: e.dma_start(
                out=Ms[c * 128:(c + 1) * 128, t0:t0 + w], in_=ho[i][:, :w]), reads=[hob[i]], dma=True)
    P.barrier()


def build_fused(depth=DEPTH, dbg=False):
    nc = bass.Bass("TRN2", target_bir_lowering=False)

    def din(name, shape, dt=F32):
        return nc.dram_tensor(name, list(shape), dt, kind="ExternalInput").ap()

    def dscr(name, shape, dt=F32):
        return nc.dram_tensor(name, list(shape), dt, kind="Internal").ap()

    xT_in = din("xT_in", [D, TT])
    sc2T = din("sc2T", [128, KC * 2])
    wm = din("wm", [DEPTH, 36, 128, KC * 512])
    bm = din("bm", [DEPTH, NMOD * D])
    ngf_d = din("ngf", [128, DEPTH * 3 * KC])
    fing_d = din("fing", [128, KC])
    wg = din("wg", [DEPTH, 2, FC // 2, 128, KC * 256])
    wu = din("wu", [DEPTH, 2, FC // 2, 128, KC * 256])
    wd = din("wd", [DEPTH, 2, 4, FC // 4, 128, 4 * 512])
    win_qk = din("win_qk", [2, 16, 128, KC * 128])
    win_vr = din("win_vr", [2, 8, 128, KC * 512])
    win_z = din("win_z", [2, 128, KC * 32])
    wgu = din("wgu", [2, GLA_R, 2 * GLA_HK])
    bgrow = din("bgrow", [2, 1, 2 * GLA_HK])
    ghbc_d = din("ghbc", [2, 128, 512])
    wout = din("wout", [2, KC, 128, KC * 128])
    cst_d = din("cst", [128, 5 * 128])
    ident_d = din("ident", [128, 128], BF16)
    sel_d = din("sel", [128, 2])
    cw_d = din("cw", [128, 1024], BF16)
    cl_l = din("cl_l", [SEQ, HALF], BF16)
    sl_l = din("sl_l", [SEQ, HALF], BF16)
    cl_c = din("cl_c", [CTX, CTX], BF16)
    sl_c = din("sl_c", [CTX, CTX], BF16)
    fwo = din("fwo", [KC, 128, KC * 128])
    cw1 = din("cw1", [2 * KC, 128, KC * 128])
    cvecs = din("cvecs", [128, 5 * KC])
    cwdw = din("cwdw", [128, KC * CONV_W])
    cwo = din("cwo", [KC, 128, KC * 128])
    bvec_d = din("bvec", [128, 2 * KC])
    oT = nc.dram_tensor("oT", [D, HALF], F32, kind="ExternalOutput").ap()

    X = dscr("X", [D, TT]); Hs = dscr("Hs", [D, TT]); Ms = dscr("Ms", [D, TT])
    QT = dscr("QT", [GLA_HK, TT]); KT = dscr("KT", [GLA_HK, TT])
    V = dscr("V", [TT, GLA_HV]); RS = dscr("RS", [TT, GLA_HV])
    GA = dscr("GA", [TT, GLA_HK]); GB = dscr("GB", [TT, GLA_HK])
    OA = dscr("OA", [TT, GLA_HV]); OB = dscr("OB", [TT, GLA_HV])
    ccb = [(dscr(f"cc_in{j}", [1024, 512]), dscr(f"cc_out{j}", [2048, 512])) for j in range(2)]
    hcc_in = [dscr(f"hcc_in{g}", [256, HALF]) for g in range(NG)]
    hcc_out = [dscr(f"hcc_out{g}", [512, HALF]) for g in range(NG)]
    if dbg:
        dbgX = nc.dram_tensor("dbgX", [D, TT], F32, kind="ExternalOutput").ap()

    with ExitStack() as es:
        P = Prog(nc, es)
        R = Shared()
        R.banks = [P.ps(f"bank{i}", [128, 512]) for i in range(8)]
        R.bank_b = [P.buf(f"bank{i}") for i in range(8)]
        R.persist_b = P.buf("persist")
        R.ones_b = R.persist_b
        R.eps_bb = R.persist_b
        R.ones = P.sb("ones", [128, 128], F32)
        R.epsb = P.sb("epsb", [128, 1], F32)
        zero16 = P.sb("zero16", [128, KC], F32)
        cst = P.sb("cst_sb", [128, 5 * 128], F32)
        R.triA, R.maskA, R.triB, R.maskB, R.identf = [cst[:, i * 128:(i + 1) * 128] for i in range(5)]
        R.ident = P.sb("ident_sb", [128, 128], BF16)
        R.sel = P.sb("sel_sb", [128, 2], F32)
        R.mvT = P.sb("mvT", [128, DEPTH, NMOD * KC, 2], F32)
        ngf = P.sb("ngf_sb", [128, DEPTH * 3 * KC], F32)
        fing = P.sb("fing_sb", [128, KC], F32)
        bvec = P.sb("bvec_sb", [128, 2 * KC], F32)
        ghbc = [P.sb(f"ghbc{j}", [128, 512], F32) for j in range(2)]
        R.final_dmas = []
        P.op("pool", lambda e: e.memset(R.ones[:], 1.0), writes=[R.persist_b])
        P.op("pool", lambda e: e.memset(R.epsb[:], EPS), writes=[R.persist_b])
        P.op("pool", lambda e: e.memset(zero16[:], 0.0), writes=[R.persist_b])
        for dst, src in ((cst[:], cst_d), (R.ident[:], ident_d), (R.sel[:], sel_d), (ngf[:], ngf_d),
                         (fing[:], fing_d), (bvec[:], bvec_d), (ghbc[0][:], ghbc_d[0]), (ghbc[1][:], ghbc_d[1])):
            P.op("sp", lambda e, dst=dst, src=src: e.dma_start(out=dst, in_=src), writes=[R.persist_b], dma=True)
        rem = nc.sbuf_bytes_remaining
        print("sbuf remaining", rem)
        P.arena = Arena(P, (rem - 2048) // 64 * 64)
        P.barrier()

        emit_mod_phase(P, R, sc2T, wm, bm, depth)

        def mv(i, r, k):
            return R.mvT[:, i, k * KC:(k + 1) * KC, r]

        def ng(i, k):
            return ngf[:, (i * 3 + k) * KC:(i * 3 + k + 1) * KC]

        x_src = xT_in
        for i in range(depth):
            kind, j, last = i % 3, i // 3, i == DEPTH - 1
            sets = [{"ffn": (ng(i, 0), mv(i, r, 0), mv(i, r, 1), mv(i, r, 2)),
                     "h": (ng(i, 1), mv(i, r, 3), mv(i, r, 4))} for r in range(2)]
            emit_ffn_phase(P, R, FT_LC, x_src, X, sets, (wg[i, 0], wu[i, 0], wd[i, 0]), h_dst=Hs)
            x_src = X
            if kind == 0:
                emit_glaproj_phase(P, R, Hs, win_qk[j], win_vr[j], win_z[j], wgu[j], bgrow[j], QT, KT, V, RS, GA, GB)
                emit_scan_phase(P, R, QT, KT, V, GA, GB, OA, OB, ccb[j][0], ccb[j][1], emit_ctx=not last)
                proj = dict(kind="gla", oa=OA, ob=OB, rs=RS, wo=wout[j], gh=ghbc[j])
                bias = zero16[:]
            elif kind == 1:
                emit_fnet_phase(P, R, Hs, Ms, hcc_in, hcc_out, cw_d, cl_l, sl_l, cl_c, sl_c)
                proj = dict(kind="m", m=Ms, wo=fwo)
                bias = bvec[:, 0:KC]
            else:
                emit_conv_phase(P, R, Hs, Ms, cw1, cvecs, cwdw)
                proj = dict(kind="m", m=Ms, wo=cwo)
                bias = bvec[:, KC:2 * KC]
            sets = [{"ffn": (ng(i, 2), mv(i, r, 6), mv(i, r, 7), mv(i, r, 8)),
                     "proj": (mv(i, r, 5), bias), "fin": fing[:]} for r in range(2)]
            if last:
                emit_ffn_phase(P, R, FT_L, X, X, sets[:1], (wg[i, 1], wu[i, 1], wd[i, 1]), proj=proj, final_dst=oT)
            else:
                emit_ffn_phase(P, R, FT_LC, X, X, sets, (wg[i, 1], wu[i, 1], wd[i, 1]), proj=proj)
        if depth < DEPTH or dbg:
            pass
        if dbg:
            for c4 in range(4):
                R.final_dmas.append(P.op("sp", lambda e, c4=c4: e.dma_start(
                    out=dbgX[c4 * 512:(c4 + 1) * 512, :], in_=X[c4 * 512:(c4 + 1) * 512, :]), dma=True))
        P.join("sp", R.final_dmas)
        P.emit()
    return nc


def _fused_inputs(x, c, ctx, c_ctx, w_mod, b_mod, norm_g, ffn_w_gate, ffn_w_up, ffn_w_down,
                  gla_w_in, gla_w_gate_up, gla_b_gate, gla_g_head, gla_w_out, fnet_w_out, fnet_b_out,
                  cm_w_pw1, cm_b_pw1, cm_w_dw, cm_b_dw, cm_ln_g, cm_ln_b, cm_w_pw2, cm_b_pw2, final_g):
    bf = ml_dtypes.bfloat16

    def tile_cols(w, nb):
        F_ = w.shape[1]
        return np.ascontiguousarray(w.reshape(KC, 128, F_ // nb, nb).transpose(2, 1, 0, 3)).reshape(F_ // nb, 128, KC * nb)

    s_, t_ = np.arange(128)[:, None], np.arange(128)[None, :]
    maskA = (s_ <= t_).astype(np.float32)
    maskB = (s_ >= t_).astype(np.float32)
    cst = np.ascontiguousarray(np.concatenate(
        [maskA / 16.0, maskA, maskB / 16.0, maskB, np.eye(128, dtype=np.float32)], axis=1).astype(np.float32))
    ident = np.eye(128).astype(bf)
    m_ = np.arange(GW)
    ang = 2 * np.pi * np.outer(m_, m_) / GW
    cwf = np.concatenate([np.cos(ang), np.sin(ang)], axis=1)
    cwc = np.ascontiguousarray(cwf.reshape(2, 128, 512).transpose(1, 0, 2)).reshape(128, 1024).astype(bf)
    ngf = np.ascontiguousarray(np.concatenate([_fm(norm_g[i, k]) for i in range(DEPTH) for k in range(3)], axis=1))
    fing = _fm(final_g)
    bvec = np.ascontiguousarray(np.concatenate([_fm(fnet_b_out[0]), _fm(cm_b_pw2[0])], axis=1))
    ghbc = np.ascontiguousarray(np.broadcast_to(gla_g_head[:, None, :], (2, 128, 512)))
    cvecs = np.ascontiguousarray(np.concatenate(
        [_fm(cm_b_pw1[0][:D]), _fm(cm_b_pw1[0][D:]), _fm(cm_b_dw[0]), _fm(cm_ln_g[0]), _fm(cm_ln_b[0])], axis=1))

    def taps(wdw):
        return np.ascontiguousarray(wdw.T.reshape(KC, 128, CONV_W).transpose(1, 0, 2)).reshape(128, KC * CONV_W)

    wdw_f = [taps(cm_w_dw[0]), taps(cm_w_dw[0][::-1])]
    nrm_l = 1.0 / np.sqrt(SEQ * GW)
    nrm_c = 1.0 / np.sqrt(CTX * GW)
    n_g = np.concatenate([np.arange(HALF), SEQ - 1 - np.arange(HALF)]).astype(np.int64)
    dft_l, dft_c = [], []
    for hf in range(2):
        k_loc = (np.arange(HALF) if hf == 0 else SEQ - 1 - np.arange(HALF)).astype(np.int64)
        a = 2 * np.pi * ((n_g[:, None] * k_loc[None, :]) % SEQ).astype(np.float64) / SEQ
        dft_l.append(((np.cos(a) * nrm_l).astype(bf), (-np.sin(a) * nrm_l).astype(bf)))
        nc_ = (np.arange(CTX) if hf == 0 else CTX - 1 - np.arange(CTX)).astype(np.int64)
        a = 2 * np.pi * ((nc_[:, None] * nc_[None, :]) % CTX).astype(np.float64) / CTX
        dft_c.append(((np.cos(a) * nrm_c).astype(bf), (-np.sin(a) * nrm_c).astype(bf)))
    win_p, wgu_p, bg_p = [], [], []
    for hf in range(2):
        da, db = (0, 1) if hf == 0 else (1, 0)
        w = gla_w_in.copy()
        if hf == 1:
            w[:, :, 6144:6160] = gla_w_in[:, :, 6160:6176]
            w[:, :, 6160:6176] = gla_w_in[:, :, 6144:6160]
        win_p.append((np.stack([tile_cols(w[j][:, 0:2048], 128) for j in range(2)]),
                      np.stack([tile_cols(w[j][:, 2048:6144], 512) for j in range(2)]),
                      np.stack([tile_cols(w[j][:, 6144:6176], 32)[0] for j in range(2)])))
        wgu_p.append(np.ascontiguousarray(np.concatenate([gla_w_gate_up[:, da], gla_w_gate_up[:, db]], axis=2)))
        bg_p.append(np.ascontiguousarray(np.concatenate([gla_b_gate[:, da], gla_b_gate[:, db]], axis=1)[:, None, :]))
    def tile_in(w, nb):
        L_, H_, _, F_ = w.shape
        return np.ascontiguousarray(
            w.reshape(L_, H_, KC, 128, F_ // nb, nb).transpose(0, 1, 4, 3, 2, 5)).reshape(L_, H_, F_ // nb, 128, KC * nb)

    wg_t = tile_in(ffn_w_gate, 256)
    wu_t = tile_in(ffn_w_up, 256)
    wd_t = np.ascontiguousarray(
        ffn_w_down.reshape(DEPTH, 2, FC // 4, 4, 128, 4, 512).transpose(0, 1, 5, 2, 4, 3, 6)).reshape(
        DEPTH, 2, 4, FC // 4, 128, 4 * 512)
    wout_t = np.stack([tile_cols(gla_w_out[j], 128) for j in range(2)])
    fwo_t = tile_cols(fnet_w_out[0], 128)
    cwo_t = tile_cols(cm_w_pw2[0], 128)
    cw1_t = tile_cols(cm_w_pw1[0], 128)
    wm_t = np.ascontiguousarray(
        w_mod.reshape(DEPTH, KC, 128, 36, 512).transpose(0, 3, 2, 1, 4)).reshape(DEPTH, 36, 128, KC * 512)
    ims = []
    for k in CORES:
        b, hf = k // 2, k % 2
        xl = x[b, hf * HALF:(hf + 1) * HALF]
        xc = ctx[b]
        if hf == 1:
            xl, xc = xl[::-1], xc[::-1]
        sc2 = np.stack([c[b], c_ctx])
        sc2T = np.ascontiguousarray(sc2.reshape(2, KC, 128).transpose(2, 1, 0)).reshape(128, KC * 2)
        sel = np.zeros((128, 2), np.float32)
        sel[:, 1 - hf] = 1.0
        ims.append({
            "xT_in": _cat([xl.T, xc.T]), "sc2T": sc2T, "wm": wm_t, "bm": b_mod, "ngf": ngf, "fing": fing,
            "wg": wg_t, "wu": wu_t, "wd": wd_t,
            "win_qk": win_p[hf][0], "win_vr": win_p[hf][1], "win_z": win_p[hf][2],
            "wgu": wgu_p[hf], "bgrow": bg_p[hf], "ghbc": ghbc, "wout": wout_t,
            "cst": cst, "ident": ident, "sel": sel, "cw": cwc,
            "cl_l": dft_l[hf][0], "sl_l": dft_l[hf][1], "cl_c": dft_c[hf][0], "sl_c": dft_c[hf][1],
            "fwo": fwo_t, "cw1": cw1_t, "cvecs": cvecs, "cwdw": wdw_f[hf], "cwo": cwo_t,
            "bvec": bvec,
        })
    return ims


def kernel(x, c, ctx, c_ctx, w_mod, b_mod, norm_g, ffn_w_gate, ffn_w_up, ffn_w_down,
           gla_w_in, gla_w_gate_up, gla_b_gate, gla_g_head, gla_w_out,
           fnet_w_out, fnet_b_out,
           cm_w_pw1, cm_b_pw1, cm_w_dw, cm_b_dw, cm_ln_g, cm_ln_b, cm_w_pw2, cm_b_pw2,
           final_g):
    args = [np.asarray(a, dtype=np.float32) for a in (
        x, c, ctx, c_ctx, w_mod, b_mod, norm_g, ffn_w_gate, ffn_w_up, ffn_w_down,
        gla_w_in, gla_w_gate_up, gla_b_gate, gla_g_head, gla_w_out, fnet_w_out, fnet_b_out,
        cm_w_pw1, cm_b_pw1, cm_w_dw, cm_b_dw, cm_ln_g, cm_ln_b, cm_w_pw2, cm_b_pw2, final_g)]
    ims = _fused_inputs(*args)
    nc = _prog("fused", lambda: build_fused(DEPTH))
    r = _run(nc, ims)
    out = np.zeros((BATCH, SEQ, D), np.float32)
    for k in CORES:
        b, hf = k // 2, k % 2
        o = r[k]["oT"].T
        if hf == 1:
            o = o[::-1]
        out[b, hf * HALF:(hf + 1) * HALF] = o
    return out
```
